# Optimizing a Trainium2 kernel written in Bass

```python
import math
import jax, jax.numpy as jnp
from jax import lax
import numpy as np

D_MODEL = 1024
BATCH = 16
SEQ = 2048
DEPTH = 1
DEC_BATCH = 32
DEC_SEQ = 1
PAST_LEN = 16384
PAGE_SIZE = 128

D_MIX = D_MODEL
D_ATT = D_MIX // 2
D_SSM = D_MIX - D_ATT
HEAD_DIM = 64
ATT_HEADS = D_ATT // HEAD_DIM
SSM_GROUP = 16
SSM_GROUPS = D_SSM // SSM_GROUP
SSM_STATE = 64
Q_BLOCK = 128
EPS = 1e-6
FGATE_BIAS_MIN = 4.0
FGATE_BIAS_MAX = 12.0
POOL_NUM = 5
POOL_DEN = 4
D_IN_PROJ = 4 * D_ATT + ATT_HEADS + 2 * D_SSM

kernel_name = "fox_s5_parallel_hybrid_step"


def _rmsnorm(x, g):
    xf = x.astype(jnp.float32)
    y = xf * lax.rsqrt(jnp.mean(xf * xf, axis=-1, keepdims=True) + EPS)
    return y.astype(x.dtype) * g


def _pre(x, c, g_norm, w_ada, b_ada, w_in, b_fgate):
    n, l, _ = x.shape
    mod = jax.nn.silu(c) @ w_ada + b_ada
    shift, scale, gate = jnp.split(mod, 3, axis=-1)
    h = _rmsnorm(x, g_norm) * (1 + scale[:, None]) + shift[:, None]
    p = h @ w_in
    sizes = (D_ATT, D_ATT, D_ATT, D_ATT, ATT_HEADS, D_SSM, D_SSM)
    idx = [int(s) for s in np.cumsum(sizes)[:-1]]
    q, k, v, z_att, fl, u, z_ssm = jnp.split(p, idx, axis=-1)
    heads = lambda t: t.reshape(n, l, ATT_HEADS, HEAD_DIM)
    logf = jax.nn.log_sigmoid((fl + b_fgate).astype(jnp.float32))
    return (gate, heads(q), heads(k), heads(v), z_att, logf,
            u.reshape(n, l, SSM_GROUPS, SSM_GROUP), z_ssm)


def _fox_prompt(q, k, v, logf):
    n, s, h, dh = q.shape
    nb = s // Q_BLOCK
    scale = HEAD_DIM ** -0.5
    cum = jnp.cumsum(logf, axis=1)
    cum_k = cum.transpose(0, 2, 1)
    qb = q.reshape(n, nb, Q_BLOCK, h, dh).swapaxes(0, 1)
    cb = cum.reshape(n, nb, Q_BLOCK, h).swapaxes(0, 1)
    qpos = jnp.arange(s).reshape(nb, Q_BLOCK)
    kpos = jnp.arange(s)

    def one_block(args):
        qi, ci, pi = args
        sc = jnp.einsum('nqhd,nkhd->nhqk', qi, k, preferred_element_type=jnp.float32) * scale
        bias = ci.transpose(0, 2, 1)[..., None] - cum_k[:, :, None, :]
        mask = kpos[None, None, None, :] <= pi[None, None, :, None]
        pr = jax.nn.softmax(jnp.where(mask, sc + bias, -jnp.inf), axis=-1)
        return jnp.einsum('nhqk,nkhd->nqhd', pr.astype(v.dtype), v)

    o = lax.map(one_block, (qb, cb, qpos))
    return o.swapaxes(0, 1).reshape(n, s, h, dh)


def _fox_sample(q, k_new, v_new, logf_new, k_past, v_past, logf_past):
    t = q.shape[1]
    scale = HEAD_DIM ** -0.5
    lp = logf_past.astype(jnp.float32)
    suffix = lax.cumsum(lp, axis=1, reverse=True) - lp
    cnew = jnp.cumsum(logf_new, axis=1).transpose(0, 2, 1)
    s_past = (jnp.einsum('nthd,nphd->nhtp', q, k_past, preferred_element_type=jnp.float32) * scale
              + suffix.transpose(0, 2, 1)[:, :, None, :] + cnew[..., None])
    s_new = (jnp.einsum('nthd,nkhd->nhtk', q, k_new, preferred_element_type=jnp.float32) * scale
             + cnew[..., None] - cnew[:, :, None, :])
    ti = jnp.arange(t)
    s_new = jnp.where(ti[None, :] <= ti[:, None], s_new, -jnp.inf)
    pr = jax.nn.softmax(jnp.concatenate([s_past, s_new], axis=-1), axis=-1)
    n_past = k_past.shape[1]
    pr = pr.astype(v_new.dtype)
    return (jnp.einsum('nhtp,nphd->nthd', pr[..., :n_past], v_past)
            + jnp.einsum('nhtk,nkhd->nthd', pr[..., n_past:], v_new))


def _s5_scan(u, h0_re, h0_im, a_re, a_im, log_dt, b_re, b_im, c_re, c_im, d_skip):
    f32 = jnp.float32
    lam = lax.complex(a_re.astype(f32), a_im.astype(f32))
    dt = jnp.exp(log_dt.astype(f32))[:, None]
    a_bar = jnp.exp(lam * dt)
    b_bar = ((a_bar - 1.0) / lam)[:, :, None] * lax.complex(b_re.astype(f32), b_im.astype(f32))
    uf = u.astype(f32)
    bu = jnp.einsum('gph,nlgh->nlgp', b_bar, uf.astype(jnp.complex64))
    h0 = lax.complex(h0_re.astype(f32), h0_im.astype(f32))
    bu = bu.at[:, 0].add(a_bar * h0)
    a = jnp.broadcast_to(a_bar, bu.shape)

    def combine(e1, e2):
        a1, b1 = e1
        a2, b2 = e2
        return a1 * a2, a2 * b1 + b2

    _, h = lax.associative_scan(combine, (a, bu), axis=1)
    c_mat = lax.complex(c_re.astype(f32), c_im.astype(f32))
    y = jnp.real(jnp.einsum('ghp,nlgp->nlgh', c_mat, h)) + d_skip.astype(f32) * uf
    h_last = h[:, -1]
    return y, jnp.real(h_last), jnp.imag(h_last)


def _post(x, gate, o_att, z_att, y_ssm, z_ssm, w_glu, b_glu, w_out):
    n, l, _ = x.shape
    att = o_att.reshape(n, l, D_ATT) * jax.nn.silu(z_att)
    g = jax.nn.gelu(y_ssm.reshape(n, l, D_SSM).astype(x.dtype))
    ssm = g * jax.nn.sigmoid(g @ w_glu + b_glu) * jax.nn.silu(z_ssm)
    mix = jnp.concatenate([att, ssm], axis=-1)
    return x + gate[:, None] * (mix @ w_out)


def setup_inputs(seed: int = 0) -> dict:
    key = jax.random.key(seed)
    ks = jax.random.split(key, 32)
    f32 = jnp.float32
    nrm = lambda k, s: jax.random.normal(k, s, f32)
    n_pages = PAST_LEN // PAGE_SIZE
    n_pool = (DEC_BATCH * n_pages * POOL_NUM) // POOL_DEN
    page_table = jax.random.permutation(ks[0], n_pool)[:DEC_BATCH * n_pages]
    page_table = page_table.reshape(DEC_BATCH, n_pages).astype(jnp.int32)
    fg = jnp.linspace(FGATE_BIAS_MIN, FGATE_BIAS_MAX, ATT_HEADS, dtype=f32)
    a_im = (math.pi * jnp.arange(SSM_STATE, dtype=f32))[None, None, :] + 0.01 * nrm(ks[13], (DEPTH, SSM_GROUPS, SSM_STATE))
    return {
        "x_prompt": nrm(ks[1], (BATCH, SEQ, D_MODEL)),
        "x_sample": nrm(ks[2], (DEC_BATCH, DEC_SEQ, D_MODEL)),
        "c_prompt": nrm(ks[3], (BATCH, D_MODEL)),
        "c_sample": nrm(ks[4], (DEC_BATCH, D_MODEL)),
        "cache_k": nrm(ks[5], (DEPTH, n_pool, PAGE_SIZE, ATT_HEADS, HEAD_DIM)),
        "cache_v": nrm(ks[6], (DEPTH, n_pool, PAGE_SIZE, ATT_HEADS, HEAD_DIM)),
        "cache_logf": jax.nn.log_sigmoid(fg + nrm(ks[7], (DEPTH, n_pool, PAGE_SIZE, ATT_HEADS))),
        "state_ssm_re": 0.1 * nrm(ks[8], (DEPTH, DEC_BATCH, SSM_GROUPS, SSM_STATE)),
        "state_ssm_im": 0.1 * nrm(ks[9], (DEPTH, DEC_BATCH, SSM_GROUPS, SSM_STATE)),
        "page_table": page_table,
        "g_norm": 1.0 + 0.02 * nrm(ks[10], (DEPTH, D_MODEL)),
        "w_ada": 0.5 * D_MODEL ** -0.5 * nrm(ks[11], (DEPTH, D_MODEL, 3 * D_MODEL)),
        "b_ada": 0.02 * nrm(ks[12], (DEPTH, 3 * D_MODEL)),
        "w_in": D_MODEL ** -0.5 * nrm(ks[14], (DEPTH, D_MODEL, D_IN_PROJ)),
        "b_fgate": fg[None, :] + 0.1 * nrm(ks[15], (DEPTH, ATT_HEADS)),
        "a_re": -0.5 + 0.01 * nrm(ks[16], (DEPTH, SSM_GROUPS, SSM_STATE)),
        "a_im": a_im,
        "log_dt": jax.random.uniform(ks[17], (DEPTH, SSM_GROUPS), f32, math.log(1e-3), math.log(1e-1)),
        "b_re": (2 * SSM_GROUP) ** -0.5 * nrm(ks[18], (DEPTH, SSM_GROUPS, SSM_STATE, SSM_GROUP)),
        "b_im": (2 * SSM_GROUP) ** -0.5 * nrm(ks[19], (DEPTH, SSM_GROUPS, SSM_STATE, SSM_GROUP)),
        "c_re": (2 * SSM_STATE) ** -0.5 * nrm(ks[20], (DEPTH, SSM_GROUPS, SSM_GROUP, SSM_STATE)),
        "c_im": (2 * SSM_STATE) ** -0.5 * nrm(ks[21], (DEPTH, SSM_GROUPS, SSM_GROUP, SSM_STATE)),
        "d_skip": nrm(ks[22], (DEPTH, SSM_GROUPS, SSM_GROUP)),
        "w_glu": D_SSM ** -0.5 * nrm(ks[23], (DEPTH, D_SSM, D_SSM)),
        "b_glu": 0.02 * nrm(ks[24], (DEPTH, D_SSM)),
        "w_out": D_MIX ** -0.5 * nrm(ks[25], (DEPTH, D_MIX, D_MODEL)),
        "g_final": 1.0 + 0.02 * nrm(ks[26], (D_MODEL,)),
    }


def reference(x_prompt, x_sample, c_prompt, c_sample, cache_k, cache_v, cache_logf,
              state_ssm_re, state_ssm_im, page_table, g_norm, w_ada, b_ada, w_in, b_fgate,
              a_re, a_im, log_dt, b_re, b_im, c_re, c_im, d_skip, w_glu, b_glu, w_out, g_final):
    xp, xs = x_prompt, x_sample
    nb_p = xp.shape[0]
    nb_s = xs.shape[0]
    kp_l, vp_l, lfp_l, hrp_l, hip_l = [], [], [], [], []
    ks_l, vs_l, lfs_l, hrs_l, his_l = [], [], [], [], []
    for l in range(DEPTH):
        ssm_w = (a_re[l], a_im[l], log_dt[l], b_re[l], b_im[l], c_re[l], c_im[l], d_skip[l])
        gate, q, k, v, z_att, logf, u, z_ssm = _pre(xp, c_prompt, g_norm[l], w_ada[l], b_ada[l], w_in[l], b_fgate[l])
        o_att = _fox_prompt(q, k, v, logf)
        h0 = jnp.zeros((nb_p, SSM_GROUPS, SSM_STATE), jnp.float32)
        y_ssm, h_re, h_im = _s5_scan(u, h0, h0, *ssm_w)
        xp = _post(xp, gate, o_att, z_att, y_ssm, z_ssm, w_glu[l], b_glu[l], w_out[l])
        kp_l.append(k); vp_l.append(v); lfp_l.append(logf); hrp_l.append(h_re); hip_l.append(h_im)
        gate, q, k, v, z_att, logf, u, z_ssm = _pre(xs, c_sample, g_norm[l], w_ada[l], b_ada[l], w_in[l], b_fgate[l])
        k_past = cache_k[l][page_table].reshape(nb_s, -1, ATT_HEADS, HEAD_DIM)
        v_past = cache_v[l][page_table].reshape(nb_s, -1, ATT_HEADS, HEAD_DIM)
        lf_past = cache_logf[l][page_table].reshape(nb_s, -1, ATT_HEADS)
        o_att = _fox_sample(q, k, v, logf, k_past, v_past, lf_past)
        y_ssm, h_re, h_im = _s5_scan(u, state_ssm_re[l], state_ssm_im[l], *ssm_w)
        xs = _post(xs, gate, o_att, z_att, y_ssm, z_ssm, w_glu[l], b_glu[l], w_out[l])
        ks_l.append(k); vs_l.append(v); lfs_l.append(logf); hrs_l.append(h_re); his_l.append(h_im)
    y_prompt = _rmsnorm(xp, g_final)
    y_sample = _rmsnorm(xs, g_final)
    return (y_prompt, y_sample,
            jnp.stack(kp_l), jnp.stack(vp_l), jnp.stack(lfp_l), jnp.stack(hrp_l), jnp.stack(hip_l),
            jnp.stack(ks_l), jnp.stack(vs_l), jnp.stack(lfs_l), jnp.stack(hrs_l), jnp.stack(his_l))
```

```python
import numpy as np
from contextlib import ExitStack
import concourse.bass as bass
import concourse.mybir as mybir
from concourse.bass_utils import run_bass_kernel_spmd

F32 = mybir.dt.float32
BF16 = mybir.dt.bfloat16
I32 = mybir.dt.int32
AF = mybir.ActivationFunctionType
ALU = mybir.AluOpType
AX = mybir.AxisListType

NCORES = 8
D = 1024
SEQ = 2048
NSEQ = 2
NSMP = 4
NTOK = NSEQ * SEQ
DIN = 3080
KC = 8
EPS = 1e-6
COL_K, COL_V, COL_FL = 512, 1024, 2048


class Buf:
    def __init__(self, name, t, psum=False):
        self.name = name
        self.t = t
        self.psum = psum
        self.lw = None
        self.rd = {}
        self.wsem = None
        self.wcnt = 0
        self.rsem = None
        self.rcnt = 0

    def __getitem__(self, idx):
        return self.t[idx]


class K:
    def __init__(self, nc):
        self.nc = nc
        self.E = {"pe": nc.tensor, "act": nc.scalar, "dve": nc.vector,
                  "pool": nc.gpsimd, "sp": nc.sync}
        self.sem = {e: nc.alloc_semaphore(name="c_" + e) for e in self.E}
        self.cnt = {e: 0 for e in self.E}
        self.seen = {e: {} for e in self.E}
        self.bufs = []
        self.nins = {e: 0 for e in self.E}

    def wrap(self, name, t, psum=False):
        b = Buf(name, t, psum)
        self.bufs.append(b)
        return b

    def _wait(self, e, tok):
        nm, sem, val = tok
        if e == "pe" and nm == "c_pe":
            return
        if self.seen[e].get(nm, 0) >= val:
            return
        self.E[e].wait_ge(sem, val)
        self.seen[e][nm] = val

    def _deps(self, e, R, W):
        for b in R:
            if b.lw is not None:
                self._wait(e, b.lw)
            if b.psum:
                for tok in b.rd.values():
                    self._wait(e, tok)
        for b in W:
            if b.lw is not None:
                self._wait(e, b.lw)
            for tok in b.rd.values():
                self._wait(e, tok)

    def _mark(self, tok, R, W):
        for b in R:
            old = b.rd.get(tok[0])
            if old is None or old[2] < tok[2]:
                b.rd[tok[0]] = tok
        for b in W:
            b.lw = tok
            b.rd = {}

    def op(self, e, name, *a, R=(), W=(), inc=True, **kw):
        self._deps(e, R, W)
        ins = getattr(self.E[e], name)(*a, **kw)
        self.nins[e] += 1
        if inc:
            self.cnt[e] += 1
            ins.then_inc(self.sem[e], 1)
            tok = ("c_" + e, self.sem[e], self.cnt[e])
        else:
            tok = ("c_" + e, self.sem[e], self.cnt[e] + 1)
        self._mark(tok, R, W)
        return ins

    def pe(self, name, *a, **kw):
        return self.op("pe", name, *a, **kw)

    def act(self, name, *a, **kw):
        return self.op("act", name, *a, **kw)

    def dve(self, name, *a, **kw):
        return self.op("dve", name, *a, **kw)

    def pool(self, name, *a, **kw):
        return self.op("pool", name, *a, **kw)

    def dma(self, e, out, in_, R=(), W=(), **kw):
        self._deps(e, R, W)
        fi = kw.pop("fn_indirect", None)
        if fi is not None:
            ins = self.E[e].indirect_dma_start(out, None, in_, bass.IndirectOffsetOnAxis(ap=fi, axis=0))
        else:
            ins = self.E[e].dma_start(out=out, in_=in_, **kw)
        self.nins[e] += 1
        if W:
            b = W[0]
            if b.wsem is None:
                b.wsem = self.nc.alloc_semaphore(name="w_" + b.name)
            b.wcnt += 16
            ins.then_inc(b.wsem, 16)
            tok = ("w_" + b.name, b.wsem, b.wcnt)
        else:
            b = R[0]
            if b.rsem is None:
                b.rsem = self.nc.alloc_semaphore(name="r_" + b.name)
            b.rcnt += 16
            ins.then_inc(b.rsem, 16)
            tok = ("r_" + b.name, b.rsem, b.rcnt)
        self._mark(tok, R, W)
        return ins

    def barrier(self):
        toks = [("c_" + e, self.sem[e], self.cnt[e]) for e in self.E if self.cnt[e] > 0]
        for b in self.bufs:
            if b.wsem is not None and b.wcnt > 0:
                toks.append(("w_" + b.name, b.wsem, b.wcnt))
            if b.rsem is not None and b.rcnt > 0:
                toks.append(("r_" + b.name, b.rsem, b.rcnt))
        for e in self.E:
            for t in toks:
                self._wait(e, t)

    def finish(self, e="sp"):
        for b in self.bufs:
            if b.rsem is not None and b.rcnt > 0:
                self._wait(e, ("r_" + b.name, b.rsem, b.rcnt))
            if b.wsem is not None and b.wcnt > 0:
                self._wait(e, ("w_" + b.name, b.wsem, b.wcnt))


BLK = 512
NBLK = SEQ // BLK
COL_Q, COL_K, COL_V, COL_ZA, COL_FL, COL_U, COL_ZS = 0, 512, 1024, 1536, 2048, 2056, 2568
VA_W = 528
class _Stop(Exception):
    pass


def _ck(i):
    if DBG.get("stop") == i:
        raise _Stop()


DBG = {"stop": None, "nseq": NSEQ, "nblk": NBLK, "ktr": True, "att": True, "post": True, "cum": True, "cores": NCORES}


def build_program():
    nc = bass.Bass("TRN2", target_bir_lowering=False)

    def din(name, shape, dt=F32):
        return nc.dram_tensor(name, list(shape), dt, kind="ExternalInput").ap()

    def dout(name, shape, dt=F32):
        return nc.dram_tensor(name, list(shape), dt, kind="ExternalOutput").ap()

    x = din("x_d", [NTOK, D])
    xs = din("xs_d", [NSMP, D])
    cnd = din("cnd", [NSMP + NSEQ, D])
    g_norm = din("g_norm_d", [1, D])
    g_final = din("g_final_d", [1, D])
    w_ada = din("w_ada_d", [D, 3 * D])
    b_ada = din("b_ada_d", [1, 3 * D])
    w_in = din("w_in_d", [D, DIN])
    w_out = din("w_out_d", [D, D])
    w_glu = din("w_glu_d", [512, 512])
    b_glu = din("b_glu_d", [512])
    b_fgate = din("b_fgate_d", [1, 8])
    ident_d = din("ident_c", [128, 128])
    tri_d = din("tri_c", [128, 128])
    sel_d = din("sel_c", [6, 6 * 128])
    shift_d = din("shift_c", [64, 128])
    m9_d = din("m9_c", [128, 16 * 9])
    m64_d = din("m64_c", [128, 16 * 64])
    a_re_d = din("a_re_d", [32, 64])
    a_im_d = din("a_im_d", [32, 64])
    log_dt_d = din("log_dt_d", [32])
    b_re_d = din("b_re_d", [32, 64, 16])
    b_im_d = din("b_im_d", [32, 64, 16])
    c_re_d = din("c_re_d", [32, 16, 64])
    c_im_d = din("c_im_d", [32, 16, 64])
    d_skip_d = din("d_skip_d", [32, 16])
    if DBG.get("sample", True):
        NROW = 5120 * 128
        ck_d = din("ck_d", [NROW, 512])
        cv_d = din("cv_d", [NROW, 512])
        clf_d = din("clf_d", [NROW, 8])
        pt_d = din("pt_d", [1, NSMP * 128], I32)
        h0re_d = din("h0re_d", [NSMP, 32, 64])
        h0im_d = din("h0im_d", [NSMP, 32, 64])
        sut_d = din("sut_c", [128, 128])
        bd8_d = din("bd8_c", [8, 512])
        sel8_d = din("sel8_c", [8, NSMP * NSMP])

    yp = dout("yp", [NTOK, D])
    if DBG.get("dump"):
        d_apw = dout("d_apw", [128, 288]); d_sm = dout("d_sm", [128, 384])
        d_klag = dout("d_klag", [128, 4096], BF16); d_wx = dout("d_wx", [128, 8192], BF16)
        d_ms = dout("d_ms", [128, 9216], BF16); d_rc = dout("d_rc", [128, 1024]); d_rs = dout("d_rs", [128, 1024])
    hre_o = dout("hre_p", [NSEQ, 32, 64])
    ys_o = dout("ys", [NSMP, D])
    hres_o = dout("hre_s", [NSMP, 32, 64])
    hims_o = dout("him_s", [NSMP, 32, 64])
    him_o = dout("him_p", [NSEQ, 32, 64])
    kp = dout("kp", [NTOK, 512])
    vp = dout("vp", [NTOK, 512])
    lfp = dout("lfp", [NTOK, 8])
    ks = dout("ks", [NSMP, 512])
    vs = dout("vs", [NSMP, 512])
    lfs = dout("lfs", [NSMP, 8])

    with ExitStack() as es:
        k = K(nc)

        acct = {"bytes": 0}

        def sb(name, shape, dt):
            n = 1
            for d in shape[1:]:
                n *= d
            acct["bytes"] += n * (2 if dt == BF16 else 4)
            return k.wrap(name, es.enter_context(nc.sbuf_tensor(name, list(shape), dt)))

        def ps(name, shape, dt=F32):
            return k.wrap(name, es.enter_context(nc.psum_tensor(name, list(shape), dt)), psum=True)

        ident32 = sb("ident32", [128, 128], F32)
        ident = sb("ident", [128, 128], BF16)
        tri32 = sb("tri32", [128, 128], F32)
        trib = sb("trib", [128, 128], BF16)
        ones32 = sb("ones32", [128, 128], F32)
        sel = sb("sel", [6, 6 * 128], F32)
        shiftm = sb("shiftm", [64, 128], F32)
        bfg_b = sb("bfg_b", [128, 8], F32)
        bglu = sb("bglu", [128, 4], F32)
        mod = sb("mod", [6, 3 * D], F32)
        cT = sb("cT", [128, KC, 6], F32)
        cTs = sb("cTs", [128, KC, 6], F32)
        w_out_bf = sb("w_out_bf", [128, KC, D], BF16)
        w_glu_bf = sb("w_glu_bf", [128, 4, 512], BF16)
        gfin_b = sb("gfin_b", [128, D], F32)
        sc1_b = sb("sc1_b", [128, D], F32)
        shift_b = sb("shift_b", [128, D], F32)
        gate_b = sb("gate_b", [128, D], F32)
        wring = [sb("wr%d" % i, [128, KC, 128], BF16) for i in range(3)]
        P = [ps("P%d" % i, [128, 512]) for i in range(7)]
        PT = ps("PT", [128, KC, 128], BF16)

        xt = [sb("xt%d" % i, [128, D], F32) for i in range(2)]
        tmp = sb("tmp", [128, D], F32)
        yst = sb("yst", [128, D], F32)
        hb = sb("hb", [128, D], BF16)
        hT = sb("hT", [128, KC, BLK], BF16)
        qT = sb("qT", [128, 4, BLK], BF16)
        uT = sb("uT", [128, 4, BLK], BF16)
        gT = qT
        mixT = sb("mixT", [128, KC, BLK], BF16)
        kT = sb("kT", [128, 4, SEQ], BF16)
        vaug = sb("vaug", [128, 16 * VA_W], BF16)
        vaug4 = vaug[:].rearrange("p (t h c) -> p t h c", t=16, h=8, c=66)
        kTf = sb("kTf", [128, BLK], F32)
        o_sb = kTf
        kst = sb("kst", [128, 4, 128], F32)
        vst = sb("vst", [128, 4, 128], F32)
        pTb = [sb("pTb%d" % i, [128, BLK], BF16) for i in range(2)]
        sg = sb("sg", [128, BLK], BF16)
        attf = sb("attf", [128, BLK], F32)
        bc_sb = sb("bc_sb", [128, BLK], F32)
        rec = sb("rec", [128, BLK], F32)
        ss = sb("ss", [128, 1], F32)
        rstd = sb("rstd", [128, 1], F32)
        lfz = sb("lfz", [128, 4, 8], F32)
        lf = sb("lf", [128, 4, 8], F32)
        E5 = sb("E5", [128, 5, 8], F32)
        ncum = sb("ncum", [128, NBLK, 4, 8], F32)
        totb = sb("totb", [128, NBLK, 8], F32)
        sacc = sb("sacc", [128, 8], F32)
        biask = sb("biask", [128, NBLK, 4, 8], F32)
        scg = sb("scg", [128, 64], F32)

        Wx = sb("Wx", [128, 4, 8, 2, 128], BF16)
        Ms = sb("Ms", [128, 16, 9, 2, 32], BF16)
        Klag = sb("Klag", [128, 4, 8, 128], BF16)
        rc = sb("rc", [128, 16, 64], F32)
        rs = sb("rs", [128, 16, 64], F32)
        sm = sb("sm", [128, 24, 16], F32)
        apw = sb("apw", [128, 2, 16, 9], F32)
        dcol = sb("dcol", [128, 4], F32)
        hin = sb("hin", [128, 2, 16], F32)
        zb = sb("zb", [128, 128], BF16)
        ARE, AIM, LDT, DT, LR, LI, DEN, KRE, KIM, NRE, NIM, TA, TB, RR = range(14)
        k.dma("sp", ident32[:], ident_d, W=[ident32])
        k.dve("tensor_copy", ident[:], ident32[:], R=[ident32], W=[ident])
        k.dma("sp", tri32[:], tri_d, W=[tri32])
        k.dve("tensor_copy", trib[:], tri32[:], R=[tri32], W=[trib])
        k.dve("memset", ones32[:], 1.0, W=[ones32])
        k.dma("sp", sel[:], sel_d, W=[sel])
        k.dma("sp", shiftm[:], shift_d, W=[shiftm])
        k.dma("sp", gfin_b[:], g_final.broadcast_to([128, D]), W=[gfin_b])
        k.dma("sp", bfg_b[:], b_fgate.broadcast_to([128, 8]), W=[bfg_b])
        k.dma("sp", mod[:], b_ada.broadcast_to([6, 3 * D]), W=[mod])
        with nc.allow_non_contiguous_dma(reason="tiny transposed loads"):
            for n in range(6):
                k.dma("sp", cT[:, :, n], cnd[n, :].rearrange("(c p) -> p c", p=128), W=[cT])
            k.dma("sp", bglu[:], b_glu.rearrange("(c p) -> p c", p=128), W=[bglu])
        k.act("activation", cTs[:], cT[:], AF.Silu, R=[cT], W=[cTs])
        w_in_v = w_in.rearrange("(c p) n -> p c n", p=128)
        w_out_v = w_out.rearrange("(c p) n -> p c n", p=128)
        w_glu_v = w_glu.rearrange("(c p) n -> p c n", p=128)
        for c in range(KC):
            k.dma("pool", w_out_bf[:, c, :], w_out_v[:, c, :], W=[w_out_bf])
        for c in range(4):
            k.dma("pool", w_glu_bf[:, c, :], w_glu_v[:, c, :], W=[w_glu_bf])

        try:
            _ck(1)
            w_ada_v = w_ada.rearrange("(c p) n -> p c n", p=128)
            aring = [tmp, yst, xt[0], xt[1]]
            ai = 0
            for cc in range(6):
                p = P[cc % 2]
                for c2 in range(KC // 2):
                    wb = aring[ai % 4]
                    ai += 1
                    wv = wb[:, :].rearrange("p (c n) -> p c n", c=2)
                    k.dma("sp", wv, w_ada_v[:, 2 * c2:2 * c2 + 2, cc * 512:(cc + 1) * 512], W=[wb])
                    for cl in range(2):
                        c = 2 * c2 + cl
                        k.pe("matmul", p[0:6, :], cTs[:, c, :], wv[:, cl, :], start=(c == 0), stop=(c == KC - 1),
                             R=[cTs, wb], W=[p], inc=(cl == 1))
                k.dve("tensor_tensor", mod[:, cc * 512:(cc + 1) * 512], p[0:6, :], mod[:, cc * 512:(cc + 1) * 512],
                      op=ALU.add, R=[p, mod], W=[mod])

            PI = float(np.pi)
            MAGIC = 12582912.0
            C1, C2 = 6.28125, 2.0 * float(np.pi) - 6.28125

            def sincos(argB, argv, sB, sv, cB, cv, t1B, t1v, t2B, t2v):
                k.dve("tensor_scalar", t1v, argv, 1.0 / (2.0 * PI), None, op0=ALU.mult, R=[argB], W=[t1B])
                k.dve("tensor_scalar", t1v, t1v, MAGIC, None, op0=ALU.add, R=[t1B], W=[t1B])
                k.dve("tensor_scalar", t1v, t1v, -MAGIC, None, op0=ALU.add, R=[t1B], W=[t1B])
                k.dve("scalar_tensor_tensor", t2v, t1v, -C1, argv, op0=ALU.mult, op1=ALU.add, R=[t1B, argB], W=[t2B])
                k.dve("scalar_tensor_tensor", t2v, t1v, -C2, t2v, op0=ALU.mult, op1=ALU.add, R=[t1B, t2B], W=[t2B])
                k.dve("tensor_scalar", t2v, t2v, PI, -PI, op0=ALU.min, op1=ALU.max, R=[t2B], W=[t2B])
                k.act("activation", sv, t2v, AF.Sin, R=[t2B], W=[sB])
                k.dve("scalar_tensor_tensor", t1v, t2v, -1.0, t2v, op0=ALU.mult, op1=ALU.max, R=[t2B], W=[t1B])
                k.act("activation", cv, t1v, AF.Sin, scale=-1.0, bias=PI / 2.0, R=[t1B], W=[cB])

            def tt(out, a, b, op, R, W):
                k.dve("tensor_tensor", out, a, b, op=op, R=R, W=W)

            def S(i):
                return sm[:, i, :]

            def bc16(ap2):
                return ap2.unsqueeze(2).broadcast_to([128, 16, 16])

            l1 = lambda d: d.rearrange("(P gm) p -> (gm p) P", gm=2)
            with nc.allow_non_contiguous_dma(reason="small S5 parameter re-layouts"):
                k.dma("sp", S(ARE), l1(a_re_d), W=[sm])
                k.dma("sp", S(AIM), l1(a_im_d), W=[sm])
                ldv = log_dt_d.rearrange("(P gm) -> gm P", gm=2)
                for gm in range(2):
                    k.dma("sp", sm[gm * 64:(gm + 1) * 64, LDT, :], ldv[gm:gm + 1, :].broadcast_to([64, 16]), W=[sm])
                k.dma("sp", dcol[:], d_skip_d.rearrange("(t g) h -> (g h) t", t=4), W=[dcol])
                Bq = [tmp[:, 0:256].rearrange("p (a h) -> p a h", a=16), tmp[:, 256:512].rearrange("p (a h) -> p a h", a=16)]
                Cq = [tmp[:, 512:768].rearrange("p (a h) -> p a h", a=16), tmp[:, 768:1024].rearrange("p (a h) -> p a h", a=16)]
                l1b = lambda d: d.rearrange("(P gm) p h -> (gm p) P h", gm=2)
                k.dma("sp", Bq[0], l1b(b_re_d), W=[tmp])
                k.dma("sp", Bq[1], l1b(b_im_d), W=[tmp])
                for ri, cd in enumerate((c_re_d, c_im_d)):
                    cv4 = cd.rearrange("(P gm) h p -> gm P p h", gm=2)
                    for gm in range(2):
                        for Pp in range(16):
                            k.dma("sp", tmp[gm * 64:(gm + 1) * 64, 512 + ri * 256 + Pp * 16:512 + ri * 256 + (Pp + 1) * 16],
                                  cv4[gm, Pp, :, :], W=[tmp])
            m9 = yst[:, 0:144].rearrange("p (a m) -> p a m", a=16)
            k.dma("sp", yst[:, 0:144], m9_d, W=[yst])
            k.act("activation", S(DT), S(LDT), AF.Exp, R=[sm], W=[sm])
            tt(S(LR), S(ARE), S(DT), ALU.mult, [sm], [sm])
            tt(S(LI), S(AIM), S(DT), ALU.mult, [sm], [sm])
            argm = yst[:, 144:288].rearrange("p (a m) -> p a m", a=16)
            magm = yst[:, 288:432].rearrange("p (a m) -> p a m", a=16)
            t1m = yst[:, 432:576].rearrange("p (a m) -> p a m", a=16)
            t2m = yst[:, 576:720].rearrange("p (a m) -> p a m", a=16)
            snm = yst[:, 720:864].rearrange("p (a m) -> p a m", a=16)
            csm = yst[:, 864:1008].rearrange("p (a m) -> p a m", a=16)
            b9 = lambda ap2: ap2.unsqueeze(2).broadcast_to([128, 16, 9])
            tt(argm, m9, b9(S(LI)), ALU.mult, [yst, sm], [yst])
            tt(magm, m9, b9(S(LR)), ALU.mult, [yst, sm], [yst])
            k.act("activation", magm, magm, AF.Exp, R=[yst], W=[yst])
            sincos(yst, argm, yst, snm, yst, csm, yst, t1m, yst, t2m)
            tt(apw[:, 0, :, :], magm, csm, ALU.mult, [yst], [apw])
            tt(apw[:, 1, :, :], magm, snm, ALU.mult, [yst], [apw])
            k.dve("tensor_scalar", S(TA), apw[:, 0, :, 1], -1.0, None, op0=ALU.add, R=[apw], W=[sm])
            tt(S(NRE), S(TA), S(ARE), ALU.mult, [sm], [sm])
            tt(S(TB), apw[:, 1, :, 1], S(AIM), ALU.mult, [apw, sm], [sm])
            tt(S(NRE), S(NRE), S(TB), ALU.add, [sm], [sm])
            tt(S(NIM), apw[:, 1, :, 1], S(ARE), ALU.mult, [apw, sm], [sm])
            tt(S(TB), S(TA), S(AIM), ALU.mult, [sm], [sm])
            tt(S(NIM), S(NIM), S(TB), ALU.subtract, [sm], [sm])
            tt(S(DEN), S(ARE), S(ARE), ALU.mult, [sm], [sm])
            tt(S(TB), S(AIM), S(AIM), ALU.mult, [sm], [sm])
            tt(S(DEN), S(DEN), S(TB), ALU.add, [sm], [sm])
            k.dve("reciprocal", S(DEN), S(DEN), R=[sm], W=[sm])
            tt(S(KRE), S(NRE), S(DEN), ALU.mult, [sm], [sm])
            tt(S(KIM), S(NIM), S(DEN), ALU.mult, [sm], [sm])
            k.dve("tensor_copy", S(RR), magm[:, :, 8], R=[yst], W=[sm])
            Bb = [xt[0][:, 0:256].rearrange("p (a h) -> p a h", a=16), xt[0][:, 256:512].rearrange("p (a h) -> p a h", a=16)]
            U1 = xt[0][:, 512:768].rearrange("p (a h) -> p a h", a=16)
            U2 = xt[0][:, 768:1024].rearrange("p (a h) -> p a h", a=16)
            tt(U1, Bq[0], bc16(S(KRE)), ALU.mult, [tmp, sm], [xt[0]])
            tt(U2, Bq[1], bc16(S(KIM)), ALU.mult, [tmp, sm], [xt[0]])
            tt(Bb[0], U1, U2, ALU.subtract, [xt[0]], [xt[0]])
            tt(U1, Bq[1], bc16(S(KRE)), ALU.mult, [tmp, sm], [xt[0]])
            tt(U2, Bq[0], bc16(S(KIM)), ALU.mult, [tmp, sm], [xt[0]])
            tt(Bb[1], U1, U2, ALU.add, [xt[0]], [xt[0]])
            Xs = [vaug[:, 0:4096].rearrange("p (m a c) -> p m a c", a=16, m=8),
                  vaug[:, 4096:8192].rearrange("p (m a c) -> p m a c", a=16, m=8)]
            k.pool("memset", vaug[:, 0:8192], 0.0, W=[vaug])
            k.pool("memset", Ms[:], 0.0, W=[Ms])
            k.pool("memset", zb[:], 0.0, W=[zb])
            V1 = xt[1][:, 0:256].rearrange("p (a h) -> p a h", a=16)
            V2 = xt[1][:, 256:512].rearrange("p (a h) -> p a h", a=16)
            V3 = xt[1][:, 512:768].rearrange("p (a h) -> p a h", a=16)

            def cmul_strips(src, m, dst_re, dst_im, neg_im):
                ar, ai = bc16(apw[:, 0, :, m]), bc16(apw[:, 1, :, m])
                tt(V1, src[0], ar, ALU.mult, [tmp, xt[0], apw], [xt[1]])
                tt(V2, src[1], ai, ALU.mult, [tmp, xt[0], apw], [xt[1]])
                tt(V3, V1, V2, ALU.subtract, [xt[1]], [xt[1]])
                for gm in range(2):
                    k.act("copy", dst_re(gm), V3[gm * 64:(gm + 1) * 64, :, :], R=[xt[1]], W=[vaug, Ms])
                tt(V1, src[1], ar, ALU.mult, [tmp, xt[0], apw], [xt[1]])
                tt(V2, src[0], ai, ALU.mult, [tmp, xt[0], apw], [xt[1]])
                tt(V3, V1, V2, ALU.add, [xt[1]], [xt[1]])
                if neg_im:
                    k.dve("tensor_scalar", V3, V3, -1.0, None, op0=ALU.mult, R=[xt[1]], W=[xt[1]])
                for gm in range(2):
                    k.act("copy", dst_im(gm), V3[gm * 64:(gm + 1) * 64, :, :], R=[xt[1]], W=[vaug, Ms])

            for m in range(8):
                cmul_strips(Bb, m,
                            lambda gm, m=m: Xs[0][gm * 64:(gm + 1) * 64, m, :, gm * 16:(gm + 1) * 16],
                            lambda gm, m=m: Xs[1][gm * 64:(gm + 1) * 64, m, :, gm * 16:(gm + 1) * 16], False)
            for jj in range(9):
                cmul_strips(Cq, jj,
                            lambda gm, jj=jj: Ms[gm * 64:(gm + 1) * 64, :, jj, 0, gm * 16:(gm + 1) * 16],
                            lambda gm, jj=jj: Ms[gm * 64:(gm + 1) * 64, :, jj, 1, gm * 16:(gm + 1) * 16], True)
            for i in range(4):
                for s_ in range(8):
                    for ri in range(2):
                        k.pe("transpose", PT[:, 0, :], Xs[ri][:, 7 - s_, 4 * i:4 * i + 4, :].rearrange("p a c -> p (a c)"), ident[:, :],
                             R=[vaug, ident], W=[PT])
                        k.act("copy", Wx[:, i, s_, ri, :], PT[:, 0, :], R=[PT], W=[Wx])
            for i in range(4):
                for tau in range(8):
                    pk = P[(i * 8 + tau) % 2]
                    k.pe("matmul", pk[:, 0:128], zb[:, :], zb[:, :], start=True, stop=False, R=[zb], W=[pk], inc=False)
                    for kk in range(4):
                        for ri in range(2):
                            last = (kk == 3 and ri == 1)
                            k.pe("matmul", pk[32 * kk:32 * kk + 32, 32 * kk:32 * kk + 32], Xs[ri][:, tau, 4 * i + kk, :],
                                 Ms[:, 4 * i + kk, 0, ri, :], start=False, stop=last, R=[vaug, Ms], W=[pk], inc=last,
                                 tile_position=(0, 32 * kk), skip_group_check=True)
                    if tau == 0:
                        k.dve("scalar_tensor_tensor", Klag[:, i, tau, :], ident32[:, :], dcol[:, i:i + 1], pk[:, 0:128],
                              op0=ALU.mult, op1=ALU.add, R=[ident32, dcol, pk], W=[Klag])
                    else:
                        k.act("copy", Klag[:, i, tau, :], pk[:, 0:128], R=[pk], W=[Klag])
            k.dma("sp", tmp[:], m64_d, W=[tmp])
            k.dve("tensor_scalar", S(TA), S(LI), 8.0, None, op0=ALU.mult, R=[sm], W=[sm])
            v3 = lambda b: b[:].rearrange("p (a c) -> p a c", a=16)
            tt(v3(yst), v3(tmp), S(TA).unsqueeze(2).broadcast_to([128, 16, 64]), ALU.mult, [tmp, sm], [yst])
            sincos(yst, v3(yst), rs, rs[:], rc, rc[:], xt[0], v3(xt[0]), xt[1], v3(xt[1]))
            k.pool("memset", vaug[:], 1.0, W=[vaug])
            if DBG.get("dump"):
                k.dma("sp", d_apw, apw[:].rearrange("p a b c -> p (a b c)"), R=[apw])
                k.dma("sp", d_sm, sm[:].rearrange("p a b -> p (a b)"), R=[sm])
                k.dma("sp", d_klag, Klag[:].rearrange("p a b c -> p (a b c)"), R=[Klag])
                k.dma("sp", d_wx, Wx[:].rearrange("p a b c d -> p (a b c d)"), R=[Wx])
                k.dma("sp", d_ms, Ms[:].rearrange("p a b c d -> p (a b c d)"), R=[Ms])
                k.dma("sp", d_rc, rc[:].rearrange("p a b -> p (a b)"), R=[rc])
                k.dma("sp", d_rs, rs[:].rearrange("p a b -> p (a b)"), R=[rs])
            _ck(2)

            def bcast_mod(row):
                k.dma("sp", tmp[:], g_norm.broadcast_to([128, D]), W=[tmp])
                for part, dst in ((0, shift_b), (1, sc1_b), (2, gate_b)):
                    for h in range(2):
                        p = P[2 + h]
                        k.pe("matmul", p[:], sel[:, row * 128:(row + 1) * 128],
                             mod[:, part * D + h * 512:part * D + (h + 1) * 512],
                             start=True, stop=True, R=[sel, mod], W=[p])
                        if part == 1:
                            k.dve("scalar_tensor_tensor", dst[:, h * 512:(h + 1) * 512], p[:], 1.0,
                                  tmp[:, h * 512:(h + 1) * 512], op0=ALU.add, op1=ALU.mult, R=[p, tmp], W=[dst])
                        else:
                            k.act("copy", dst[:, h * 512:(h + 1) * 512], p[:], R=[p], W=[dst])

            st = {"piece": 0, "xt": 0}

            def load_piece(col0, ncols):
                b = wring[st["piece"] % 3]
                st["piece"] += 1
                k.dma("pool", b[:, :, 0:ncols], w_in_v[:, :, col0:col0 + ncols], W=[b])
                return b

            def rms_rstd(src, Pn):
                k.act("activation", yst[0:Pn, :], src, AF.Square, accum_out=ss[0:Pn, :], R=[tmp, xt[0], xt[1]], W=[yst, ss])
                k.dve("tensor_scalar", rstd[0:Pn, :], ss[0:Pn, :], 1.0 / D, EPS, op0=ALU.mult, op1=ALU.add,
                      R=[ss], W=[rstd])
                k.act("activation", rstd[0:Pn, :], rstd[0:Pn, :], AF.Sqrt, R=[rstd], W=[rstd])
                k.dve("reciprocal", rstd[0:Pn, :], rstd[0:Pn, :], R=[rstd], W=[rstd])

            def pre_tile(Pn, x_src, sc1, shf, col):
                xb = xt[st["xt"] % 2]
                st["xt"] += 1
                k.dma("sp", xb[0:Pn, :], x_src, W=[xb])
                rms_rstd(xb[0:Pn, :], Pn)
                k.dve("scalar_tensor_tensor", tmp[0:Pn, :], xb[0:Pn, :], rstd[0:Pn, 0:1], sc1,
                      op0=ALU.mult, op1=ALU.mult, R=[xb, rstd, sc1_b, mod], W=[tmp])
                k.dve("tensor_tensor", hb[0:Pn, :], tmp[0:Pn, :], shf, op=ALU.add, R=[tmp, shift_b, mod], W=[hb])
                for c in range(KC):
                    k.pe("transpose", PT[:, c, 0:Pn], hb[0:Pn, c * 128:(c + 1) * 128], ident[0:Pn, 0:Pn],
                         R=[hb, ident], W=[PT], inc=(c == KC - 1))
                k.act("copy", hT[:, :, col:col + Pn], PT[:, :, 0:Pn], R=[PT], W=[hT])

            def logf_from(psrc, Pn, nj, dst):
                k.dve("tensor_tensor", lfz[0:Pn, 0:nj, :], psrc, bfg_b[0:Pn, :].unsqueeze(1).broadcast_to([Pn, nj, 8]),
                      op=ALU.add, R=[P[3], bfg_b], W=[lfz])
                k.act("activation", lfz[0:Pn, 0:nj, :], lfz[0:Pn, 0:nj, :], AF.Exp, scale=-1.0, R=[lfz], W=[lfz])
                k.act("activation", lfz[0:Pn, 0:nj, :], lfz[0:Pn, 0:nj, :], AF.Ln, bias=1.0, R=[lfz], W=[lfz])
                k.dve("tensor_scalar", dst, lfz[0:Pn, 0:nj, :], -1.0, None, op0=ALU.mult, R=[lfz], W=[lf])

            def fm_piece(col0, evac):
                wb = load_piece(col0, 128)
                p = P[st["piece"] % 2]
                for c in range(KC):
                    k.pe("matmul", p[:, 0:BLK], wb[:, c, 0:128], hT[:, c, 0:BLK], start=(c == 0), stop=(c == KC - 1),
                         R=[wb, hT], W=[p], inc=(c == KC - 1))
                evac(p)

            for n in range(DBG["nseq"]):
                bcast_mod(NSMP + n)
                _ck(3)
                for B in range(DBG["nblk"]):
                    t0 = B * BLK
                    r0 = n * SEQ + t0
                    for j in range(4):
                        pre_tile(128, x[r0 + j * 128:r0 + (j + 1) * 128, :], sc1_b[:], shift_b[:], j * 128)
                    _ck(4)
                    for i in range(4):
                        fm_piece(COL_Q + i * 128,
                                 lambda p, i=i: k.act("copy", qT[:, i, :], p[:, 0:BLK], R=[p], W=[qT]))
                    _ck(5)
                    for i in range(4):
                        def ev_k(p, i=i):
                            k.act("copy", kT[:, i, t0:t0 + BLK], p[:, 0:BLK], R=[p], W=[kT])
                            if not DBG["ktr"]:
                                return
                            k.dve("tensor_copy", kTf[:], p[:, 0:BLK], R=[p], W=[kTf])
                            for j in range(4):
                                k.pe("transpose", P[4][:, j * 128:(j + 1) * 128], kTf[:, j * 128:(j + 1) * 128],
                                     ident32[:], R=[kTf, ident32], W=[P[4]], inc=(j == 3))
                            k.dve("tensor_copy", kst[:, :, :],
                                  P[4][:, 0:512].rearrange("p (j c) -> p j c", j=4), R=[P[4]], W=[kst])
                            for j in range(4):
                                k.dma("sp", kp[r0 + j * 128:r0 + (j + 1) * 128, i * 128:(i + 1) * 128], kst[:, j, :], R=[kst])
                        fm_piece(COL_K + i * 128, ev_k)
                    _ck(6)
                    for i in range(4):
                        wb = load_piece(COL_V + i * 128, 128)
                        pv = P[5]
                        for j in range(4):
                            for c in range(KC):
                                k.pe("matmul", pv[:, j * 128:(j + 1) * 128], hT[:, c, j * 128:(j + 1) * 128], wb[:, c, 0:128],
                                     start=(c == 0), stop=(c == KC - 1), R=[hT, wb], W=[pv],
                                     inc=(c == KC - 1 and j == 3))
                        k.dve("tensor_copy", vst[:, :, :],
                              pv[:, 0:512].rearrange("p (j c) -> p j c", j=4), R=[pv], W=[vst])
                        for j in range(4):
                            k.dma("sp", vp[r0 + j * 128:r0 + (j + 1) * 128, i * 128:(i + 1) * 128], vst[:, j, :], R=[vst])
                        for j in range(4):
                            T = B * 4 + j
                            if DBG.get("novaug"):
                                continue
                            for a in range(2):
                                o = T * VA_W + (2 * i + a) * 66
                                k.act("copy", vaug[:, o:o + 64], pv[:, j * 128 + a * 64:j * 128 + (a + 1) * 64],
                                      R=[pv], W=[vaug])
                    _ck(7)
                    for i in range(4):
                        fm_piece(COL_ZA + i * 128,
                                 lambda p, i=i: k.act("activation", mixT[:, i, :], p[:, 0:BLK], AF.Silu, R=[p], W=[mixT]))
                    _ck(8)
                    wb = load_piece(COL_FL, 8)
                    pf = P[3]
                    for j in range(4):
                        for c in range(KC):
                            k.pe("matmul", pf[:, j * 8:(j + 1) * 8], hT[:, c, j * 128:(j + 1) * 128], wb[:, c, 0:8],
                                 start=(c == 0), stop=(c == KC - 1), R=[hT, wb], W=[pf], inc=(c == KC - 1 and j == 3))
                    logf_from(pf[:, 0:32].rearrange("p (j h) -> p j h", j=4), 128, 4, lf[:])
                    for j in range(4):
                        k.dma("sp", lfp[r0 + j * 128:r0 + (j + 1) * 128, :], lf[:, j, :], R=[lf])
                    if DBG["cum"]:
                      k.dve("memset", E5[:, 0, :], 0.0, W=[E5])
                      for j in range(4):
                          k.dve("tensor_tensor", E5[:, j + 1, :], E5[:, j, :], lf[:, j, :], op=ALU.add, R=[E5, lf], W=[E5])
                      pc = P[6]
                      k.pe("matmul", pc[:, 0:32], tri32[:], lf[:].rearrange("p j h -> p (j h)"), start=True, stop=False,
                           R=[tri32, lf], W=[pc], inc=False)
                      k.pe("matmul", pc[:, 0:32], ones32[:], E5[:, 0:4, :].rearrange("p j h -> p (j h)"), start=False, stop=True,
                           R=[ones32, E5], W=[pc], inc=False)
                      k.pe("matmul", pc[:, 32:40], ones32[:], E5[:, 4, :], start=True, stop=True, R=[ones32, E5], W=[pc])
                      k.dve("tensor_scalar", ncum[:, B, :, :], pc[:, 0:32].rearrange("p (j h) -> p j h", j=4), -1.0, None,
                            op0=ALU.mult, R=[pc], W=[ncum])
                      k.dve("tensor_copy", totb[:, B, :], pc[:, 32:40], R=[pc], W=[totb])
                      k.dve("memset", sacc[:], 0.0, W=[sacc])
                      for Bp in range(B, -1, -1):
                          k.dve("tensor_tensor", biask[:, Bp, :, :], ncum[:, Bp, :, :],
                                sacc[:].unsqueeze(1).broadcast_to([128, 4, 8]), op=ALU.add, R=[ncum, sacc], W=[biask])
                          if Bp > 0:
                              k.dve("tensor_tensor", sacc[:], sacc[:], totb[:, Bp - 1, :], op=ALU.add, R=[sacc, totb], W=[sacc])
                    _ck(9)
                    for i in range(4):
                        fm_piece(COL_U + i * 128,
                                 lambda p, i=i: k.act("copy", uT[:, i, :], p[:, 0:BLK], R=[p], W=[uT]))
                    for i in range(4):
                        fm_piece(COL_ZS + i * 128,
                                 lambda p, i=i: k.act("activation", mixT[:, 4 + i, :], p[:, 0:BLK], AF.Silu, R=[p], W=[mixT]))

                    nT = 4 * (B + 1)
                    for h in (range(8) if DBG["att"] else []):
                        i, odd = h // 2, h % 2
                        pb = 64 * odd
                        po = P[2 + odd]

                        def qk(T):
                            jd = T - 4 * B
                            c0 = max(jd, 0) * 128
                            pS = P[T % 2]
                            k.pe("matmul", pS[:, c0:BLK], kT[pb:pb + 64, i, T * 128:(T + 1) * 128], qT[pb:pb + 64, i, c0:BLK],
                                 start=True, stop=True, R=[kT, qT], W=[pS], tile_position=(pb, 0))
                            return c0
                        c0n = qk(0)
                        for T in range(nT):
                            c0 = c0n
                            pS = P[T % 2]
                            pt = pTb[T % 2]
                            if T + 1 < nT:
                                c0n = qk(T + 1)
                            Bp, j = T // 4, T % 4
                            k.act("activation", pt[:, c0:BLK], pS[:, c0:BLK], AF.Exp, scale=0.125,
                                  bias=biask[:, Bp, j, h:h + 1], R=[pS, biask], W=[pt])
                            if Bp == B:
                                k.dve("tensor_tensor", pt[:, c0:c0 + 128], pt[:, c0:c0 + 128], trib[:], op=ALU.mult,
                                      R=[pt, trib], W=[pt])
                            o = T * VA_W + h * 66
                            k.pe("matmul", po[0:65, c0:BLK], vaug[:, o:o + 65], pt[:, c0:BLK],
                                 start=(T == 0), stop=(T == nT - 1), R=[vaug, pt], W=[po], inc=(T == nT - 1))
                        k.dve("reciprocal", rec[64:65, :], po[64:65, 0:BLK], R=[po], W=[rec])
                        k.pe("matmul", P[4][pb:pb + 64, 0:BLK], ones32[64:65, 0:64], rec[64:65, :],
                             start=True, stop=True, R=[ones32, rec], W=[P[4]], tile_position=(64, pb))
                        k.act("copy", bc_sb[pb:pb + 64, :], P[4][pb:pb + 64, 0:BLK], R=[P[4]], W=[bc_sb])
                        if not odd:
                            osrc = po
                        else:
                            k.act("copy", o_sb[0:64, :], po[0:64, 0:BLK], R=[po], W=[o_sb])
                            k.pe("matmul", P[6][:, 0:BLK], shiftm[:, :], o_sb[0:64, :], start=True, stop=True,
                                 R=[shiftm, o_sb], W=[P[6]])
                            osrc = P[6]
                        k.dve("tensor_tensor", attf[pb:pb + 64, :], osrc[pb:pb + 64, 0:BLK], bc_sb[pb:pb + 64, :], op=ALU.mult,
                              R=[osrc, bc_sb], W=[attf])
                        k.dve("tensor_tensor", mixT[pb:pb + 64, i, :], attf[pb:pb + 64, :], mixT[pb:pb + 64, i, :], op=ALU.mult,
                              R=[attf, mixT], W=[mixT])

                    if DBG.get("s5off"):
                        k.dve("memset", mixT[:, 4:8, :], 0.0, W=[mixT])
                    else:
                        if B == 0:
                            k.dve("memset", hin[:], 0.0, W=[hin])
                        NC_ = BLK // 8
                        S2, Gin, TT_, Gout, Hp = attf, bc_sb, rec, kTf, sg
                        v4 = lambda b: b[:, 0:BLK].rearrange("p (r a c) -> p r a c", r=2, a=4)
                        for i in range(4):
                            tb = lambda t: t[:, 4 * i:4 * i + 4, :]
                            Sv, Gi, Tv, Go = v4(S2), v4(Gin), v4(TT_), v4(Gout)
                            for kk in range(4):
                                PS = P[2 + kk]
                                for ri in range(2):
                                    for s_ in range(8):
                                        last = (s_ == 7 and ri == 1)
                                        k.pe("matmul", PS[:, ri * NC_:(ri + 1) * NC_],
                                             Wx[32 * kk:32 * kk + 32, i, s_, ri, :], uT[32 * kk:32 * kk + 32, i, s_:BLK:8],
                                             start=(s_ == 0), stop=(s_ == 7), R=[Wx, uT], W=[PS], inc=last,
                                             tile_position=(32 * kk, 0))
                                k.act("copy", Sv[:, :, kk, :], PS[:, 0:2 * NC_].rearrange("p (r c) -> p r c", r=2), R=[PS], W=[S2])
                            tt(Tv[:, 0], Sv[:, 0], tb(rc), ALU.mult, [S2, rc], [TT_])
                            tt(Tv[:, 1], Sv[:, 1], tb(rs), ALU.mult, [S2, rs], [TT_])
                            tt(Gi[:, 0], Tv[:, 0], Tv[:, 1], ALU.add, [TT_], [Gin])
                            tt(Tv[:, 0], Sv[:, 1], tb(rc), ALU.mult, [S2, rc], [TT_])
                            tt(Tv[:, 1], Sv[:, 0], tb(rs), ALU.mult, [S2, rs], [TT_])
                            tt(Gi[:, 1], Tv[:, 0], Tv[:, 1], ALU.subtract, [TT_], [Gin])
                            for ri in range(2):
                                for kk in range(4):
                                    Pp = 4 * i + kk
                                    k.dve("tensor_tensor_scan", Go[:, ri, kk, :], sm[:, RR, Pp:Pp + 1].broadcast_to([128, NC_]),
                                          Gi[:, ri, kk, :], hin[:, ri, Pp:Pp + 1], op0=ALU.mult, op1=ALU.add,
                                          R=[sm, Gin, hin], W=[Gout])
                            tt(Tv[:, 0], Go[:, 0], tb(rc), ALU.mult, [Gout, rc], [TT_])
                            tt(Tv[:, 1], Go[:, 1], tb(rs), ALU.mult, [Gout, rs], [TT_])
                            tt(Sv[:, 0], Tv[:, 0], Tv[:, 1], ALU.subtract, [TT_], [S2])
                            tt(Tv[:, 0], Go[:, 1], tb(rc), ALU.mult, [Gout, rc], [TT_])
                            tt(Tv[:, 1], Go[:, 0], tb(rs), ALU.mult, [Gout, rs], [TT_])
                            tt(Sv[:, 1], Tv[:, 0], Tv[:, 1], ALU.add, [TT_], [S2])
                            Hv = Hp[:, 0:BLK].rearrange("p (r a c) -> p r a c", r=2, a=4)
                            k.act("copy", Hv[:, :, :, 0], hin[:, :, 4 * i:4 * i + 4], R=[hin], W=[Hp])
                            k.act("copy", Hv[:, :, :, 1:NC_], Sv[:, :, :, 0:NC_ - 1], R=[S2], W=[Hp])
                            k.dve("tensor_copy", hin[:, :, 4 * i:4 * i + 4], Sv[:, :, :, NC_ - 1], R=[S2], W=[hin])
                            for j in range(8):
                                PY = P[j % 2]
                                for s_ in range(j + 1):
                                    k.pe("matmul", PY[:, 0:NC_], Klag[:, i, j - s_, :], uT[:, i, s_:BLK:8],
                                         start=(s_ == 0), stop=False, R=[Klag, uT], W=[PY], inc=False, skip_group_check=True)
                                for kk in range(4):
                                    for ri in range(2):
                                        last = (kk == 3 and ri == 1)
                                        k.pe("matmul", PY[32 * kk:32 * kk + 32, 0:NC_], Ms[:, 4 * i + kk, j + 1, ri, :],
                                             Hv[:, ri, kk, :], start=False, stop=last, R=[Ms, Hp], W=[PY], inc=last,
                                             tile_position=(0, 32 * kk), skip_group_check=True)
                                if j % 2 == 0:
                                    k.act("copy", Gin[:, j:BLK:8], PY[:, 0:NC_], R=[PY], W=[Gin])
                                else:
                                    k.dve("tensor_copy", Gin[:, j:BLK:8], PY[:, 0:NC_], R=[PY], W=[Gin])
                            Y, Z = Gin[:, 0:BLK], TT_[:, 0:BLK]
                            tt(Z, Y, Y, ALU.mult, [Gin], [TT_])
                            k.dve("tensor_scalar", Z, Z, 0.044715, 1.0, op0=ALU.mult, op1=ALU.add, R=[TT_], W=[TT_])
                            tt(Z, Z, Y, ALU.mult, [TT_, Gin], [TT_])
                            k.act("activation", Z, Z, AF.Sigmoid, scale=1.5957691216057308, R=[TT_], W=[TT_])
                            tt(gT[:, i, :], Y, Z, ALU.mult, [Gin, TT_], [gT])
                        for ct in range(4):
                            pg = P[ct % 2]
                            for c in range(4):
                                k.pe("matmul", pg[:, 0:BLK], w_glu_bf[:, c, ct * 128:(ct + 1) * 128], gT[:, c, :],
                                     start=(c == 0), stop=(c == 3), R=[w_glu_bf, gT], W=[pg], inc=(c == 3))
                            k.act("activation", sg[:, :], pg[:, 0:BLK], AF.Sigmoid, bias=bglu[:, ct:ct + 1], R=[pg, bglu], W=[sg])
                            tt(sg[:, :], sg[:, :], gT[:, ct, :], ALU.mult, [sg, gT], [sg])
                            tt(mixT[:, 4 + ct, :], sg[:, :], mixT[:, 4 + ct, :], ALU.mult, [sg, mixT], [mixT])
                        if B == NBLK - 1:
                            with nc.allow_non_contiguous_dma(reason="final S5 state, small"):
                                l1o = lambda d: d.rearrange("(P gm) p -> (gm p) P", gm=2)
                                k.dma("sp", l1o(hre_o[n]), hin[:, 0, :], R=[hin])
                                k.dma("sp", l1o(him_o[n]), hin[:, 1, :], R=[hin])
                    for j in (range(4) if DBG["post"] else []):
                        xb = xt[st["xt"] % 2]
                        st["xt"] += 1
                        k.dma("sp", xb[:], x[r0 + j * 128:r0 + (j + 1) * 128, :], W=[xb])
                        for hf in range(2):
                            p = P[5 + hf]
                            for c in range(KC):
                                k.pe("matmul", p[:, :], mixT[:, c, j * 128:(j + 1) * 128], w_out_bf[:, c, hf * 512:(hf + 1) * 512],
                                     start=(c == 0), stop=(c == KC - 1), R=[mixT, w_out_bf], W=[p], inc=(c == KC - 1))
                            k.dve("tensor_tensor", tmp[:, hf * 512:(hf + 1) * 512], p[:, :], gate_b[:, hf * 512:(hf + 1) * 512],
                                  op=ALU.mult, R=[p, gate_b], W=[tmp])
                        k.dve("tensor_tensor", tmp[:], tmp[:], xb[:], op=ALU.add, R=[tmp, xb], W=[tmp])
                        rms_rstd(tmp[:], 128)
                        k.dve("scalar_tensor_tensor", yst[:], tmp[:], rstd[:, 0:1], gfin_b[:], op0=ALU.mult, op1=ALU.mult,
                              R=[tmp, rstd, gfin_b], W=[yst])
                        k.dma("sp", yp[r0 + j * 128:r0 + (j + 1) * 128, :], yst[:], R=[yst])


            if DBG.get("sample", True):
                k.barrier()
                NP_ = 128
                GP = 8
                kring = [k.wrap("kring%d" % i_, kT[:, 2 * i_:2 * i_ + 2, :].rearrange("p a (b c) -> p (a b) c", c=512)) for i_ in range(2)]
                vring = [k.wrap("vring%d" % i_, vaug[:, 4096 * i_:4096 * (i_ + 1)].rearrange("p (b c) -> p b c", c=512)) for i_ in range(2)]
                LFb, Fb, rows, prod = tmp, yst, xt[0], xt[1]
                LF = LFb[:].rearrange("p (j h) -> p j h", h=8)
                Fv = Fb[:].rearrange("p (j h) -> p j h", h=8)
                idx = sb("idx", [128, 128], I32)
                sel8 = sb("sel8", [8, NSMP * NSMP], F32)
                sml = sb("sml", [128, 64], F32)
                onesb = sb("onesb", [128, 8], BF16)
                qb = pTb[0]
                pbf = pTb[1]
                k.dma("sp", sel8[:], sel8_d, W=[sel8])
                k.dve("memset", onesb[:], 1.0, W=[onesb])
                RG = 8
                NG = 128 // RG
                sutB, bd8B = bc_sb, rec
                k.dma("sp", sutB[:, 0:128], sut_d, W=[sutB])
                k.dma("sp", bd8B[0:8, :], bd8_d, W=[bd8B])
                idxL = idx[:, 0:NSMP]
                idxK = idx[:, 64:64 + NSMP * NG].rearrange("p (s g) -> p s g", s=NSMP)
                with nc.allow_non_contiguous_dma(reason="page table, pages on partitions"):
                    k.dma("sp", idxL, pt_d.rearrange("o (s j) -> j (o s)", s=NSMP), W=[idx])
                k.dve("tensor_copy", sml[:, 0:NSMP], idxL, R=[idx], W=[sml])
                for g_ in range(NG):
                    k.dve("tensor_scalar", sml[:, 8:8 + NSMP], sml[:, 0:NSMP], float(NG), float(g_), op0=ALU.mult, op1=ALU.add,
                          R=[sml], W=[sml])
                    k.dve("tensor_copy", idxK[:, :, g_], sml[:, 8:8 + NSMP], R=[sml], W=[idx])
                ck_v = ck_d.rearrange("(n r) c -> n (r c)", r=RG)
                cv_v = cv_d.rearrange("(n r) c -> n (r c)", r=RG)
                clf_v = clf_d.rearrange("(n r) c -> n (r c)", r=128)

                R4 = slice(0, NSMP)
                k.dma("sp", prod[R4, :], g_norm.broadcast_to([NSMP, D]), W=[prod])
                k.dve("scalar_tensor_tensor", mod[R4, D:2 * D], mod[R4, D:2 * D], 1.0, prod[R4, :],
                      op0=ALU.add, op1=ALU.mult, R=[mod, prod], W=[mod])
                pre_tile(NSMP, xs, mod[R4, D:2 * D], mod[R4, 0:D], 0)
                vrow = attf
                for ci, (col0, dst, dB) in enumerate([(COL_Q + i_ * 128, rows[R4, i_ * 128:(i_ + 1) * 128], rows) for i_ in range(8)]
                                                     + [(COL_V + i_ * 128, vrow[R4, i_ * 128:(i_ + 1) * 128], vrow) for i_ in range(4)]):
                    wb = load_piece(col0, 128)
                    p = P[5 + ci % 2]
                    for c in range(KC):
                        k.pe("matmul", p[R4, 0:128], hT[:, c, 0:NSMP], wb[:, c, 0:128], start=(c == 0), stop=(c == KC - 1),
                             R=[hT, wb], W=[p], inc=(c == KC - 1))
                    k.act("copy", dst, p[R4, 0:128], R=[p], W=[dB])
                wb = load_piece(COL_FL, 8)
                for c in range(KC):
                    k.pe("matmul", P[3][R4, 0:8], hT[:, c, 0:NSMP], wb[:, c, 0:8], start=(c == 0), stop=(c == KC - 1),
                         R=[hT, wb], W=[P[3]], inc=(c == KC - 1))
                logf_from(P[3][R4, 0:8].rearrange("p (j h) -> p j h", j=1), NSMP, 1, lf[R4, 0:1, :])
                k.dma("sp", ks, rows[R4, 512:1024], R=[rows])
                k.dma("sp", vs, vrow[R4, :], R=[vrow])
                k.dma("sp", lfs, lf[R4, 0, :], R=[lf])
                fms = sb("fms", [128, 3, 4, NSMP], BF16)
                for gi, (colb, fn) in enumerate(((COL_ZA, AF.Silu), (COL_U, AF.Copy), (COL_ZS, AF.Silu))):
                    for i_ in range(4):
                        wb = load_piece(colb + i_ * 128, 128)
                        p = P[5 + i_ % 2]
                        for c in range(KC):
                            k.pe("matmul", p[:, 0:NSMP], wb[:, c, 0:128], hT[:, c, 0:NSMP], start=(c == 0), stop=(c == KC - 1),
                                 R=[wb, hT], W=[p], inc=(c == KC - 1))
                        if fn == AF.Copy:
                            k.act("copy", fms[:, gi, i_, :], p[:, 0:NSMP], R=[p], W=[fms])
                        else:
                            k.act("activation", fms[:, gi, i_, :], p[:, 0:NSMP], fn, R=[p], W=[fms])
                pself = sml[R4, 8:16]
                k.dve("tensor_tensor", prod[R4, 0:512], rows[R4, 0:512], rows[R4, 512:1024], op=ALU.mult, R=[rows], W=[prod])
                k.dve("tensor_reduce", sml[R4, 8:16], prod[R4, 0:512].rearrange("p (h d) -> p h d", h=8), axis=AX.X, op=ALU.add,
                      R=[prod], W=[sml])
                k.act("activation", sml[R4, 8:16], sml[R4, 8:16], AF.Exp, scale=0.125, R=[sml], W=[sml])
                k.act("activation", rows[R4, 0:512], rows[R4, 0:512], AF.Copy, scale=0.125, R=[rows], W=[rows])

                PO, PD, PA = P[2], P[3], P[4]
                for s_ in range(NSMP):
                    j0 = s_ * NP_
                    k.dma("pool", LFb[:, :], clf_v, R=[idx], W=[LFb], fn_indirect=idx[:, s_:s_ + 1])
                    for hh in range(8):
                        k.dve("tensor_tensor_scan", Fv[:, :, hh], ones32[:, 0:NP_], LF[:, :, hh], 0.0,
                              op0=ALU.mult, op1=ALU.add, R=[ones32, LFb], W=[Fb])
                    k.dve("tensor_copy", sml[:, 16:24], Fv[:, NP_ - 1, :], R=[Fb], W=[sml])
                    k.dve("scalar_tensor_tensor", Fv[:, :, :], Fv[:, :, :], -1.0,
                          sml[:, 16:24].unsqueeze(1).broadcast_to([128, NP_, 8]), op0=ALU.mult, op1=ALU.add,
                          R=[Fb, sml], W=[Fb])
                    k.pe("matmul", P[0][:, 0:8], sutB[:, 0:128], sml[:, 16:24], start=True, stop=False,
                         R=[sutB, sml], W=[P[0]], inc=False)
                    k.pe("matmul", P[0][:, 0:8], sel[0:NSMP, s_ * 128:(s_ + 1) * 128], lf[R4, 0, :], start=False, stop=True,
                         R=[sel, lf], W=[P[0]])
                    k.dve("tensor_tensor", Fv[:, :, :], Fv[:, :, :], P[0][:, 0:8].unsqueeze(1).broadcast_to([128, NP_, 8]),
                          op=ALU.add, R=[Fb, P[0]], W=[Fb])
                    k.pe("matmul", PA[:, :], sel[0:NSMP, s_ * 128:(s_ + 1) * 128], rows[R4, 0:512], start=True, stop=True,
                         R=[sel, rows], W=[PA])
                    k.act("copy", qb[:, :], PA[:, :], R=[PA], W=[qb])
                    for g_ in range(NP_ // GP):
                        kr, vr = kring[g_ % 2], vring[g_ % 2]
                        k.dma("pool", kr[:, :, :].rearrange("p a c -> p (a c)"), ck_v, R=[idx], W=[kr],
                              fn_indirect=idxK[:, s_, g_:g_ + 1])
                        k.dma("pool", vr[:, :, :].rearrange("p a c -> p (a c)"), cv_v, R=[idx], W=[vr],
                              fn_indirect=idxK[:, s_, g_:g_ + 1])
                        for jp in range(0, GP, 2):
                            k.dve("tensor_tensor", prod[:, :].rearrange("p (a c) -> p a c", a=2), kr[:, jp:jp + 2, :],
                                  qb[:, :].unsqueeze(1).broadcast_to([128, 2, 512]), op=ALU.mult, R=[kr, qb], W=[prod])
                            k.dve("tensor_reduce", scg[:, jp * 8:(jp + 2) * 8], prod[:, :].rearrange("p (a d) -> p a d", d=64),
                                  axis=AX.X, op=ALU.add, R=[prod], W=[scg])
                        k.dve("tensor_tensor", scg[:, :], scg[:, :], Fb[:, g_ * GP * 8:(g_ + 1) * GP * 8], op=ALU.add,
                              R=[scg, Fb], W=[scg])
                        k.act("activation", pbf[:, 0:GP * 8], scg[:, :], AF.Exp, R=[scg], W=[pbf])
                        for jj in range(GP):
                            j = g_ * GP + jj
                            k.pe("matmul", PO[0:8, :], pbf[:, jj * 8:(jj + 1) * 8], vr[:, jj, :], start=(j == 0), stop=(j == NP_ - 1),
                                 R=[pbf, vr], W=[PO], inc=False)
                            k.pe("matmul", PD[0:8, 0:1], pbf[:, jj * 8:(jj + 1) * 8], onesb[:, 0:1], start=(j == 0), stop=(j == NP_ - 1),
                                 R=[pbf, onesb], W=[PD], inc=(jj == GP - 1))
                    k.dve("tensor_tensor", kTf[0:8, :], PO[0:8, :], bd8B[0:8, :], op=ALU.mult, R=[PO, bd8B], W=[kTf])
                    k.dve("tensor_scalar", sml[0:8, 32:40], ident32[0:8, 0:8], PD[0:8, 0:1], None, op0=ALU.mult, R=[ident32, PD], W=[sml])
                    k.pe("matmul", P[5][R4, :], sel8[:, s_ * NSMP:(s_ + 1) * NSMP], kTf[0:8, :], start=(s_ == 0), stop=(s_ == NSMP - 1),
                         R=[sel8, kTf], W=[P[5]], inc=False)
                    k.pe("matmul", P[6][R4, 0:8], sel8[:, s_ * NSMP:(s_ + 1) * NSMP], sml[0:8, 32:40], start=(s_ == 0), stop=(s_ == NSMP - 1),
                         R=[sel8, sml], W=[P[6]])
                orow = kTf
                k.dve("tensor_tensor", prod[R4, 0:512].rearrange("p (h d) -> p h d", h=8), vrow[R4, :].rearrange("p (h d) -> p h d", h=8),
                      sml[R4, 8:16].unsqueeze(2).broadcast_to([NSMP, 8, 64]), op=ALU.mult, R=[vrow, sml], W=[prod])
                k.dve("tensor_tensor", prod[R4, 0:512], prod[R4, 0:512], P[5][R4, :], op=ALU.add, R=[prod, P[5]], W=[prod])
                k.dve("tensor_tensor", sml[R4, 40:48], sml[R4, 8:16], P[6][R4, 0:8], op=ALU.add, R=[sml, P[6]], W=[sml])
                k.dve("reciprocal", sml[R4, 40:48], sml[R4, 40:48], R=[sml], W=[sml])
                k.dve("tensor_tensor", orow[R4, :].rearrange("p (h d) -> p h d", h=8), prod[R4, 0:512].rearrange("p (h d) -> p h d", h=8),
                      sml[R4, 40:48].unsqueeze(2).broadcast_to([NSMP, 8, 64]), op=ALU.mult, R=[prod, sml], W=[orow])
                mixs = sb("mixs", [128, KC, NSMP], BF16)
                for i_ in range(4):
                    k.pe("transpose", P[0][:, i_ * NSMP:(i_ + 1) * NSMP], orow[R4, i_ * 128:(i_ + 1) * 128], ident32[R4, R4],
                         R=[orow, ident32], W=[P[0]], inc=(i_ == 3))
                k.dve("tensor_tensor", mixs[:, 0:4, :], P[0][:, 0:4 * NSMP].rearrange("p (a s) -> p a s", a=4), fms[:, 0, :, :],
                      op=ALU.mult, R=[P[0], fms], W=[mixs])
                h0b = sb("h0b", [128, 2, 16, NSMP], BF16)
                h0v = kst[:, 0, :].rearrange("p (r a s) -> p r a s", r=2, a=16)
                hnv = kst[:, 1, :].rearrange("p (r a s) -> p r a s", r=2, a=16)
                with nc.allow_non_contiguous_dma(reason="S5 state re-layout, small"):
                    for ri, hd in enumerate((h0re_d, h0im_d)):
                        for s_ in range(NSMP):
                            k.dma("sp", h0v[:, ri, :, s_], hd[s_].rearrange("(P gm) p -> (gm p) P", gm=2), W=[kst])
                k.act("copy", h0b[:], h0v, R=[kst], W=[h0b])
                uS = fms[:, 1, :, :]
                for kk in range(4):
                    PSk = P[2 + kk]
                    for i_ in range(4):
                        for ri in range(2):
                            k.pe("matmul", PSk[:, (i_ * 2 + ri) * NSMP:(i_ * 2 + ri + 1) * NSMP], Wx[32 * kk:32 * kk + 32, i_, 7, ri, :],
                                 fms[32 * kk:32 * kk + 32, 1, i_, :], start=True, stop=True, R=[Wx, fms], W=[PSk],
                                 inc=(i_ == 3 and ri == 1), tile_position=(32 * kk, 0))
                    k.act("copy", hnv[:, :, kk:16:4, :].rearrange("p r a s -> p a r s"),
                          PSk[:, 0:8 * NSMP].rearrange("p (a r s) -> p a r s", a=4, r=2), R=[PSk], W=[kst])
                a1r = apw[:, 0, :, 1].unsqueeze(2).broadcast_to([128, 16, NSMP])
                a1i = apw[:, 1, :, 1].unsqueeze(2).broadcast_to([128, 16, NSMP])
                T1 = sml[:, 0:64].rearrange("p (a s) -> p a s", a=16)
                tt(T1, h0v[:, 0], a1r, ALU.mult, [kst, apw], [sml]); tt(hnv[:, 0], hnv[:, 0], T1, ALU.add, [kst, sml], [kst])
                tt(T1, h0v[:, 1], a1i, ALU.mult, [kst, apw], [sml]); tt(hnv[:, 0], hnv[:, 0], T1, ALU.subtract, [kst, sml], [kst])
                tt(T1, h0v[:, 1], a1r, ALU.mult, [kst, apw], [sml]); tt(hnv[:, 1], hnv[:, 1], T1, ALU.add, [kst, sml], [kst])
                tt(T1, h0v[:, 0], a1i, ALU.mult, [kst, apw], [sml]); tt(hnv[:, 1], hnv[:, 1], T1, ALU.add, [kst, sml], [kst])
                with nc.allow_non_contiguous_dma(reason="S5 state re-layout, small"):
                    for ri, hd in enumerate((hres_o, hims_o)):
                        for s_ in range(NSMP):
                            k.dma("sp", hd[s_].rearrange("(P gm) p -> (gm p) P", gm=2), hnv[:, ri, :, s_], R=[kst])
                for i_ in range(4):
                    PY = P[i_ % 2]
                    k.pe("matmul", PY[:, 0:NSMP], Klag[:, i_, 0, :], fms[:, 1, i_, :], start=True, stop=False,
                         R=[Klag, fms], W=[PY], inc=False, skip_group_check=True)
                    for kk in range(4):
                        for ri in range(2):
                            last = (kk == 3 and ri == 1)
                            k.pe("matmul", PY[32 * kk:32 * kk + 32, 0:NSMP], Ms[:, 4 * i_ + kk, 1, ri, :], h0b[:, ri, 4 * i_ + kk, :],
                                 start=False, stop=last, R=[Ms, h0b], W=[PY], inc=last, tile_position=(0, 32 * kk),
                                 skip_group_check=True)
                    Yv, Zv = sml[:, 0:NSMP], sml[:, 8:8 + NSMP]
                    k.act("copy", Yv, PY[:, 0:NSMP], R=[PY], W=[sml])
                    tt(Zv, Yv, Yv, ALU.mult, [sml], [sml])
                    k.dve("tensor_scalar", Zv, Zv, 0.044715, 1.0, op0=ALU.mult, op1=ALU.add, R=[sml], W=[sml])
                    tt(Zv, Zv, Yv, ALU.mult, [sml], [sml])
                    k.act("activation", Zv, Zv, AF.Sigmoid, scale=1.5957691216057308, R=[sml], W=[sml])
                    tt(fms[:, 1, i_, :], Yv, Zv, ALU.mult, [sml], [fms])
                for ct in range(4):
                    pg = P[ct % 2]
                    for c in range(4):
                        k.pe("matmul", pg[:, 0:NSMP], w_glu_bf[:, c, ct * 128:(ct + 1) * 128], fms[:, 1, c, :],
                             start=(c == 0), stop=(c == 3), R=[w_glu_bf, fms], W=[pg], inc=(c == 3))
                    k.act("activation", sml[:, 16:16 + NSMP], pg[:, 0:NSMP], AF.Sigmoid, bias=bglu[:, ct:ct + 1], R=[pg, bglu], W=[sml])
                    tt(sml[:, 16:16 + NSMP], sml[:, 16:16 + NSMP], fms[:, 1, ct, :], ALU.mult, [sml, fms], [sml])
                    tt(mixs[:, 4 + ct, :], sml[:, 16:16 + NSMP], fms[:, 2, ct, :], ALU.mult, [sml, fms], [mixs])
                k.dma("sp", xt[1][R4, :], xs, W=[xt[1]])
                for hf in range(2):
                    p = P[5 + hf]
                    for c in range(KC):
                        k.pe("matmul", p[R4, :], mixs[:, c, :], w_out_bf[:, c, hf * 512:(hf + 1) * 512], start=(c == 0), stop=(c == KC - 1),
                             R=[mixs, w_out_bf], W=[p], inc=(c == KC - 1))
                    k.dve("tensor_tensor", tmp[R4, hf * 512:(hf + 1) * 512], p[R4, :], mod[R4, 2 * D + hf * 512:2 * D + (hf + 1) * 512],
                          op=ALU.mult, R=[p, mod], W=[tmp])
                k.dve("tensor_tensor", tmp[R4, :], tmp[R4, :], xt[1][R4, :], op=ALU.add, R=[tmp, xt[1]], W=[tmp])
                rms_rstd(tmp[R4, :], NSMP)
                k.dve("scalar_tensor_tensor", yst[R4, :], tmp[R4, :], rstd[R4, 0:1], gfin_b[R4, :], op0=ALU.mult, op1=ALU.mult,
                      R=[tmp, rstd, gfin_b], W=[yst])
                k.dma("sp", ys_o, yst[R4, :], R=[yst])
        except _Stop:
            pass
        k.finish()
        print("instructions:", k.nins, "sbuf bytes/partition:", acct["bytes"])
    return nc


_CONST = {}


def _consts():
    if not _CONST:
        sel = np.zeros((6, 6, 128), np.float32)
        for r in range(6):
            sel[r, r, :] = 1.0
        _CONST["ident"] = np.eye(128, dtype=np.float32)
        _CONST["tri"] = np.triu(np.ones((128, 128), np.float32))
        _CONST["sel"] = sel.reshape(6, 6 * 128)
        sh = np.zeros((64, 128), np.float32)
        sh[np.arange(64), 64 + np.arange(64)] = 1.0
        _CONST["shift"] = sh
        _CONST["sut"] = np.tril(np.ones((128, 128), np.float32), -1)
        bd = np.zeros((8, 8, 64), np.float32)
        bd[np.arange(8), np.arange(8), :] = 1.0
        _CONST["bd8"] = bd.reshape(8, 512)
        s8 = np.zeros((8, 4, 4), np.float32)
        s8[:, np.arange(4), np.arange(4)] = 1.0
        _CONST["sel8"] = s8.reshape(8, 16)
        _CONST["m9"] = np.tile(np.arange(9, dtype=np.float32), (128, 16)).reshape(128, 144)
        _CONST["m64"] = np.tile(np.arange(1, 65, dtype=np.float32), (128, 16)).reshape(128, 1024)
    return _CONST


def kernel(x_prompt, x_sample, c_prompt, c_sample, cache_k, cache_v, cache_logf,
           state_ssm_re, state_ssm_im, page_table, g_norm, w_ada, b_ada, w_in, b_fgate,
           a_re, a_im, log_dt, b_re, b_im, c_re, c_im, d_skip, w_glu, b_glu, w_out, g_final):
    f = lambda a: np.ascontiguousarray(np.asarray(a, dtype=np.float32))
    x_prompt, x_sample, c_prompt, c_sample = f(x_prompt), f(x_sample), f(c_prompt), f(c_sample)
    cst = _consts()
    nc = build_program()
    shared = {
        "g_norm_d": f(g_norm).reshape(1, D), "g_final_d": f(g_final).reshape(1, D),
        "w_ada_d": f(w_ada).reshape(D, 3 * D), "b_ada_d": f(b_ada).reshape(1, 3 * D),
        "w_in_d": f(w_in).reshape(D, DIN), "w_out_d": f(w_out).reshape(D, D),
        "w_glu_d": f(w_glu).reshape(512, 512), "b_glu_d": f(b_glu).reshape(512),
        "b_fgate_d": f(b_fgate).reshape(1, 8),
        "ident_c": cst["ident"], "tri_c": cst["tri"], "sel_c": cst["sel"], "shift_c": cst["shift"], "m9_c": cst["m9"], "m64_c": cst["m64"],
        **({"sut_c": cst["sut"], "bd8_c": cst["bd8"], "sel8_c": cst["sel8"]} if DBG.get("sample", True) else {}),
        "a_re_d": f(a_re).reshape(32, 64), "a_im_d": f(a_im).reshape(32, 64), "log_dt_d": f(log_dt).reshape(32),
        "b_re_d": f(b_re).reshape(32, 64, 16), "b_im_d": f(b_im).reshape(32, 64, 16),
        "c_re_d": f(c_re).reshape(32, 16, 64), "c_im_d": f(c_im).reshape(32, 16, 64), "d_skip_d": f(d_skip).reshape(32, 16),
    }
    in_maps = []
    for i in range(NCORES):
        m = dict(shared)
        m["x_d"] = x_prompt[NSEQ * i:NSEQ * (i + 1)].reshape(NTOK, D)
        m["xs_d"] = x_sample[NSMP * i:NSMP * (i + 1)].reshape(NSMP, D)
        if DBG.get("sample", True):
            m["ck_d"] = np.asarray(cache_k, dtype=np.float32).reshape(5120 * 128, 512)
            m["cv_d"] = np.asarray(cache_v, dtype=np.float32).reshape(5120 * 128, 512)
            m["clf_d"] = np.asarray(cache_logf, dtype=np.float32).reshape(5120 * 128, 8)
            m["pt_d"] = np.ascontiguousarray(np.asarray(page_table, dtype=np.int32)[NSMP * i:NSMP * (i + 1)]).reshape(1, NSMP * 128)
            m["h0re_d"] = f(state_ssm_re)[0, NSMP * i:NSMP * (i + 1)]
            m["h0im_d"] = f(state_ssm_im)[0, NSMP * i:NSMP * (i + 1)]
        m["cnd"] = np.concatenate([c_sample[NSMP * i:NSMP * (i + 1)], c_prompt[NSEQ * i:NSEQ * (i + 1)]], 0)
        in_maps.append(m)
    in_maps = in_maps[:DBG["cores"]]
    res = run_bass_kernel_spmd(nc, in_maps, core_ids=list(range(DBG["cores"])), **({"trace": True} if DBG.get("trace") else {}))
    if DBG.get("trace"):
        DBG["exec_ns"] = res.exec_time_ns
    R = res.results
    if DBG.get("dump"):
        DBG["dumped"] = {kk: np.asarray(R[0][kk]) for kk in R[0] if kk.startswith("d_")}
    cat = lambda key: np.concatenate([r[key] for r in R], 0)
    y_prompt = cat("yp").reshape(-1, SEQ, D)
    k_prompt = cat("kp").reshape(1, -1, SEQ, 8, 64)
    v_prompt = cat("vp").reshape(1, -1, SEQ, 8, 64)
    logf_prompt = cat("lfp").reshape(1, -1, SEQ, 8)
    hre_p = cat("hre_p").reshape(1, -1, 32, 64)
    him_p = cat("him_p").reshape(1, -1, 32, 64)
    return (y_prompt, cat("ys").reshape(-1, 1, D), k_prompt, v_prompt, logf_prompt, hre_p, him_p,
            cat("ks").reshape(1, -1, 1, 8, 64), cat("vs").reshape(1, -1, 1, 8, 64), cat("lfs").reshape(1, -1, 1, 8),
            cat("hre_s").reshape(1, -1, 32, 64), cat("him_s").reshape(1, -1, 32, 64))
```

```python
import numpy as np
from contextlib import ExitStack
import concourse.bass as bass
import concourse.mybir as mybir
from concourse.bass_utils import run_bass_kernel_spmd

F32 = mybir.dt.float32
BF16 = mybir.dt.bfloat16
I32 = mybir.dt.int32
AF = mybir.ActivationFunctionType
ALU = mybir.AluOpType
AX = mybir.AxisListType

NCORES = 8
D = 1024
SEQ = 2048
NSEQ = 2
NSMP = 4
NTOK = NSEQ * SEQ
DIN = 3080
KC = 8
EPS = 1e-6
COL_K, COL_V, COL_FL = 512, 1024, 2048


class Buf:
    def __init__(self, name, t, psum=False):
        self.name = name
        self.t = t
        self.psum = psum
        self.lw = None
        self.rd = {}
        self.wsem = None
        self.wcnt = 0
        self.rsem = None
        self.rcnt = 0

    def __getitem__(self, idx):
        return self.t[idx]


class K:
    def __init__(self, nc):
        self.nc = nc
        self.E = {"pe": nc.tensor, "act": nc.scalar, "dve": nc.vector,
                  "pool": nc.gpsimd, "sp": nc.sync}
        self.sem = {e: nc.alloc_semaphore(name="c_" + e) for e in self.E}
        self.cnt = {e: 0 for e in self.E}
        self.seen = {e: {} for e in self.E}
        self.bufs = []
        self.nins = {e: 0 for e in self.E}

    def wrap(self, name, t, psum=False):
        b = Buf(name, t, psum)
        self.bufs.append(b)
        return b

    def _wait(self, e, tok):
        nm, sem, val = tok
        if e == "pe" and nm == "c_pe":
            return
        if self.seen[e].get(nm, 0) >= val:
            return
        self.E[e].wait_ge(sem, val)
        self.seen[e][nm] = val

    def _deps(self, e, R, W):
        for b in R:
            if b.lw is not None:
                self._wait(e, b.lw)
            if b.psum:
                for tok in b.rd.values():
                    self._wait(e, tok)
        for b in W:
            if b.lw is not None:
                self._wait(e, b.lw)
            for tok in b.rd.values():
                self._wait(e, tok)

    def _mark(self, tok, R, W):
        for b in R:
            old = b.rd.get(tok[0])
            if old is None or old[2] < tok[2]:
                b.rd[tok[0]] = tok
        for b in W:
            b.lw = tok
            b.rd = {}

    def op(self, e, name, *a, R=(), W=(), inc=True, **kw):
        self._deps(e, R, W)
        ins = getattr(self.E[e], name)(*a, **kw)
        self.nins[e] += 1
        if inc:
            self.cnt[e] += 1
            ins.then_inc(self.sem[e], 1)
            tok = ("c_" + e, self.sem[e], self.cnt[e])
        else:
            tok = ("c_" + e, self.sem[e], self.cnt[e] + 1)
        self._mark(tok, R, W)
        return ins

    def pe(self, name, *a, **kw):
        return self.op("pe", name, *a, **kw)

    def act(self, name, *a, **kw):
        return self.op("act", name, *a, **kw)

    def dve(self, name, *a, **kw):
        return self.op("dve", name, *a, **kw)

    def pool(self, name, *a, **kw):
        return self.op("pool", name, *a, **kw)

    def dma(self, e, out, in_, R=(), W=(), **kw):
        self._deps(e, R, W)
        fi = kw.pop("fn_indirect", None)
        if fi is not None:
            ins = self.E[e].indirect_dma_start(out, None, in_, bass.IndirectOffsetOnAxis(ap=fi, axis=0))
        else:
            ins = self.E[e].dma_start(out=out, in_=in_, **kw)
        self.nins[e] += 1
        if W:
            b = W[0]
            if b.wsem is None:
                b.wsem = self.nc.alloc_semaphore(name="w_" + b.name)
            b.wcnt += 16
            ins.then_inc(b.wsem, 16)
            tok = ("w_" + b.name, b.wsem, b.wcnt)
        else:
            b = R[0]
            if b.rsem is None:
                b.rsem = self.nc.alloc_semaphore(name="r_" + b.name)
            b.rcnt += 16
            ins.then_inc(b.rsem, 16)
            tok = ("r_" + b.name, b.rsem, b.rcnt)
        self._mark(tok, R, W)
        return ins

    def barrier(self):
        toks = [("c_" + e, self.sem[e], self.cnt[e]) for e in self.E if self.cnt[e] > 0]
        for b in self.bufs:
            if b.wsem is not None and b.wcnt > 0:
                toks.append(("w_" + b.name, b.wsem, b.wcnt))
            if b.rsem is not None and b.rcnt > 0:
                toks.append(("r_" + b.name, b.rsem, b.rcnt))
        for e in self.E:
            for t in toks:
                self._wait(e, t)

    def finish(self, e="sp"):
        for b in self.bufs:
            if b.rsem is not None and b.rcnt > 0:
                self._wait(e, ("r_" + b.name, b.rsem, b.rcnt))
            if b.wsem is not None and b.wcnt > 0:
                self._wait(e, ("w_" + b.name, b.wsem, b.wcnt))


BLK = 512
NBLK = SEQ // BLK
COL_Q, COL_K, COL_V, COL_ZA, COL_FL, COL_U, COL_ZS = 0, 512, 1024, 1536, 2048, 2056, 2568
VA_W = 528
class _Stop(Exception):
    pass


def _ck(i):
    if DBG.get("stop") == i:
        raise _Stop()


DBG = {"stop": None, "nseq": NSEQ, "nblk": NBLK, "ktr": True, "att": True, "post": True, "cum": True, "cores": NCORES}


def build_program():
    nc = bass.Bass("TRN2", target_bir_lowering=False)

    def din(name, shape, dt=F32):
        return nc.dram_tensor(name, list(shape), dt, kind="ExternalInput").ap()

    def dout(name, shape, dt=F32):
        return nc.dram_tensor(name, list(shape), dt, kind="ExternalOutput").ap()

    x = din("x_d", [NTOK, D])
    xs = din("xs_d", [NSMP, D])
    cnd = din("cnd", [NSMP + NSEQ, D])
    g_norm = din("g_norm_d", [1, D])
    g_final = din("g_final_d", [1, D])
    w_ada = din("w_ada_d", [D, 3 * D])
    b_ada = din("b_ada_d", [1, 3 * D])
    w_in = din("w_in_d", [D, DIN])
    w_out = din("w_out_d", [D, D])
    w_glu = din("w_glu_d", [512, 512])
    b_glu = din("b_glu_d", [512])
    b_fgate = din("b_fgate_d", [1, 8])
    ident_d = din("ident_c", [128, 128])
    tri_d = din("tri_c", [128, 128])
    sel_d = din("sel_c", [6, 6 * 128])
    shift_d = din("shift_c", [64, 128])
    m9_d = din("m9_c", [128, 16 * 9])
    m64_d = din("m64_c", [128, 16 * 64])
    a_re_d = din("a_re_d", [32, 64])
    a_im_d = din("a_im_d", [32, 64])
    log_dt_d = din("log_dt_d", [32])
    b_re_d = din("b_re_d", [32, 64, 16])
    b_im_d = din("b_im_d", [32, 64, 16])
    c_re_d = din("c_re_d", [32, 16, 64])
    c_im_d = din("c_im_d", [32, 16, 64])
    d_skip_d = din("d_skip_d", [32, 16])
    if DBG.get("sample", True):
        NROW = 5120 * 128
        ck_d = din("ck_d", [NROW, 512])
        cv_d = din("cv_d", [NROW, 512])
        clf_d = din("clf_d", [NROW, 8])
        pt_d = din("pt_d", [1, NSMP * 128], I32)
        h0re_d = din("h0re_d", [NSMP, 32, 64])
        h0im_d = din("h0im_d", [NSMP, 32, 64])
        sut_d = din("sut_c", [128, 128])
        bd8_d = din("bd8_c", [8, 512])
        sel8_d = din("sel8_c", [8, NSMP * NSMP])

    yp = dout("yp", [NTOK, D])
    if DBG.get("dump"):
        d_apw = dout("d_apw", [128, 288]); d_sm = dout("d_sm", [128, 384])
        d_klag = dout("d_klag", [128, 4096], BF16); d_wx = dout("d_wx", [128, 8192], BF16)
        d_ms = dout("d_ms", [128, 9216], BF16); d_rc = dout("d_rc", [128, 1024]); d_rs = dout("d_rs", [128, 1024])
    hre_o = dout("hre_p", [NSEQ, 32, 64])
    ys_o = dout("ys", [NSMP, D])
    hres_o = dout("hre_s", [NSMP, 32, 64])
    hims_o = dout("him_s", [NSMP, 32, 64])
    him_o = dout("him_p", [NSEQ, 32, 64])
    kp = dout("kp", [NTOK, 512])
    vp = dout("vp", [NTOK, 512])
    lfp = dout("lfp", [NTOK, 8])
    ks = dout("ks", [NSMP, 512])
    vs = dout("vs", [NSMP, 512])
    lfs = dout("lfs", [NSMP, 8])

    with ExitStack() as es:
        k = K(nc)

        acct = {"bytes": 0}

        def sb(name, shape, dt):
            n = 1
            for d in shape[1:]:
                n *= d
            acct["bytes"] += n * (2 if dt == BF16 else 4)
            return k.wrap(name, es.enter_context(nc.sbuf_tensor(name, list(shape), dt)))

        def ps(name, shape, dt=F32):
            return k.wrap(name, es.enter_context(nc.psum_tensor(name, list(shape), dt)), psum=True)

        ident32 = sb("ident32", [128, 128], F32)
        ident = sb("ident", [128, 128], BF16)
        tri32 = sb("tri32", [128, 128], F32)
        trib = sb("trib", [128, 128], BF16)
        ones32 = sb("ones32", [128, 128], F32)
        sel = sb("sel", [6, 6 * 128], F32)
        shiftm = sb("shiftm", [64, 128], F32)
        bfg_b = sb("bfg_b", [128, 8], F32)
        bglu = sb("bglu", [128, 4], F32)
        mod = sb("mod", [6, 3 * D], F32)
        cT = sb("cT", [128, KC, 6], F32)
        cTs = sb("cTs", [128, KC, 6], F32)
        w_out_bf = sb("w_out_bf", [128, KC, D], BF16)
        w_glu_bf = sb("w_glu_bf", [128, 4, 512], BF16)
        gfin_b = sb("gfin_b", [128, D], F32)
        sc1_b = sb("sc1_b", [128, D], F32)
        shift_b = sb("shift_b", [128, D], F32)
        gate_b = sb("gate_b", [128, D], F32)
        wring = [sb("wr%d" % i, [128, KC, 128], BF16) for i in range(3)]
        P = [ps("P%d" % i, [128, 512]) for i in range(7)]
        PT = ps("PT", [128, KC, 128], BF16)

        xt = [sb("xt%d" % i, [128, D], F32) for i in range(2)]
        tmp = sb("tmp", [128, D], F32)
        yst = sb("yst", [128, D], F32)
        hb = sb("hb", [128, D], BF16)
        hT = sb("hT", [128, KC, BLK], BF16)
        qT = sb("qT", [128, 4, BLK], BF16)
        uT = sb("uT", [128, 4, BLK], BF16)
        gT = qT
        mixT = sb("mixT", [128, KC, BLK], BF16)
        kT = sb("kT", [128, 4, SEQ], BF16)
        vaug = sb("vaug", [128, 16 * VA_W], BF16)
        vaug4 = vaug[:].rearrange("p (t h c) -> p t h c", t=16, h=8, c=66)
        kTf = sb("kTf", [128, BLK], F32)
        o_sb = kTf
        kst = sb("kst", [128, 4, 128], F32)
        vst = sb("vst", [128, 4, 128], F32)
        pTb = [sb("pTb%d" % i, [128, BLK], BF16) for i in range(2)]
        sg = sb("sg", [128, BLK], BF16)
        attf = sb("attf", [128, BLK], F32)
        bc_sb = sb("bc_sb", [128, BLK], F32)
        rec = sb("rec", [128, BLK], F32)
        ss = sb("ss", [128, 1], F32)
        rstd = sb("rstd", [128, 1], F32)
        lfz = sb("lfz", [128, 4, 8], F32)
        lf = sb("lf", [128, 4, 8], F32)
        E5 = sb("E5", [128, 5, 8], F32)
        ncum = sb("ncum", [128, NBLK, 4, 8], F32)
        totb = sb("totb", [128, NBLK, 8], F32)
        sacc = sb("sacc", [128, 8], F32)
        biask = sb("biask", [128, NBLK, 4, 8], F32)
        scg = sb("scg", [128, 64], F32)

        Wx = sb("Wx", [128, 4, 8, 2, 128], BF16)
        Ms = sb("Ms", [128, 16, 9, 2, 32], BF16)
        Klag = sb("Klag", [128, 4, 8, 128], BF16)
        rc = sb("rc", [128, 16, 64], F32)
        rs = sb("rs", [128, 16, 64], F32)
        sm = sb("sm", [128, 24, 16], F32)
        apw = sb("apw", [128, 2, 16, 9], F32)
        dcol = sb("dcol", [128, 4], F32)
        hin = sb("hin", [128, 2, 16], F32)
        zb = sb("zb", [128, 128], BF16)
        ARE, AIM, LDT, DT, LR, LI, DEN, KRE, KIM, NRE, NIM, TA, TB, RR = range(14)
        k.dma("sp", ident32[:], ident_d, W=[ident32])
        k.dve("tensor_copy", ident[:], ident32[:], R=[ident32], W=[ident])
        k.dma("sp", tri32[:], tri_d, W=[tri32])
        k.dve("tensor_copy", trib[:], tri32[:], R=[tri32], W=[trib])
        k.dve("memset", ones32[:], 1.0, W=[ones32])
        k.dma("sp", sel[:], sel_d, W=[sel])
        k.dma("sp", shiftm[:], shift_d, W=[shiftm])
        k.dma("sp", gfin_b[:], g_final.broadcast_to([128, D]), W=[gfin_b])
        k.dma("sp", bfg_b[:], b_fgate.broadcast_to([128, 8]), W=[bfg_b])
        k.dma("sp", mod[:], b_ada.broadcast_to([6, 3 * D]), W=[mod])
        with nc.allow_non_contiguous_dma(reason="tiny transposed loads"):
            for n in range(6):
                k.dma("sp", cT[:, :, n], cnd[n, :].rearrange("(c p) -> p c", p=128), W=[cT])
            k.dma("sp", bglu[:], b_glu.rearrange("(c p) -> p c", p=128), W=[bglu])
        k.act("activation", cTs[:], cT[:], AF.Silu, R=[cT], W=[cTs])
        w_in_v = w_in.rearrange("(c p) n -> p c n", p=128)
        w_out_v = w_out.rearrange("(c p) n -> p c n", p=128)
        w_glu_v = w_glu.rearrange("(c p) n -> p c n", p=128)
        for c in range(KC):
            k.dma("pool", w_out_bf[:, c, :], w_out_v[:, c, :], W=[w_out_bf])
        for c in range(4):
            k.dma("pool", w_glu_bf[:, c, :], w_glu_v[:, c, :], W=[w_glu_bf])

        try:
            _ck(1)
            w_ada_v = w_ada.rearrange("(c p) n -> p c n", p=128)
            aring = [gate_b, sc1_b, shift_b]
            ai = 0
            for cc in range(6):
                p = P[cc % 2]
                for c2 in range(KC // 2):
                    wb = aring[ai % 3]
                    ai += 1
                    wv = wb[:, :].rearrange("p (c n) -> p c n", c=2)
                    k.dma("sp", wv, w_ada_v[:, 2 * c2:2 * c2 + 2, cc * 512:(cc + 1) * 512], W=[wb])
                    for cl in range(2):
                        c = 2 * c2 + cl
                        k.pe("matmul", p[0:6, :], cTs[:, c, :], wv[:, cl, :], start=(c == 0), stop=(c == KC - 1),
                             R=[cTs, wb], W=[p], inc=(cl == 1))
                k.dve("tensor_tensor", mod[:, cc * 512:(cc + 1) * 512], p[0:6, :], mod[:, cc * 512:(cc + 1) * 512],
                      op=ALU.add, R=[p, mod], W=[mod])

            PI = float(np.pi)
            MAGIC = 12582912.0
            C1, C2 = 6.28125, 2.0 * float(np.pi) - 6.28125

            def sincos(argB, argv, sB, sv, cB, cv, t1B, t1v, t2B, t2v):
                k.dve("tensor_scalar", t1v, argv, 1.0 / (2.0 * PI), None, op0=ALU.mult, R=[argB], W=[t1B])
                k.dve("tensor_scalar", t1v, t1v, MAGIC, None, op0=ALU.add, R=[t1B], W=[t1B])
                k.dve("tensor_scalar", t1v, t1v, -MAGIC, None, op0=ALU.add, R=[t1B], W=[t1B])
                k.dve("scalar_tensor_tensor", t2v, t1v, -C1, argv, op0=ALU.mult, op1=ALU.add, R=[t1B, argB], W=[t2B])
                k.dve("scalar_tensor_tensor", t2v, t1v, -C2, t2v, op0=ALU.mult, op1=ALU.add, R=[t1B, t2B], W=[t2B])
                k.dve("tensor_scalar", t2v, t2v, PI, -PI, op0=ALU.min, op1=ALU.max, R=[t2B], W=[t2B])
                k.act("activation", sv, t2v, AF.Sin, R=[t2B], W=[sB])
                k.dve("scalar_tensor_tensor", t1v, t2v, -1.0, t2v, op0=ALU.mult, op1=ALU.max, R=[t2B], W=[t1B])
                k.act("activation", cv, t1v, AF.Sin, scale=-1.0, bias=PI / 2.0, R=[t1B], W=[cB])

            def tt(out, a, b, op, R, W):
                k.dve("tensor_tensor", out, a, b, op=op, R=R, W=W)

            def S(i):
                return sm[:, i, :]

            def bc16(ap2):
                return ap2.unsqueeze(2).broadcast_to([128, 16, 16])

            l1 = lambda d: d.rearrange("(P gm) p -> (gm p) P", gm=2)
            with nc.allow_non_contiguous_dma(reason="small S5 parameter re-layouts"):
                k.dma("sp", S(ARE), l1(a_re_d), W=[sm])
                k.dma("sp", S(AIM), l1(a_im_d), W=[sm])
                ldv = log_dt_d.rearrange("(P gm) -> gm P", gm=2)
                for gm in range(2):
                    k.dma("sp", sm[gm * 64:(gm + 1) * 64, LDT, :], ldv[gm:gm + 1, :].broadcast_to([64, 16]), W=[sm])
                k.dma("sp", dcol[:], d_skip_d.rearrange("(t g) h -> (g h) t", t=4), W=[dcol])
                Bq = [tmp[:, 0:256].rearrange("p (a h) -> p a h", a=16), tmp[:, 256:512].rearrange("p (a h) -> p a h", a=16)]
                Cq = [tmp[:, 512:768].rearrange("p (a h) -> p a h", a=16), tmp[:, 768:1024].rearrange("p (a h) -> p a h", a=16)]
                l1b = lambda d: d.rearrange("(P gm) p h -> (gm p) P h", gm=2)
                k.dma("sp", Bq[0], l1b(b_re_d), W=[tmp])
                k.dma("sp", Bq[1], l1b(b_im_d), W=[tmp])
                for ri, cd in enumerate((c_re_d, c_im_d)):
                    cv4 = cd.rearrange("(P gm) h p -> gm P p h", gm=2)
                    for gm in range(2):
                        for Pp in range(16):
                            k.dma("sp", tmp[gm * 64:(gm + 1) * 64, 512 + ri * 256 + Pp * 16:512 + ri * 256 + (Pp + 1) * 16],
                                  cv4[gm, Pp, :, :], W=[tmp])
            m9 = yst[:, 0:144].rearrange("p (a m) -> p a m", a=16)
            k.dma("sp", yst[:, 0:144], m9_d, W=[yst])
            k.act("activation", S(DT), S(LDT), AF.Exp, R=[sm], W=[sm])
            tt(S(LR), S(ARE), S(DT), ALU.mult, [sm], [sm])
            tt(S(LI), S(AIM), S(DT), ALU.mult, [sm], [sm])
            argm = yst[:, 144:288].rearrange("p (a m) -> p a m", a=16)
            magm = yst[:, 288:432].rearrange("p (a m) -> p a m", a=16)
            t1m = yst[:, 432:576].rearrange("p (a m) -> p a m", a=16)
            t2m = yst[:, 576:720].rearrange("p (a m) -> p a m", a=16)
            snm = yst[:, 720:864].rearrange("p (a m) -> p a m", a=16)
            csm = yst[:, 864:1008].rearrange("p (a m) -> p a m", a=16)
            b9 = lambda ap2: ap2.unsqueeze(2).broadcast_to([128, 16, 9])
            tt(argm, m9, b9(S(LI)), ALU.mult, [yst, sm], [yst])
            tt(magm, m9, b9(S(LR)), ALU.mult, [yst, sm], [yst])
            k.act("activation", magm, magm, AF.Exp, R=[yst], W=[yst])
            sincos(yst, argm, yst, snm, yst, csm, yst, t1m, yst, t2m)
            tt(apw[:, 0, :, :], magm, csm, ALU.mult, [yst], [apw])
            tt(apw[:, 1, :, :], magm, snm, ALU.mult, [yst], [apw])
            k.dve("tensor_scalar", S(TA), apw[:, 0, :, 1], -1.0, None, op0=ALU.add, R=[apw], W=[sm])
            tt(S(NRE), S(TA), S(ARE), ALU.mult, [sm], [sm])
            tt(S(TB), apw[:, 1, :, 1], S(AIM), ALU.mult, [apw, sm], [sm])
            tt(S(NRE), S(NRE), S(TB), ALU.add, [sm], [sm])
            tt(S(NIM), apw[:, 1, :, 1], S(ARE), ALU.mult, [apw, sm], [sm])
            tt(S(TB), S(TA), S(AIM), ALU.mult, [sm], [sm])
            tt(S(NIM), S(NIM), S(TB), ALU.subtract, [sm], [sm])
            tt(S(DEN), S(ARE), S(ARE), ALU.mult, [sm], [sm])
            tt(S(TB), S(AIM), S(AIM), ALU.mult, [sm], [sm])
            tt(S(DEN), S(DEN), S(TB), ALU.add, [sm], [sm])
            k.dve("reciprocal", S(DEN), S(DEN), R=[sm], W=[sm])
            tt(S(KRE), S(NRE), S(DEN), ALU.mult, [sm], [sm])
            tt(S(KIM), S(NIM), S(DEN), ALU.mult, [sm], [sm])
            k.dve("tensor_copy", S(RR), magm[:, :, 8], R=[yst], W=[sm])
            Bb = [xt[0][:, 0:256].rearrange("p (a h) -> p a h", a=16), xt[0][:, 256:512].rearrange("p (a h) -> p a h", a=16)]
            U1 = xt[0][:, 512:768].rearrange("p (a h) -> p a h", a=16)
            U2 = xt[0][:, 768:1024].rearrange("p (a h) -> p a h", a=16)
            tt(U1, Bq[0], bc16(S(KRE)), ALU.mult, [tmp, sm], [xt[0]])
            tt(U2, Bq[1], bc16(S(KIM)), ALU.mult, [tmp, sm], [xt[0]])
            tt(Bb[0], U1, U2, ALU.subtract, [xt[0]], [xt[0]])
            tt(U1, Bq[1], bc16(S(KRE)), ALU.mult, [tmp, sm], [xt[0]])
            tt(U2, Bq[0], bc16(S(KIM)), ALU.mult, [tmp, sm], [xt[0]])
            tt(Bb[1], U1, U2, ALU.add, [xt[0]], [xt[0]])
            Xs = [vaug[:, 0:4096].rearrange("p (m a c) -> p m a c", a=16, m=8),
                  vaug[:, 4096:8192].rearrange("p (m a c) -> p m a c", a=16, m=8)]
            k.pool("memset", vaug[:, 0:8192], 0.0, W=[vaug])
            k.pool("memset", Ms[:], 0.0, W=[Ms])
            k.pool("memset", zb[:], 0.0, W=[zb])
            V1 = xt[1][:, 0:256].rearrange("p (a h) -> p a h", a=16)
            V2 = xt[1][:, 256:512].rearrange("p (a h) -> p a h", a=16)
            V3 = xt[1][:, 512:768].rearrange("p (a h) -> p a h", a=16)

            def cmul_strips(src, m, dst_re, dst_im, neg_im):
                ar, ai = bc16(apw[:, 0, :, m]), bc16(apw[:, 1, :, m])
                tt(V1, src[0], ar, ALU.mult, [tmp, xt[0], apw], [xt[1]])
                tt(V2, src[1], ai, ALU.mult, [tmp, xt[0], apw], [xt[1]])
                tt(V3, V1, V2, ALU.subtract, [xt[1]], [xt[1]])
                for gm in range(2):
                    k.act("copy", dst_re(gm), V3[gm * 64:(gm + 1) * 64, :, :], R=[xt[1]], W=[vaug, Ms])
                tt(V1, src[1], ar, ALU.mult, [tmp, xt[0], apw], [xt[1]])
                tt(V2, src[0], ai, ALU.mult, [tmp, xt[0], apw], [xt[1]])
                tt(V3, V1, V2, ALU.add, [xt[1]], [xt[1]])
                if neg_im:
                    k.dve("tensor_scalar", V3, V3, -1.0, None, op0=ALU.mult, R=[xt[1]], W=[xt[1]])
                for gm in range(2):
                    k.act("copy", dst_im(gm), V3[gm * 64:(gm + 1) * 64, :, :], R=[xt[1]], W=[vaug, Ms])

            for m in range(8):
                cmul_strips(Bb, m,
                            lambda gm, m=m: Xs[0][gm * 64:(gm + 1) * 64, m, :, gm * 16:(gm + 1) * 16],
                            lambda gm, m=m: Xs[1][gm * 64:(gm + 1) * 64, m, :, gm * 16:(gm + 1) * 16], False)
            for jj in range(9):
                cmul_strips(Cq, jj,
                            lambda gm, jj=jj: Ms[gm * 64:(gm + 1) * 64, :, jj, 0, gm * 16:(gm + 1) * 16],
                            lambda gm, jj=jj: Ms[gm * 64:(gm + 1) * 64, :, jj, 1, gm * 16:(gm + 1) * 16], True)
            for i in range(4):
                for s_ in range(8):
                    for ri in range(2):
                        k.pe("transpose", PT[:, 0, :], Xs[ri][:, 7 - s_, 4 * i:4 * i + 4, :].rearrange("p a c -> p (a c)"), ident[:, :],
                             R=[vaug, ident], W=[PT])
                        k.act("copy", Wx[:, i, s_, ri, :], PT[:, 0, :], R=[PT], W=[Wx])
            for i in range(4):
                for tau in range(8):
                    pk = P[(i * 8 + tau) % 2]
                    k.pe("matmul", pk[:, 0:128], zb[:, :], zb[:, :], start=True, stop=False, R=[zb], W=[pk], inc=False)
                    for kk in range(4):
                        for ri in range(2):
                            last = (kk == 3 and ri == 1)
                            k.pe("matmul", pk[32 * kk:32 * kk + 32, 32 * kk:32 * kk + 32], Xs[ri][:, tau, 4 * i + kk, :],
                                 Ms[:, 4 * i + kk, 0, ri, :], start=False, stop=last, R=[vaug, Ms], W=[pk], inc=last,
                                 tile_position=(0, 32 * kk), skip_group_check=True)
                    if tau == 0:
                        k.dve("scalar_tensor_tensor", Klag[:, i, tau, :], ident32[:, :], dcol[:, i:i + 1], pk[:, 0:128],
                              op0=ALU.mult, op1=ALU.add, R=[ident32, dcol, pk], W=[Klag])
                    else:
                        k.act("copy", Klag[:, i, tau, :], pk[:, 0:128], R=[pk], W=[Klag])
            k.dma("sp", tmp[:], m64_d, W=[tmp])
            k.dve("tensor_scalar", S(TA), S(LI), 8.0, None, op0=ALU.mult, R=[sm], W=[sm])
            v3 = lambda b: b[:].rearrange("p (a c) -> p a c", a=16)
            tt(v3(yst), v3(tmp), S(TA).unsqueeze(2).broadcast_to([128, 16, 64]), ALU.mult, [tmp, sm], [yst])
            sincos(yst, v3(yst), rs, rs[:], rc, rc[:], xt[0], v3(xt[0]), xt[1], v3(xt[1]))
            k.pool("memset", vaug[:], 1.0, W=[vaug])
            if DBG.get("dump"):
                k.dma("sp", d_apw, apw[:].rearrange("p a b c -> p (a b c)"), R=[apw])
                k.dma("sp", d_sm, sm[:].rearrange("p a b -> p (a b)"), R=[sm])
                k.dma("sp", d_klag, Klag[:].rearrange("p a b c -> p (a b c)"), R=[Klag])
                k.dma("sp", d_wx, Wx[:].rearrange("p a b c d -> p (a b c d)"), R=[Wx])
                k.dma("sp", d_ms, Ms[:].rearrange("p a b c d -> p (a b c d)"), R=[Ms])
                k.dma("sp", d_rc, rc[:].rearrange("p a b -> p (a b)"), R=[rc])
                k.dma("sp", d_rs, rs[:].rearrange("p a b -> p (a b)"), R=[rs])
            _ck(2)

            def bcast_mod(row):
                k.dma("sp", tmp[:], g_norm.broadcast_to([128, D]), W=[tmp])
                for part, dst in ((0, shift_b), (1, sc1_b), (2, gate_b)):
                    for h in range(2):
                        p = P[2 + h]
                        k.pe("matmul", p[:], sel[:, row * 128:(row + 1) * 128],
                             mod[:, part * D + h * 512:part * D + (h + 1) * 512],
                             start=True, stop=True, R=[sel, mod], W=[p])
                        if part == 1:
                            k.dve("scalar_tensor_tensor", dst[:, h * 512:(h + 1) * 512], p[:], 1.0,
                                  tmp[:, h * 512:(h + 1) * 512], op0=ALU.add, op1=ALU.mult, R=[p, tmp], W=[dst])
                        else:
                            k.act("copy", dst[:, h * 512:(h + 1) * 512], p[:], R=[p], W=[dst])

            st = {"piece": 0, "xt": 0}

            def load_piece(col0, ncols):
                b = wring[st["piece"] % 3]
                st["piece"] += 1
                k.dma("pool", b[:, :, 0:ncols], w_in_v[:, :, col0:col0 + ncols], W=[b])
                return b

            def rms_rstd(src, Pn):
                k.act("activation", yst[0:Pn, :], src, AF.Square, accum_out=ss[0:Pn, :], R=[tmp, xt[0], xt[1]], W=[yst, ss])
                k.dve("tensor_scalar", rstd[0:Pn, :], ss[0:Pn, :], 1.0 / D, EPS, op0=ALU.mult, op1=ALU.add,
                      R=[ss], W=[rstd])
                k.act("activation", rstd[0:Pn, :], rstd[0:Pn, :], AF.Sqrt, R=[rstd], W=[rstd])
                k.dve("reciprocal", rstd[0:Pn, :], rstd[0:Pn, :], R=[rstd], W=[rstd])

            def pre_tile(Pn, x_src, sc1, shf, col):
                xb = xt[st["xt"] % 2]
                st["xt"] += 1
                k.dma("sp", xb[0:Pn, :], x_src, W=[xb])
                rms_rstd(xb[0:Pn, :], Pn)
                k.dve("scalar_tensor_tensor", tmp[0:Pn, :], xb[0:Pn, :], rstd[0:Pn, 0:1], sc1,
                      op0=ALU.mult, op1=ALU.mult, R=[xb, rstd, sc1_b, mod], W=[tmp])
                k.dve("tensor_tensor", hb[0:Pn, :], tmp[0:Pn, :], shf, op=ALU.add, R=[tmp, shift_b, mod], W=[hb])
                for c in range(KC):
                    k.pe("transpose", PT[:, c, 0:Pn], hb[0:Pn, c * 128:(c + 1) * 128], ident[0:Pn, 0:Pn],
                         R=[hb, ident], W=[PT], inc=(c == KC - 1))
                k.act("copy", hT[:, :, col:col + Pn], PT[:, :, 0:Pn], R=[PT], W=[hT])

            def logf_from(psrc, Pn, nj, dst):
                k.dve("tensor_tensor", lfz[0:Pn, 0:nj, :], psrc, bfg_b[0:Pn, :].unsqueeze(1).broadcast_to([Pn, nj, 8]),
                      op=ALU.add, R=[P[3], bfg_b], W=[lfz])
                k.act("activation", lfz[0:Pn, 0:nj, :], lfz[0:Pn, 0:nj, :], AF.Exp, scale=-1.0, R=[lfz], W=[lfz])
                k.act("activation", lfz[0:Pn, 0:nj, :], lfz[0:Pn, 0:nj, :], AF.Ln, bias=1.0, R=[lfz], W=[lfz])
                k.dve("tensor_scalar", dst, lfz[0:Pn, 0:nj, :], -1.0, None, op0=ALU.mult, R=[lfz], W=[lf])

            def fm_piece(col0, evac):
                wb = load_piece(col0, 128)
                p = P[st["piece"] % 2]
                for c in range(KC):
                    k.pe("matmul", p[:, 0:BLK], wb[:, c, 0:128], hT[:, c, 0:BLK], start=(c == 0), stop=(c == KC - 1),
                         R=[wb, hT], W=[p], inc=(c == KC - 1))
                evac(p)

            for n in range(DBG["nseq"]):
                bcast_mod(NSMP + n)
                _ck(3)
                for B in range(DBG["nblk"]):
                    t0 = B * BLK
                    r0 = n * SEQ + t0
                    for j in range(4):
                        pre_tile(128, x[r0 + j * 128:r0 + (j + 1) * 128, :], sc1_b[:], shift_b[:], j * 128)
                    _ck(4)
                    for i in range(4):
                        fm_piece(COL_Q + i * 128,
                                 lambda p, i=i: k.act("copy", qT[:, i, :], p[:, 0:BLK], R=[p], W=[qT]))
                    _ck(5)
                    for i in range(4):
                        def ev_k(p, i=i):
                            k.act("copy", kT[:, i, t0:t0 + BLK], p[:, 0:BLK], R=[p], W=[kT])
                            if not DBG["ktr"]:
                                return
                            k.dve("tensor_copy", kTf[:], p[:, 0:BLK], R=[p], W=[kTf])
                            for j in range(4):
                                k.pe("transpose", P[4][:, j * 128:(j + 1) * 128], kTf[:, j * 128:(j + 1) * 128],
                                     ident32[:], R=[kTf, ident32], W=[P[4]], inc=(j == 3))
                            k.dve("tensor_copy", kst[:, :, :],
                                  P[4][:, 0:512].rearrange("p (j c) -> p j c", j=4), R=[P[4]], W=[kst])
                            for j in range(4):
                                k.dma("sp", kp[r0 + j * 128:r0 + (j + 1) * 128, i * 128:(i + 1) * 128], kst[:, j, :], R=[kst])
                        fm_piece(COL_K + i * 128, ev_k)
                    _ck(6)
                    for i in range(4):
                        wb = load_piece(COL_V + i * 128, 128)
                        pv = P[5]
                        for j in range(4):
                            for c in range(KC):
                                k.pe("matmul", pv[:, j * 128:(j + 1) * 128], hT[:, c, j * 128:(j + 1) * 128], wb[:, c, 0:128],
                                     start=(c == 0), stop=(c == KC - 1), R=[hT, wb], W=[pv],
                                     inc=(c == KC - 1 and j == 3))
                        k.dve("tensor_copy", vst[:, :, :],
                              pv[:, 0:512].rearrange("p (j c) -> p j c", j=4), R=[pv], W=[vst])
                        for j in range(4):
                            k.dma("sp", vp[r0 + j * 128:r0 + (j + 1) * 128, i * 128:(i + 1) * 128], vst[:, j, :], R=[vst])
                        for j in range(4):
                            T = B * 4 + j
                            if DBG.get("novaug"):
                                continue
                            for a in range(2):
                                o = T * VA_W + (2 * i + a) * 66
                                k.act("copy", vaug[:, o:o + 64], pv[:, j * 128 + a * 64:j * 128 + (a + 1) * 64],
                                      R=[pv], W=[vaug])
                    _ck(7)
                    for i in range(4):
                        fm_piece(COL_ZA + i * 128,
                                 lambda p, i=i: k.act("activation", mixT[:, i, :], p[:, 0:BLK], AF.Silu, R=[p], W=[mixT]))
                    _ck(8)
                    wb = load_piece(COL_FL, 8)
                    pf = P[3]
                    for j in range(4):
                        for c in range(KC):
                            k.pe("matmul", pf[:, j * 8:(j + 1) * 8], hT[:, c, j * 128:(j + 1) * 128], wb[:, c, 0:8],
                                 start=(c == 0), stop=(c == KC - 1), R=[hT, wb], W=[pf], inc=(c == KC - 1 and j == 3))
                    logf_from(pf[:, 0:32].rearrange("p (j h) -> p j h", j=4), 128, 4, lf[:])
                    for j in range(4):
                        k.dma("sp", lfp[r0 + j * 128:r0 + (j + 1) * 128, :], lf[:, j, :], R=[lf])
                    if DBG["cum"]:
                      k.dve("memset", E5[:, 0, :], 0.0, W=[E5])
                      for j in range(4):
                          k.dve("tensor_tensor", E5[:, j + 1, :], E5[:, j, :], lf[:, j, :], op=ALU.add, R=[E5, lf], W=[E5])
                      pc = P[6]
                      k.pe("matmul", pc[:, 0:32], tri32[:], lf[:].rearrange("p j h -> p (j h)"), start=True, stop=False,
                           R=[tri32, lf], W=[pc], inc=False)
                      k.pe("matmul", pc[:, 0:32], ones32[:], E5[:, 0:4, :].rearrange("p j h -> p (j h)"), start=False, stop=True,
                           R=[ones32, E5], W=[pc], inc=False)
                      k.pe("matmul", pc[:, 32:40], ones32[:], E5[:, 4, :], start=True, stop=True, R=[ones32, E5], W=[pc])
                      k.dve("tensor_scalar", ncum[:, B, :, :], pc[:, 0:32].rearrange("p (j h) -> p j h", j=4), -1.0, None,
                            op0=ALU.mult, R=[pc], W=[ncum])
                      k.dve("tensor_copy", totb[:, B, :], pc[:, 32:40], R=[pc], W=[totb])
                      k.dve("memset", sacc[:], 0.0, W=[sacc])
                      for Bp in range(B, -1, -1):
                          k.dve("tensor_tensor", biask[:, Bp, :, :], ncum[:, Bp, :, :],
                                sacc[:].unsqueeze(1).broadcast_to([128, 4, 8]), op=ALU.add, R=[ncum, sacc], W=[biask])
                          if Bp > 0:
                              k.dve("tensor_tensor", sacc[:], sacc[:], totb[:, Bp - 1, :], op=ALU.add, R=[sacc, totb], W=[sacc])
                    _ck(9)
                    for i in range(4):
                        fm_piece(COL_U + i * 128,
                                 lambda p, i=i: k.act("copy", uT[:, i, :], p[:, 0:BLK], R=[p], W=[uT]))
                    for i in range(4):
                        fm_piece(COL_ZS + i * 128,
                                 lambda p, i=i: k.act("activation", mixT[:, 4 + i, :], p[:, 0:BLK], AF.Silu, R=[p], W=[mixT]))

                    nT = 4 * (B + 1)
                    for h in (range(8) if DBG["att"] else []):
                        i, odd = h // 2, h % 2
                        pb = 64 * odd
                        po = P[2 + odd]

                        def qk(T):
                            jd = T - 4 * B
                            c0 = max(jd, 0) * 128
                            pS = P[T % 2]
                            k.pe("matmul", pS[:, c0:BLK], kT[pb:pb + 64, i, T * 128:(T + 1) * 128], qT[pb:pb + 64, i, c0:BLK],
                                 start=True, stop=True, R=[kT, qT], W=[pS], tile_position=(pb, 0))
                            return c0
                        c0n = qk(0)
                        for T in range(nT):
                            c0 = c0n
                            pS = P[T % 2]
                            pt = pTb[T % 2]
                            if T + 1 < nT:
                                c0n = qk(T + 1)
                            Bp, j = T // 4, T % 4
                            k.act("activation", pt[:, c0:BLK], pS[:, c0:BLK], AF.Exp, scale=0.125,
                                  bias=biask[:, Bp, j, h:h + 1], R=[pS, biask], W=[pt])
                            if Bp == B:
                                k.dve("tensor_tensor", pt[:, c0:c0 + 128], pt[:, c0:c0 + 128], trib[:], op=ALU.mult,
                                      R=[pt, trib], W=[pt])
                            o = T * VA_W + h * 66
                            k.pe("matmul", po[0:65, c0:BLK], vaug[:, o:o + 65], pt[:, c0:BLK],
                                 start=(T == 0), stop=(T == nT - 1), R=[vaug, pt], W=[po], inc=(T == nT - 1))
                        k.dve("reciprocal", rec[64:65, :], po[64:65, 0:BLK], R=[po], W=[rec])
                        k.pe("matmul", P[4][pb:pb + 64, 0:BLK], ones32[64:65, 0:64], rec[64:65, :],
                             start=True, stop=True, R=[ones32, rec], W=[P[4]], tile_position=(64, pb))
                        k.act("copy", bc_sb[pb:pb + 64, :], P[4][pb:pb + 64, 0:BLK], R=[P[4]], W=[bc_sb])
                        if not odd:
                            osrc = po
                        else:
                            k.act("copy", o_sb[0:64, :], po[0:64, 0:BLK], R=[po], W=[o_sb])
                            k.pe("matmul", P[6][:, 0:BLK], shiftm[:, :], o_sb[0:64, :], start=True, stop=True,
                                 R=[shiftm, o_sb], W=[P[6]])
                            osrc = P[6]
                        k.dve("tensor_tensor", attf[pb:pb + 64, :], osrc[pb:pb + 64, 0:BLK], bc_sb[pb:pb + 64, :], op=ALU.mult,
                              R=[osrc, bc_sb], W=[attf])
                        k.dve("tensor_tensor", mixT[pb:pb + 64, i, :], attf[pb:pb + 64, :], mixT[pb:pb + 64, i, :], op=ALU.mult,
                              R=[attf, mixT], W=[mixT])

                    if DBG.get("s5off"):
                        k.dve("memset", mixT[:, 4:8, :], 0.0, W=[mixT])
                    else:
                        if B == 0:
                            k.dve("memset", hin[:], 0.0, W=[hin])
                        NC_ = BLK // 8
                        S2, Gin, TT_, Gout, Hp = attf, bc_sb, rec, kTf, sg
                        v4 = lambda b: b[:, 0:BLK].rearrange("p (r a c) -> p r a c", r=2, a=4)
                        for i in range(4):
                            tb = lambda t: t[:, 4 * i:4 * i + 4, :]
                            Sv, Gi, Tv, Go = v4(S2), v4(Gin), v4(TT_), v4(Gout)
                            for kk in range(4):
                                PS = P[2 + kk]
                                for ri in range(2):
                                    for s_ in range(8):
                                        last = (s_ == 7 and ri == 1)
                                        k.pe("matmul", PS[:, ri * NC_:(ri + 1) * NC_],
                                             Wx[32 * kk:32 * kk + 32, i, s_, ri, :], uT[32 * kk:32 * kk + 32, i, s_:BLK:8],
                                             start=(s_ == 0), stop=(s_ == 7), R=[Wx, uT], W=[PS], inc=last,
                                             tile_position=(32 * kk, 0))
                                k.act("copy", Sv[:, :, kk, :], PS[:, 0:2 * NC_].rearrange("p (r c) -> p r c", r=2), R=[PS], W=[S2])
                            tt(Tv[:, 0], Sv[:, 0], tb(rc), ALU.mult, [S2, rc], [TT_])
                            tt(Tv[:, 1], Sv[:, 1], tb(rs), ALU.mult, [S2, rs], [TT_])
                            tt(Gi[:, 0], Tv[:, 0], Tv[:, 1], ALU.add, [TT_], [Gin])
                            tt(Tv[:, 0], Sv[:, 1], tb(rc), ALU.mult, [S2, rc], [TT_])
                            tt(Tv[:, 1], Sv[:, 0], tb(rs), ALU.mult, [S2, rs], [TT_])
                            tt(Gi[:, 1], Tv[:, 0], Tv[:, 1], ALU.subtract, [TT_], [Gin])
                            for ri in range(2):
                                for kk in range(4):
                                    Pp = 4 * i + kk
                                    k.dve("tensor_tensor_scan", Go[:, ri, kk, :], sm[:, RR, Pp:Pp + 1].broadcast_to([128, NC_]),
                                          Gi[:, ri, kk, :], hin[:, ri, Pp:Pp + 1], op0=ALU.mult, op1=ALU.add,
                                          R=[sm, Gin, hin], W=[Gout])
                            tt(Tv[:, 0], Go[:, 0], tb(rc), ALU.mult, [Gout, rc], [TT_])
                            tt(Tv[:, 1], Go[:, 1], tb(rs), ALU.mult, [Gout, rs], [TT_])
                            tt(Sv[:, 0], Tv[:, 0], Tv[:, 1], ALU.subtract, [TT_], [S2])
                            tt(Tv[:, 0], Go[:, 1], tb(rc), ALU.mult, [Gout, rc], [TT_])
                            tt(Tv[:, 1], Go[:, 0], tb(rs), ALU.mult, [Gout, rs], [TT_])
                            tt(Sv[:, 1], Tv[:, 0], Tv[:, 1], ALU.add, [TT_], [S2])
                            Hv = Hp[:, 0:BLK].rearrange("p (r a c) -> p r a c", r=2, a=4)
                            k.act("copy", Hv[:, :, :, 0], hin[:, :, 4 * i:4 * i + 4], R=[hin], W=[Hp])
                            k.act("copy", Hv[:, :, :, 1:NC_], Sv[:, :, :, 0:NC_ - 1], R=[S2], W=[Hp])
                            k.dve("tensor_copy", hin[:, :, 4 * i:4 * i + 4], Sv[:, :, :, NC_ - 1], R=[S2], W=[hin])
                            for j in range(8):
                                PY = P[j % 2]
                                for s_ in range(j + 1):
                                    k.pe("matmul", PY[:, 0:NC_], Klag[:, i, j - s_, :], uT[:, i, s_:BLK:8],
                                         start=(s_ == 0), stop=False, R=[Klag, uT], W=[PY], inc=False, skip_group_check=True)
                                for kk in range(4):
                                    for ri in range(2):
                                        last = (kk == 3 and ri == 1)
                                        k.pe("matmul", PY[32 * kk:32 * kk + 32, 0:NC_], Ms[:, 4 * i + kk, j + 1, ri, :],
                                             Hv[:, ri, kk, :], start=False, stop=last, R=[Ms, Hp], W=[PY], inc=last,
                                             tile_position=(0, 32 * kk), skip_group_check=True)
                                if j % 2 == 0:
                                    k.act("copy", Gin[:, j:BLK:8], PY[:, 0:NC_], R=[PY], W=[Gin])
                                else:
                                    k.dve("tensor_copy", Gin[:, j:BLK:8], PY[:, 0:NC_], R=[PY], W=[Gin])
                            Y, Z = Gin[:, 0:BLK], TT_[:, 0:BLK]
                            tt(Z, Y, Y, ALU.mult, [Gin], [TT_])
                            k.dve("tensor_scalar", Z, Z, 0.044715, 1.0, op0=ALU.mult, op1=ALU.add, R=[TT_], W=[TT_])
                            tt(Z, Z, Y, ALU.mult, [TT_, Gin], [TT_])
                            k.act("activation", Z, Z, AF.Sigmoid, scale=1.5957691216057308, R=[TT_], W=[TT_])
                            tt(gT[:, i, :], Y, Z, ALU.mult, [Gin, TT_], [gT])
                        for ct in range(4):
                            pg = P[ct % 2]
                            for c in range(4):
                                k.pe("matmul", pg[:, 0:BLK], w_glu_bf[:, c, ct * 128:(ct + 1) * 128], gT[:, c, :],
                                     start=(c == 0), stop=(c == 3), R=[w_glu_bf, gT], W=[pg], inc=(c == 3))
                            k.act("activation", sg[:, :], pg[:, 0:BLK], AF.Sigmoid, bias=bglu[:, ct:ct + 1], R=[pg, bglu], W=[sg])
                            tt(sg[:, :], sg[:, :], gT[:, ct, :], ALU.mult, [sg, gT], [sg])
                            tt(mixT[:, 4 + ct, :], sg[:, :], mixT[:, 4 + ct, :], ALU.mult, [sg, mixT], [mixT])
                        if B == NBLK - 1:
                            with nc.allow_non_contiguous_dma(reason="final S5 state, small"):
                                l1o = lambda d: d.rearrange("(P gm) p -> (gm p) P", gm=2)
                                k.dma("sp", l1o(hre_o[n]), hin[:, 0, :], R=[hin])
                                k.dma("sp", l1o(him_o[n]), hin[:, 1, :], R=[hin])
                    for j in (range(4) if DBG["post"] else []):
                        xb = xt[st["xt"] % 2]
                        st["xt"] += 1
                        k.dma("sp", xb[:], x[r0 + j * 128:r0 + (j + 1) * 128, :], W=[xb])
                        for hf in range(2):
                            p = P[5 + hf]
                            for c in range(KC):
                                k.pe("matmul", p[:, :], mixT[:, c, j * 128:(j + 1) * 128], w_out_bf[:, c, hf * 512:(hf + 1) * 512],
                                     start=(c == 0), stop=(c == KC - 1), R=[mixT, w_out_bf], W=[p], inc=(c == KC - 1))
                            k.dve("tensor_tensor", tmp[:, hf * 512:(hf + 1) * 512], p[:, :], gate_b[:, hf * 512:(hf + 1) * 512],
                                  op=ALU.mult, R=[p, gate_b], W=[tmp])
                        k.dve("tensor_tensor", tmp[:], tmp[:], xb[:], op=ALU.add, R=[tmp, xb], W=[tmp])
                        rms_rstd(tmp[:], 128)
                        k.dve("scalar_tensor_tensor", yst[:], tmp[:], rstd[:, 0:1], gfin_b[:], op0=ALU.mult, op1=ALU.mult,
                              R=[tmp, rstd, gfin_b], W=[yst])
                        k.dma("sp", yp[r0 + j * 128:r0 + (j + 1) * 128, :], yst[:], R=[yst])


            if DBG.get("sample", True):
                k.barrier()
                NP_ = 128
                GP = 8
                kring = [k.wrap("kring%d" % i_, kT[:, 2 * i_:2 * i_ + 2, :].rearrange("p a (b c) -> p (a b) c", c=512)) for i_ in range(2)]
                vring = [k.wrap("vring%d" % i_, vaug[:, 4096 * i_:4096 * (i_ + 1)].rearrange("p (b c) -> p b c", c=512)) for i_ in range(2)]
                LFb, Fb, rows, prod = tmp, yst, xt[0], xt[1]
                LF = LFb[:].rearrange("p (j h) -> p j h", h=8)
                Fv = Fb[:].rearrange("p (j h) -> p j h", h=8)
                idx = sb("idx", [128, 128], I32)
                sel8 = sb("sel8", [8, NSMP * NSMP], F32)
                sml = sb("sml", [128, 64], F32)
                onesb = sb("onesb", [128, 8], BF16)
                qb = pTb[0]
                pbf = pTb[1]
                k.dma("sp", sel8[:], sel8_d, W=[sel8])
                k.dve("memset", onesb[:], 1.0, W=[onesb])
                RG = 8
                NG = 128 // RG
                sutB, bd8B = bc_sb, rec
                k.dma("sp", sutB[:, 0:128], sut_d, W=[sutB])
                k.dma("sp", bd8B[0:8, :], bd8_d, W=[bd8B])
                idxL = idx[:, 0:NSMP]
                idxK = idx[:, 64:64 + NSMP * NG].rearrange("p (s g) -> p s g", s=NSMP)
                with nc.allow_non_contiguous_dma(reason="page table, pages on partitions"):
                    k.dma("sp", idxL, pt_d.rearrange("o (s j) -> j (o s)", s=NSMP), W=[idx])
                k.dve("tensor_copy", sml[:, 0:NSMP], idxL, R=[idx], W=[sml])
                for g_ in range(NG):
                    k.dve("tensor_scalar", sml[:, 8:8 + NSMP], sml[:, 0:NSMP], float(NG), float(g_), op0=ALU.mult, op1=ALU.add,
                          R=[sml], W=[sml])
                    k.dve("tensor_copy", idxK[:, :, g_], sml[:, 8:8 + NSMP], R=[sml], W=[idx])
                ck_v = ck_d.rearrange("(n r) c -> n (r c)", r=RG)
                cv_v = cv_d.rearrange("(n r) c -> n (r c)", r=RG)
                clf_v = clf_d.rearrange("(n r) c -> n (r c)", r=128)

                R4 = slice(0, NSMP)
                k.dma("sp", prod[R4, :], g_norm.broadcast_to([NSMP, D]), W=[prod])
                k.dve("scalar_tensor_tensor", mod[R4, D:2 * D], mod[R4, D:2 * D], 1.0, prod[R4, :],
                      op0=ALU.add, op1=ALU.mult, R=[mod, prod], W=[mod])
                pre_tile(NSMP, xs, mod[R4, D:2 * D], mod[R4, 0:D], 0)
                vrow = attf
                for ci, (col0, dst, dB) in enumerate([(COL_Q + i_ * 128, rows[R4, i_ * 128:(i_ + 1) * 128], rows) for i_ in range(8)]
                                                     + [(COL_V + i_ * 128, vrow[R4, i_ * 128:(i_ + 1) * 128], vrow) for i_ in range(4)]):
                    wb = load_piece(col0, 128)
                    p = P[5 + ci % 2]
                    for c in range(KC):
                        k.pe("matmul", p[R4, 0:128], hT[:, c, 0:NSMP], wb[:, c, 0:128], start=(c == 0), stop=(c == KC - 1),
                             R=[hT, wb], W=[p], inc=(c == KC - 1))
                    k.act("copy", dst, p[R4, 0:128], R=[p], W=[dB])
                wb = load_piece(COL_FL, 8)
                for c in range(KC):
                    k.pe("matmul", P[3][R4, 0:8], hT[:, c, 0:NSMP], wb[:, c, 0:8], start=(c == 0), stop=(c == KC - 1),
                         R=[hT, wb], W=[P[3]], inc=(c == KC - 1))
                logf_from(P[3][R4, 0:8].rearrange("p (j h) -> p j h", j=1), NSMP, 1, lf[R4, 0:1, :])
                k.dma("sp", ks, rows[R4, 512:1024], R=[rows])
                k.dma("sp", vs, vrow[R4, :], R=[vrow])
                k.dma("sp", lfs, lf[R4, 0, :], R=[lf])
                fms = sb("fms", [128, 3, 4, NSMP], BF16)
                for gi, (colb, fn) in enumerate(((COL_ZA, AF.Silu), (COL_U, AF.Copy), (COL_ZS, AF.Silu))):
                    for i_ in range(4):
                        wb = load_piece(colb + i_ * 128, 128)
                        p = P[5 + i_ % 2]
                        for c in range(KC):
                            k.pe("matmul", p[:, 0:NSMP], wb[:, c, 0:128], hT[:, c, 0:NSMP], start=(c == 0), stop=(c == KC - 1),
                                 R=[wb, hT], W=[p], inc=(c == KC - 1))
                        if fn == AF.Copy:
                            k.act("copy", fms[:, gi, i_, :], p[:, 0:NSMP], R=[p], W=[fms])
                        else:
                            k.act("activation", fms[:, gi, i_, :], p[:, 0:NSMP], fn, R=[p], W=[fms])
                pself = sml[R4, 8:16]
                k.dve("tensor_tensor", prod[R4, 0:512], rows[R4, 0:512], rows[R4, 512:1024], op=ALU.mult, R=[rows], W=[prod])
                k.dve("tensor_reduce", sml[R4, 8:16], prod[R4, 0:512].rearrange("p (h d) -> p h d", h=8), axis=AX.X, op=ALU.add,
                      R=[prod], W=[sml])
                k.act("activation", sml[R4, 8:16], sml[R4, 8:16], AF.Exp, scale=0.125, R=[sml], W=[sml])
                k.act("activation", rows[R4, 0:512], rows[R4, 0:512], AF.Copy, scale=0.125, R=[rows], W=[rows])

                PO, PD, PA = P[2], P[3], P[4]
                for s_ in range(NSMP):
                    j0 = s_ * NP_
                    k.dma("pool", LFb[:, :], clf_v, R=[idx], W=[LFb], fn_indirect=idx[:, s_:s_ + 1])
                    for hh in range(8):
                        k.dve("tensor_tensor_scan", Fv[:, :, hh], ones32[:, 0:NP_], LF[:, :, hh], 0.0,
                              op0=ALU.mult, op1=ALU.add, R=[ones32, LFb], W=[Fb])
                    k.dve("tensor_copy", sml[:, 16:24], Fv[:, NP_ - 1, :], R=[Fb], W=[sml])
                    k.dve("scalar_tensor_tensor", Fv[:, :, :], Fv[:, :, :], -1.0,
                          sml[:, 16:24].unsqueeze(1).broadcast_to([128, NP_, 8]), op0=ALU.mult, op1=ALU.add,
                          R=[Fb, sml], W=[Fb])
                    k.pe("matmul", P[0][:, 0:8], sutB[:, 0:128], sml[:, 16:24], start=True, stop=False,
                         R=[sutB, sml], W=[P[0]], inc=False)
                    k.pe("matmul", P[0][:, 0:8], sel[0:NSMP, s_ * 128:(s_ + 1) * 128], lf[R4, 0, :], start=False, stop=True,
                         R=[sel, lf], W=[P[0]])
                    k.dve("tensor_tensor", Fv[:, :, :], Fv[:, :, :], P[0][:, 0:8].unsqueeze(1).broadcast_to([128, NP_, 8]),
                          op=ALU.add, R=[Fb, P[0]], W=[Fb])
                    k.pe("matmul", PA[:, :], sel[0:NSMP, s_ * 128:(s_ + 1) * 128], rows[R4, 0:512], start=True, stop=True,
                         R=[sel, rows], W=[PA])
                    k.act("copy", qb[:, :], PA[:, :], R=[PA], W=[qb])
                    for g_ in range(NP_ // GP):
                        kr, vr = kring[g_ % 2], vring[g_ % 2]
                        k.dma("pool", kr[:, :, :].rearrange("p a c -> p (a c)"), ck_v, R=[idx], W=[kr],
                              fn_indirect=idxK[:, s_, g_:g_ + 1])
                        k.dma("pool", vr[:, :, :].rearrange("p a c -> p (a c)"), cv_v, R=[idx], W=[vr],
                              fn_indirect=idxK[:, s_, g_:g_ + 1])
                        for jp in range(0, GP, 2):
                            k.dve("tensor_tensor", prod[:, :].rearrange("p (a c) -> p a c", a=2), kr[:, jp:jp + 2, :],
                                  qb[:, :].unsqueeze(1).broadcast_to([128, 2, 512]), op=ALU.mult, R=[kr, qb], W=[prod])
                            k.dve("tensor_reduce", scg[:, jp * 8:(jp + 2) * 8], prod[:, :].rearrange("p (a d) -> p a d", d=64),
                                  axis=AX.X, op=ALU.add, R=[prod], W=[scg])
                        k.dve("tensor_tensor", scg[:, :], scg[:, :], Fb[:, g_ * GP * 8:(g_ + 1) * GP * 8], op=ALU.add,
                              R=[scg, Fb], W=[scg])
                        k.act("activation", pbf[:, 0:GP * 8], scg[:, :], AF.Exp, R=[scg], W=[pbf])
                        for jj in range(GP):
                            j = g_ * GP + jj
                            k.pe("matmul", PO[0:8, :], pbf[:, jj * 8:(jj + 1) * 8], vr[:, jj, :], start=(j == 0), stop=(j == NP_ - 1),
                                 R=[pbf, vr], W=[PO], inc=False)
                            k.pe("matmul", PD[0:8, 0:1], pbf[:, jj * 8:(jj + 1) * 8], onesb[:, 0:1], start=(j == 0), stop=(j == NP_ - 1),
                                 R=[pbf, onesb], W=[PD], inc=(jj == GP - 1))
                    k.dve("tensor_tensor", kTf[0:8, :], PO[0:8, :], bd8B[0:8, :], op=ALU.mult, R=[PO, bd8B], W=[kTf])
                    k.dve("tensor_scalar", sml[0:8, 32:40], ident32[0:8, 0:8], PD[0:8, 0:1], None, op0=ALU.mult, R=[ident32, PD], W=[sml])
                    k.pe("matmul", P[5][R4, :], sel8[:, s_ * NSMP:(s_ + 1) * NSMP], kTf[0:8, :], start=(s_ == 0), stop=(s_ == NSMP - 1),
                         R=[sel8, kTf], W=[P[5]], inc=False)
                    k.pe("matmul", P[6][R4, 0:8], sel8[:, s_ * NSMP:(s_ + 1) * NSMP], sml[0:8, 32:40], start=(s_ == 0), stop=(s_ == NSMP - 1),
                         R=[sel8, sml], W=[P[6]])
                orow = kTf
                k.dve("tensor_tensor", prod[R4, 0:512].rearrange("p (h d) -> p h d", h=8), vrow[R4, :].rearrange("p (h d) -> p h d", h=8),
                      sml[R4, 8:16].unsqueeze(2).broadcast_to([NSMP, 8, 64]), op=ALU.mult, R=[vrow, sml], W=[prod])
                k.dve("tensor_tensor", prod[R4, 0:512], prod[R4, 0:512], P[5][R4, :], op=ALU.add, R=[prod, P[5]], W=[prod])
                k.dve("tensor_tensor", sml[R4, 40:48], sml[R4, 8:16], P[6][R4, 0:8], op=ALU.add, R=[sml, P[6]], W=[sml])
                k.dve("reciprocal", sml[R4, 40:48], sml[R4, 40:48], R=[sml], W=[sml])
                k.dve("tensor_tensor", orow[R4, :].rearrange("p (h d) -> p h d", h=8), prod[R4, 0:512].rearrange("p (h d) -> p h d", h=8),
                      sml[R4, 40:48].unsqueeze(2).broadcast_to([NSMP, 8, 64]), op=ALU.mult, R=[prod, sml], W=[orow])
                mixs = sb("mixs", [128, KC, NSMP], BF16)
                for i_ in range(4):
                    k.pe("transpose", P[0][:, i_ * NSMP:(i_ + 1) * NSMP], orow[R4, i_ * 128:(i_ + 1) * 128], ident32[R4, R4],
                         R=[orow, ident32], W=[P[0]], inc=(i_ == 3))
                k.dve("tensor_tensor", mixs[:, 0:4, :], P[0][:, 0:4 * NSMP].rearrange("p (a s) -> p a s", a=4), fms[:, 0, :, :],
                      op=ALU.mult, R=[P[0], fms], W=[mixs])
                h0b = sb("h0b", [128, 2, 16, NSMP], BF16)
                h0v = kst[:, 0, :].rearrange("p (r a s) -> p r a s", r=2, a=16)
                hnv = kst[:, 1, :].rearrange("p (r a s) -> p r a s", r=2, a=16)
                with nc.allow_non_contiguous_dma(reason="S5 state re-layout, small"):
                    for ri, hd in enumerate((h0re_d, h0im_d)):
                        for s_ in range(NSMP):
                            k.dma("sp", h0v[:, ri, :, s_], hd[s_].rearrange("(P gm) p -> (gm p) P", gm=2), W=[kst])
                k.act("copy", h0b[:], h0v, R=[kst], W=[h0b])
                uS = fms[:, 1, :, :]
                for kk in range(4):
                    PSk = P[2 + kk]
                    for i_ in range(4):
                        for ri in range(2):
                            k.pe("matmul", PSk[:, (i_ * 2 + ri) * NSMP:(i_ * 2 + ri + 1) * NSMP], Wx[32 * kk:32 * kk + 32, i_, 7, ri, :],
                                 fms[32 * kk:32 * kk + 32, 1, i_, :], start=True, stop=True, R=[Wx, fms], W=[PSk],
                                 inc=(i_ == 3 and ri == 1), tile_position=(32 * kk, 0))
                    k.act("copy", hnv[:, :, kk:16:4, :].rearrange("p r a s -> p a r s"),
                          PSk[:, 0:8 * NSMP].rearrange("p (a r s) -> p a r s", a=4, r=2), R=[PSk], W=[kst])
                a1r = apw[:, 0, :, 1].unsqueeze(2).broadcast_to([128, 16, NSMP])
                a1i = apw[:, 1, :, 1].unsqueeze(2).broadcast_to([128, 16, NSMP])
                T1 = sml[:, 0:64].rearrange("p (a s) -> p a s", a=16)
                tt(T1, h0v[:, 0], a1r, ALU.mult, [kst, apw], [sml]); tt(hnv[:, 0], hnv[:, 0], T1, ALU.add, [kst, sml], [kst])
                tt(T1, h0v[:, 1], a1i, ALU.mult, [kst, apw], [sml]); tt(hnv[:, 0], hnv[:, 0], T1, ALU.subtract, [kst, sml], [kst])
                tt(T1, h0v[:, 1], a1r, ALU.mult, [kst, apw], [sml]); tt(hnv[:, 1], hnv[:, 1], T1, ALU.add, [kst, sml], [kst])
                tt(T1, h0v[:, 0], a1i, ALU.mult, [kst, apw], [sml]); tt(hnv[:, 1], hnv[:, 1], T1, ALU.add, [kst, sml], [kst])
                with nc.allow_non_contiguous_dma(reason="S5 state re-layout, small"):
                    for ri, hd in enumerate((hres_o, hims_o)):
                        for s_ in range(NSMP):
                            k.dma("sp", hd[s_].rearrange("(P gm) p -> (gm p) P", gm=2), hnv[:, ri, :, s_], R=[kst])
                for i_ in range(4):
                    PY = P[i_ % 2]
                    k.pe("matmul", PY[:, 0:NSMP], Klag[:, i_, 0, :], fms[:, 1, i_, :], start=True, stop=False,
                         R=[Klag, fms], W=[PY], inc=False, skip_group_check=True)
                    for kk in range(4):
                        for ri in range(2):
                            last = (kk == 3 and ri == 1)
                            k.pe("matmul", PY[32 * kk:32 * kk + 32, 0:NSMP], Ms[:, 4 * i_ + kk, 1, ri, :], h0b[:, ri, 4 * i_ + kk, :],
                                 start=False, stop=last, R=[Ms, h0b], W=[PY], inc=last, tile_position=(0, 32 * kk),
                                 skip_group_check=True)
                    Yv, Zv = sml[:, 0:NSMP], sml[:, 8:8 + NSMP]
                    k.act("copy", Yv, PY[:, 0:NSMP], R=[PY], W=[sml])
                    tt(Zv, Yv, Yv, ALU.mult, [sml], [sml])
                    k.dve("tensor_scalar", Zv, Zv, 0.044715, 1.0, op0=ALU.mult, op1=ALU.add, R=[sml], W=[sml])
                    tt(Zv, Zv, Yv, ALU.mult, [sml], [sml])
                    k.act("activation", Zv, Zv, AF.Sigmoid, scale=1.5957691216057308, R=[sml], W=[sml])
                    tt(fms[:, 1, i_, :], Yv, Zv, ALU.mult, [sml], [fms])
                for ct in range(4):
                    pg = P[ct % 2]
                    for c in range(4):
                        k.pe("matmul", pg[:, 0:NSMP], w_glu_bf[:, c, ct * 128:(ct + 1) * 128], fms[:, 1, c, :],
                             start=(c == 0), stop=(c == 3), R=[w_glu_bf, fms], W=[pg], inc=(c == 3))
                    k.act("activation", sml[:, 16:16 + NSMP], pg[:, 0:NSMP], AF.Sigmoid, bias=bglu[:, ct:ct + 1], R=[pg, bglu], W=[sml])
                    tt(sml[:, 16:16 + NSMP], sml[:, 16:16 + NSMP], fms[:, 1, ct, :], ALU.mult, [sml, fms], [sml])
                    tt(mixs[:, 4 + ct, :], sml[:, 16:16 + NSMP], fms[:, 2, ct, :], ALU.mult, [sml, fms], [mixs])
                k.dma("sp", xt[1][R4, :], xs, W=[xt[1]])
                for hf in range(2):
                    p = P[5 + hf]
                    for c in range(KC):
                        k.pe("matmul", p[R4, :], mixs[:, c, :], w_out_bf[:, c, hf * 512:(hf + 1) * 512], start=(c == 0), stop=(c == KC - 1),
                             R=[mixs, w_out_bf], W=[p], inc=(c == KC - 1))
                    k.dve("tensor_tensor", tmp[R4, hf * 512:(hf + 1) * 512], p[R4, :], mod[R4, 2 * D + hf * 512:2 * D + (hf + 1) * 512],
                          op=ALU.mult, R=[p, mod], W=[tmp])
                k.dve("tensor_tensor", tmp[R4, :], tmp[R4, :], xt[1][R4, :], op=ALU.add, R=[tmp, xt[1]], W=[tmp])
                rms_rstd(tmp[R4, :], NSMP)
                k.dve("scalar_tensor_tensor", yst[R4, :], tmp[R4, :], rstd[R4, 0:1], gfin_b[R4, :], op0=ALU.mult, op1=ALU.mult,
                      R=[tmp, rstd, gfin_b], W=[yst])
                k.dma("sp", ys_o, yst[R4, :], R=[yst])
        except _Stop:
            pass
        k.finish()
        print("instructions:", k.nins, "sbuf bytes/partition:", acct["bytes"])
    return nc


_CONST = {}


def _consts():
    if not _CONST:
        sel = np.zeros((6, 6, 128), np.float32)
        for r in range(6):
            sel[r, r, :] = 1.0
        _CONST["ident"] = np.eye(128, dtype=np.float32)
        _CONST["tri"] = np.triu(np.ones((128, 128), np.float32))
        _CONST["sel"] = sel.reshape(6, 6 * 128)
        sh = np.zeros((64, 128), np.float32)
        sh[np.arange(64), 64 + np.arange(64)] = 1.0
        _CONST["shift"] = sh
        _CONST["sut"] = np.tril(np.ones((128, 128), np.float32), -1)
        bd = np.zeros((8, 8, 64), np.float32)
        bd[np.arange(8), np.arange(8), :] = 1.0
        _CONST["bd8"] = bd.reshape(8, 512)
        s8 = np.zeros((8, 4, 4), np.float32)
        s8[:, np.arange(4), np.arange(4)] = 1.0
        _CONST["sel8"] = s8.reshape(8, 16)
        _CONST["m9"] = np.tile(np.arange(9, dtype=np.float32), (128, 16)).reshape(128, 144)
        _CONST["m64"] = np.tile(np.arange(1, 65, dtype=np.float32), (128, 16)).reshape(128, 1024)
    return _CONST


def kernel(x_prompt, x_sample, c_prompt, c_sample, cache_k, cache_v, cache_logf,
           state_ssm_re, state_ssm_im, page_table, g_norm, w_ada, b_ada, w_in, b_fgate,
           a_re, a_im, log_dt, b_re, b_im, c_re, c_im, d_skip, w_glu, b_glu, w_out, g_final):
    f = lambda a: np.ascontiguousarray(np.asarray(a, dtype=np.float32))
    x_prompt, x_sample, c_prompt, c_sample = f(x_prompt), f(x_sample), f(c_prompt), f(c_sample)
    cst = _consts()
    nc = build_program()
    shared = {
        "g_norm_d": f(g_norm).reshape(1, D), "g_final_d": f(g_final).reshape(1, D),
        "w_ada_d": f(w_ada).reshape(D, 3 * D), "b_ada_d": f(b_ada).reshape(1, 3 * D),
        "w_in_d": f(w_in).reshape(D, DIN), "w_out_d": f(w_out).reshape(D, D),
        "w_glu_d": f(w_glu).reshape(512, 512), "b_glu_d": f(b_glu).reshape(512),
        "b_fgate_d": f(b_fgate).reshape(1, 8),
        "ident_c": cst["ident"], "tri_c": cst["tri"], "sel_c": cst["sel"], "shift_c": cst["shift"], "m9_c": cst["m9"], "m64_c": cst["m64"],
        **({"sut_c": cst["sut"], "bd8_c": cst["bd8"], "sel8_c": cst["sel8"]} if DBG.get("sample", True) else {}),
        "a_re_d": f(a_re).reshape(32, 64), "a_im_d": f(a_im).reshape(32, 64), "log_dt_d": f(log_dt).reshape(32),
        "b_re_d": f(b_re).reshape(32, 64, 16), "b_im_d": f(b_im).reshape(32, 64, 16),
        "c_re_d": f(c_re).reshape(32, 16, 64), "c_im_d": f(c_im).reshape(32, 16, 64), "d_skip_d": f(d_skip).reshape(32, 16),
    }
    in_maps = []
    for i in range(NCORES):
        m = dict(shared)
        m["x_d"] = x_prompt[NSEQ * i:NSEQ * (i + 1)].reshape(NTOK, D)
        m["xs_d"] = x_sample[NSMP * i:NSMP * (i + 1)].reshape(NSMP, D)
        if DBG.get("sample", True):
            m["ck_d"] = np.asarray(cache_k, dtype=np.float32).reshape(5120 * 128, 512)
            m["cv_d"] = np.asarray(cache_v, dtype=np.float32).reshape(5120 * 128, 512)
            m["clf_d"] = np.asarray(cache_logf, dtype=np.float32).reshape(5120 * 128, 8)
            m["pt_d"] = np.ascontiguousarray(np.asarray(page_table, dtype=np.int32)[NSMP * i:NSMP * (i + 1)]).reshape(1, NSMP * 128)
            m["h0re_d"] = f(state_ssm_re)[0, NSMP * i:NSMP * (i + 1)]
            m["h0im_d"] = f(state_ssm_im)[0, NSMP * i:NSMP * (i + 1)]
        m["cnd"] = np.concatenate([c_sample[NSMP * i:NSMP * (i + 1)], c_prompt[NSEQ * i:NSEQ * (i + 1)]], 0)
        in_maps.append(m)
    in_maps = in_maps[:DBG["cores"]]
    res = run_bass_kernel_spmd(nc, in_maps, core_ids=list(range(DBG["cores"])), **({"trace": True} if DBG.get("trace") else {}))
    if DBG.get("trace"):
        DBG["exec_ns"] = res.exec_time_ns
    R = res.results
    if DBG.get("dump"):
        DBG["dumped"] = {kk: np.asarray(R[0][kk]) for kk in R[0] if kk.startswith("d_")}
    cat = lambda key: np.concatenate([r[key] for r in R], 0)
    y_prompt = cat("yp").reshape(-1, SEQ, D)
    k_prompt = cat("kp").reshape(1, -1, SEQ, 8, 64)
    v_prompt = cat("vp").reshape(1, -1, SEQ, 8, 64)
    logf_prompt = cat("lfp").reshape(1, -1, SEQ, 8)
    hre_p = cat("hre_p").reshape(1, -1, 32, 64)
    him_p = cat("him_p").reshape(1, -1, 32, 64)
    return (y_prompt, cat("ys").reshape(-1, 1, D), k_prompt, v_prompt, logf_prompt, hre_p, him_p,
            cat("ks").reshape(1, -1, 1, 8, 64), cat("vs").reshape(1, -1, 1, 8, 64), cat("lfs").reshape(1, -1, 1, 8),
            cat("hre_s").reshape(1, -1, 32, 64), cat("him_s").reshape(1, -1, 32, 64))
```

```python
import numpy as np
from contextlib import ExitStack
import concourse.bass as bass
import concourse.mybir as mybir
from concourse.bass_utils import run_bass_kernel_spmd

F32 = mybir.dt.float32
BF16 = mybir.dt.bfloat16
I32 = mybir.dt.int32
AF = mybir.ActivationFunctionType
ALU = mybir.AluOpType
AX = mybir.AxisListType

NCORES = 8
D = 1024
SEQ = 2048
NSEQ = 2
NSMP = 4
NTOK = NSEQ * SEQ
DIN = 3080
KC = 8
EPS = 1e-6
COL_K, COL_V, COL_FL = 512, 1024, 2048


class Buf:
    def __init__(self, name, t, psum=False):
        self.name = name
        self.t = t
        self.psum = psum
        self.lw = None
        self.rd = {}
        self.wsem = None
        self.wcnt = 0
        self.rsem = None
        self.rcnt = 0

    def __getitem__(self, idx):
        return self.t[idx]


class K:
    def __init__(self, nc):
        self.nc = nc
        self.E = {"pe": nc.tensor, "act": nc.scalar, "dve": nc.vector,
                  "pool": nc.gpsimd, "sp": nc.sync}
        self.sem = {e: nc.alloc_semaphore(name="c_" + e) for e in self.E}
        self.cnt = {e: 0 for e in self.E}
        self.seen = {e: {} for e in self.E}
        self.bufs = []
        self.nins = {e: 0 for e in self.E}

    def wrap(self, name, t, psum=False):
        b = Buf(name, t, psum)
        self.bufs.append(b)
        return b

    def _wait(self, e, tok):
        nm, sem, val = tok
        if e == "pe" and nm == "c_pe":
            return
        if self.seen[e].get(nm, 0) >= val:
            return
        self.E[e].wait_ge(sem, val)
        self.seen[e][nm] = val

    def _deps(self, e, R, W):
        for b in R:
            if b.lw is not None:
                self._wait(e, b.lw)
            if b.psum:
                for tok in b.rd.values():
                    self._wait(e, tok)
        for b in W:
            if b.lw is not None:
                self._wait(e, b.lw)
            for tok in b.rd.values():
                self._wait(e, tok)

    def _mark(self, tok, R, W):
        for b in R:
            old = b.rd.get(tok[0])
            if old is None or old[2] < tok[2]:
                b.rd[tok[0]] = tok
        for b in W:
            b.lw = tok
            b.rd = {}

    def op(self, e, name, *a, R=(), W=(), inc=True, **kw):
        self._deps(e, R, W)
        ins = getattr(self.E[e], name)(*a, **kw)
        self.nins[e] += 1
        if inc:
            self.cnt[e] += 1
            ins.then_inc(self.sem[e], 1)
            tok = ("c_" + e, self.sem[e], self.cnt[e])
        else:
            tok = ("c_" + e, self.sem[e], self.cnt[e] + 1)
        self._mark(tok, R, W)
        return ins

    def pe(self, name, *a, **kw):
        return self.op("pe", name, *a, **kw)

    def act(self, name, *a, **kw):
        return self.op("act", name, *a, **kw)

    def dve(self, name, *a, **kw):
        return self.op("dve", name, *a, **kw)

    def pool(self, name, *a, **kw):
        return self.op("pool", name, *a, **kw)

    def dma(self, e, out, in_, R=(), W=(), **kw):
        self._deps(e, R, W)
        fi = kw.pop("fn_indirect", None)
        if fi is not None:
            ins = self.E[e].indirect_dma_start(out, None, in_, bass.IndirectOffsetOnAxis(ap=fi, axis=0))
        else:
            ins = self.E[e].dma_start(out=out, in_=in_, **kw)
        self.nins[e] += 1
        if W:
            b = W[0]
            if b.wsem is None:
                b.wsem = self.nc.alloc_semaphore(name="w_" + b.name)
            b.wcnt += 16
            ins.then_inc(b.wsem, 16)
            tok = ("w_" + b.name, b.wsem, b.wcnt)
        else:
            b = R[0]
            if b.rsem is None:
                b.rsem = self.nc.alloc_semaphore(name="r_" + b.name)
            b.rcnt += 16
            ins.then_inc(b.rsem, 16)
            tok = ("r_" + b.name, b.rsem, b.rcnt)
        self._mark(tok, R, W)
        return ins

    def barrier(self):
        toks = [("c_" + e, self.sem[e], self.cnt[e]) for e in self.E if self.cnt[e] > 0]
        for b in self.bufs:
            if b.wsem is not None and b.wcnt > 0:
                toks.append(("w_" + b.name, b.wsem, b.wcnt))
            if b.rsem is not None and b.rcnt > 0:
                toks.append(("r_" + b.name, b.rsem, b.rcnt))
        for e in self.E:
            for t in toks:
                self._wait(e, t)

    def finish(self, e="sp"):
        for b in self.bufs:
            if b.rsem is not None and b.rcnt > 0:
                self._wait(e, ("r_" + b.name, b.rsem, b.rcnt))
            if b.wsem is not None and b.wcnt > 0:
                self._wait(e, ("w_" + b.name, b.wsem, b.wcnt))


BLK = 512
NBLK = SEQ // BLK
COL_Q, COL_K, COL_V, COL_ZA, COL_FL, COL_U, COL_ZS = 0, 512, 1024, 1536, 2048, 2056, 2568
VA_W = 528
class _Stop(Exception):
    pass


def _ck(i):
    if DBG.get("stop") == i:
        raise _Stop()


DBG = {"stop": None, "nseq": NSEQ, "nblk": NBLK, "ktr": True, "att": True, "post": True, "cum": True, "cores": NCORES}


def build_program():
    nc = bass.Bass("TRN2", target_bir_lowering=False)

    def din(name, shape, dt=F32):
        return nc.dram_tensor(name, list(shape), dt, kind="ExternalInput").ap()

    def dout(name, shape, dt=F32):
        return nc.dram_tensor(name, list(shape), dt, kind="ExternalOutput").ap()

    x = din("x_d", [NTOK, D])
    xs = din("xs_d", [NSMP, D])
    cnd = din("cnd", [NSMP + NSEQ, D])
    g_norm = din("g_norm_d", [1, D])
    g_final = din("g_final_d", [1, D])
    w_ada = din("w_ada_d", [D, 3 * D])
    b_ada = din("b_ada_d", [1, 3 * D])
    w_in = din("w_in_d", [D, DIN])
    w_out = din("w_out_d", [D, D])
    w_glu = din("w_glu_d", [512, 512])
    b_glu = din("b_glu_d", [512])
    b_fgate = din("b_fgate_d", [1, 8])
    ident_d = din("ident_c", [128, 128])
    tri_d = din("tri_c", [128, 128])
    sel_d = din("sel_c", [6, 6 * 128])
    shift_d = din("shift_c", [64, 128])
    m9_d = din("m9_c", [128, 16 * 9])
    m64_d = din("m64_c", [128, 16 * 64])
    a_re_d = din("a_re_d", [32, 64])
    a_im_d = din("a_im_d", [32, 64])
    log_dt_d = din("log_dt_d", [32])
    b_re_d = din("b_re_d", [32, 64, 16])
    b_im_d = din("b_im_d", [32, 64, 16])
    c_re_d = din("c_re_d", [32, 16, 64])
    c_im_d = din("c_im_d", [32, 16, 64])
    d_skip_d = din("d_skip_d", [32, 16])
    if DBG.get("sample", True):
        NROW = 5120 * 128
        ck_d = din("ck_d", [NROW, 512])
        cv_d = din("cv_d", [NROW, 512])
        clf_d = din("clf_d", [NROW, 8])
        pt_d = din("pt_d", [1, NSMP * 128], I32)
        h0re_d = din("h0re_d", [NSMP, 32, 64])
        h0im_d = din("h0im_d", [NSMP, 32, 64])
        sut_d = din("sut_c", [128, 128])
        bd8_d = din("bd8_c", [8, 512])
        sel8_d = din("sel8_c", [8, NSMP * NSMP])

    yp = dout("yp", [NTOK, D])
    if DBG.get("dump"):
        d_apw = dout("d_apw", [128, 288]); d_sm = dout("d_sm", [128, 384])
        d_klag = dout("d_klag", [128, 4096], BF16); d_wx = dout("d_wx", [128, 8192], BF16)
        d_ms = dout("d_ms", [128, 9216], BF16); d_rc = dout("d_rc", [128, 1024]); d_rs = dout("d_rs", [128, 1024])
    hre_o = dout("hre_p", [NSEQ, 32, 64])
    ys_o = dout("ys", [NSMP, D])
    hres_o = dout("hre_s", [NSMP, 32, 64])
    hims_o = dout("him_s", [NSMP, 32, 64])
    him_o = dout("him_p", [NSEQ, 32, 64])
    kp = dout("kp", [NTOK, 512])
    vp = dout("vp", [NTOK, 512])
    lfp = dout("lfp", [NTOK, 8])
    ks = dout("ks", [NSMP, 512])
    vs = dout("vs", [NSMP, 512])
    lfs = dout("lfs", [NSMP, 8])

    with ExitStack() as es:
        k = K(nc)

        acct = {"bytes": 0}

        def sb(name, shape, dt):
            n = 1
            for d in shape[1:]:
                n *= d
            acct["bytes"] += n * (2 if dt == BF16 else 4)
            return k.wrap(name, es.enter_context(nc.sbuf_tensor(name, list(shape), dt)))

        def ps(name, shape, dt=F32):
            return k.wrap(name, es.enter_context(nc.psum_tensor(name, list(shape), dt)), psum=True)

        ident32 = sb("ident32", [128, 128], F32)
        ident = sb("ident", [128, 128], BF16)
        tri32 = sb("tri32", [128, 128], F32)
        trib = sb("trib", [128, 128], BF16)
        ones32 = sb("ones32", [128, 128], F32)
        sel = sb("sel", [6, 6 * 128], F32)
        shiftm = sb("shiftm", [64, 128], F32)
        bfg_b = sb("bfg_b", [128, 8], F32)
        bglu = sb("bglu", [128, 4], F32)
        mod = sb("mod", [6, 3 * D], F32)
        cT = sb("cT", [128, KC, 6], F32)
        cTs = sb("cTs", [128, KC, 6], F32)
        w_out_bf = sb("w_out_bf", [128, KC, D], BF16)
        w_glu_bf = sb("w_glu_bf", [128, 4, 512], BF16)
        gfin_b = sb("gfin_b", [128, D], F32)
        sc1_b = sb("sc1_b", [128, D], F32)
        shift_b = sb("shift_b", [128, D], F32)
        gate_b = sb("gate_b", [128, D], F32)
        wring = [sb("wr%d" % i, [128, KC, 128], BF16) for i in range(3)]
        P = [ps("P%d" % i, [128, 512]) for i in range(7)]
        PT = ps("PT", [128, KC, 128], BF16)

        xt = [sb("xt%d" % i, [128, D], F32) for i in range(2)]
        tmp = sb("tmp", [128, D], F32)
        yst = sb("yst", [128, D], F32)
        hb = sb("hb", [128, D], BF16)
        hT = sb("hT", [128, KC, BLK], BF16)
        qT = sb("qT", [128, 4, BLK], BF16)
        uT = sb("uT", [128, 4, BLK], BF16)
        gT = qT
        mixT = sb("mixT", [128, KC, BLK], BF16)
        kT = sb("kT", [128, 4, SEQ], BF16)
        vaug = sb("vaug", [128, 16 * VA_W], BF16)
        vaug4 = vaug[:].rearrange("p (t h c) -> p t h c", t=16, h=8, c=66)
        kTf = sb("kTf", [128, BLK], F32)
        o_sb = kTf
        kst = sb("kst", [128, 4, 128], F32)
        vst = sb("vst", [128, 4, 128], F32)
        pTb = [sb("pTb%d" % i, [128, BLK], BF16) for i in range(2)]
        sg = sb("sg", [128, BLK], BF16)
        attf = sb("attf", [128, BLK], F32)
        bc_sb = sb("bc_sb", [128, BLK], F32)
        rec = sb("rec", [128, BLK], F32)
        ss = sb("ss", [128, 1], F32)
        rstd = sb("rstd", [128, 1], F32)
        lfz = sb("lfz", [128, 4, 8], F32)
        lf = sb("lf", [128, 4, 8], F32)
        E5 = sb("E5", [128, 5, 8], F32)
        ncum = sb("ncum", [128, NBLK, 4, 8], F32)
        totb = sb("totb", [128, NBLK, 8], F32)
        sacc = sb("sacc", [128, 8], F32)
        biask = sb("biask", [128, NBLK, 4, 8], F32)
        scg = sb("scg", [128, 64], F32)

        Wx = sb("Wx", [128, 4, 8, 2, 128], BF16)
        Ms = sb("Ms", [128, 16, 9, 2, 32], BF16)
        Klag = sb("Klag", [128, 4, 8, 128], BF16)
        rc = sb("rc", [128, 16, 64], F32)
        rs = sb("rs", [128, 16, 64], F32)
        sm = sb("sm", [128, 24, 16], F32)
        apw = sb("apw", [128, 2, 16, 9], F32)
        dcol = sb("dcol", [128, 4], F32)
        hin = sb("hin", [128, 2, 16], F32)
        zb = sb("zb", [128, 128], BF16)
        ARE, AIM, LDT, DT, LR, LI, DEN, KRE, KIM, NRE, NIM, TA, TB, RR = range(14)
        k.dma("sp", ident32[:], ident_d, W=[ident32])
        k.dve("tensor_copy", ident[:], ident32[:], R=[ident32], W=[ident])
        k.dma("sp", tri32[:], tri_d, W=[tri32])
        k.dve("tensor_copy", trib[:], tri32[:], R=[tri32], W=[trib])
        k.dve("memset", ones32[:], 1.0, W=[ones32])
        k.dma("sp", sel[:], sel_d, W=[sel])
        k.dma("sp", shiftm[:], shift_d, W=[shiftm])
        k.dma("sp", gfin_b[:], g_final.broadcast_to([128, D]), W=[gfin_b])
        k.dma("sp", bfg_b[:], b_fgate.broadcast_to([128, 8]), W=[bfg_b])
        k.dma("sp", mod[:], b_ada.broadcast_to([6, 3 * D]), W=[mod])
        with nc.allow_non_contiguous_dma(reason="tiny transposed loads"):
            for n in range(6):
                k.dma("sp", cT[:, :, n], cnd[n, :].rearrange("(c p) -> p c", p=128), W=[cT])
            k.dma("sp", bglu[:], b_glu.rearrange("(c p) -> p c", p=128), W=[bglu])
        k.act("activation", cTs[:], cT[:], AF.Silu, R=[cT], W=[cTs])
        w_in_v = w_in.rearrange("(c p) n -> p c n", p=128)
        w_out_v = w_out.rearrange("(c p) n -> p c n", p=128)
        w_glu_v = w_glu.rearrange("(c p) n -> p c n", p=128)
        for c in range(KC):
            k.dma("pool", w_out_bf[:, c, :], w_out_v[:, c, :], W=[w_out_bf])
        for c in range(4):
            k.dma("pool", w_glu_bf[:, c, :], w_glu_v[:, c, :], W=[w_glu_bf])

        try:
            _ck(1)
            w_ada_v = w_ada.rearrange("(c p) n -> p c n", p=128)
            aring = [gate_b, sc1_b, shift_b]
            ai = 0
            for cc in range(6):
                p = P[cc % 2]
                for c2 in range(KC // 2):
                    wb = aring[ai % 3]
                    ai += 1
                    wv = wb[:, :].rearrange("p (c n) -> p c n", c=2)
                    k.dma("sp", wv, w_ada_v[:, 2 * c2:2 * c2 + 2, cc * 512:(cc + 1) * 512], W=[wb])
                    for cl in range(2):
                        c = 2 * c2 + cl
                        k.pe("matmul", p[0:6, :], cTs[:, c, :], wv[:, cl, :], start=(c == 0), stop=(c == KC - 1),
                             R=[cTs, wb], W=[p], inc=(cl == 1))
                k.dve("tensor_tensor", mod[:, cc * 512:(cc + 1) * 512], p[0:6, :], mod[:, cc * 512:(cc + 1) * 512],
                      op=ALU.add, R=[p, mod], W=[mod])

            PI = float(np.pi)
            MAGIC = 12582912.0
            C1, C2 = 6.28125, 2.0 * float(np.pi) - 6.28125

            def sincos(argB, argv, sB, sv, cB, cv, t1B, t1v, t2B, t2v):
                k.dve("tensor_scalar", t1v, argv, 1.0 / (2.0 * PI), None, op0=ALU.mult, R=[argB], W=[t1B])
                k.dve("tensor_scalar", t1v, t1v, MAGIC, None, op0=ALU.add, R=[t1B], W=[t1B])
                k.dve("tensor_scalar", t1v, t1v, -MAGIC, None, op0=ALU.add, R=[t1B], W=[t1B])
                k.dve("scalar_tensor_tensor", t2v, t1v, -C1, argv, op0=ALU.mult, op1=ALU.add, R=[t1B, argB], W=[t2B])
                k.dve("scalar_tensor_tensor", t2v, t1v, -C2, t2v, op0=ALU.mult, op1=ALU.add, R=[t1B, t2B], W=[t2B])
                k.dve("tensor_scalar", t2v, t2v, PI, -PI, op0=ALU.min, op1=ALU.max, R=[t2B], W=[t2B])
                k.act("activation", sv, t2v, AF.Sin, R=[t2B], W=[sB])
                k.dve("scalar_tensor_tensor", t1v, t2v, -1.0, t2v, op0=ALU.mult, op1=ALU.max, R=[t2B], W=[t1B])
                k.act("activation", cv, t1v, AF.Sin, scale=-1.0, bias=PI / 2.0, R=[t1B], W=[cB])

            def tt(out, a, b, op, R, W):
                k.dve("tensor_tensor", out, a, b, op=op, R=R, W=W)

            def S(i):
                return sm[:, i, :]

            def bc16(ap2):
                return ap2.unsqueeze(2).broadcast_to([128, 16, 16])

            l1 = lambda d: d.rearrange("(P gm) p -> (gm p) P", gm=2)
            with nc.allow_non_contiguous_dma(reason="small S5 parameter re-layouts"):
                k.dma("sp", S(ARE), l1(a_re_d), W=[sm])
                k.dma("sp", S(AIM), l1(a_im_d), W=[sm])
                ldv = log_dt_d.rearrange("(P gm) -> gm P", gm=2)
                for gm in range(2):
                    k.dma("sp", sm[gm * 64:(gm + 1) * 64, LDT, :], ldv[gm:gm + 1, :].broadcast_to([64, 16]), W=[sm])
                k.dma("sp", dcol[:], d_skip_d.rearrange("(t g) h -> (g h) t", t=4), W=[dcol])
                Bq = [tmp[:, 0:256].rearrange("p (a h) -> p a h", a=16), tmp[:, 256:512].rearrange("p (a h) -> p a h", a=16)]
                Cq = [tmp[:, 512:768].rearrange("p (a h) -> p a h", a=16), tmp[:, 768:1024].rearrange("p (a h) -> p a h", a=16)]
                l1b = lambda d: d.rearrange("(P gm) p h -> (gm p) P h", gm=2)
                k.dma("sp", Bq[0], l1b(b_re_d), W=[tmp])
                k.dma("sp", Bq[1], l1b(b_im_d), W=[tmp])
                for ri, cd in enumerate((c_re_d, c_im_d)):
                    cv4 = cd.rearrange("(P gm) h p -> gm P p h", gm=2)
                    for gm in range(2):
                        for Pp in range(16):
                            k.dma("sp", tmp[gm * 64:(gm + 1) * 64, 512 + ri * 256 + Pp * 16:512 + ri * 256 + (Pp + 1) * 16],
                                  cv4[gm, Pp, :, :], W=[tmp])
            m9 = yst[:, 0:144].rearrange("p (a m) -> p a m", a=16)
            k.dma("sp", yst[:, 0:144], m9_d, W=[yst])
            k.act("activation", S(DT), S(LDT), AF.Exp, R=[sm], W=[sm])
            tt(S(LR), S(ARE), S(DT), ALU.mult, [sm], [sm])
            tt(S(LI), S(AIM), S(DT), ALU.mult, [sm], [sm])
            argm = yst[:, 144:288].rearrange("p (a m) -> p a m", a=16)
            magm = yst[:, 288:432].rearrange("p (a m) -> p a m", a=16)
            t1m = yst[:, 432:576].rearrange("p (a m) -> p a m", a=16)
            t2m = yst[:, 576:720].rearrange("p (a m) -> p a m", a=16)
            snm = yst[:, 720:864].rearrange("p (a m) -> p a m", a=16)
            csm = yst[:, 864:1008].rearrange("p (a m) -> p a m", a=16)
            b9 = lambda ap2: ap2.unsqueeze(2).broadcast_to([128, 16, 9])
            tt(argm, m9, b9(S(LI)), ALU.mult, [yst, sm], [yst])
            tt(magm, m9, b9(S(LR)), ALU.mult, [yst, sm], [yst])
            k.act("activation", magm, magm, AF.Exp, R=[yst], W=[yst])
            sincos(yst, argm, yst, snm, yst, csm, yst, t1m, yst, t2m)
            tt(apw[:, 0, :, :], magm, csm, ALU.mult, [yst], [apw])
            tt(apw[:, 1, :, :], magm, snm, ALU.mult, [yst], [apw])
            k.dve("tensor_scalar", S(TA), apw[:, 0, :, 1], -1.0, None, op0=ALU.add, R=[apw], W=[sm])
            tt(S(NRE), S(TA), S(ARE), ALU.mult, [sm], [sm])
            tt(S(TB), apw[:, 1, :, 1], S(AIM), ALU.mult, [apw, sm], [sm])
            tt(S(NRE), S(NRE), S(TB), ALU.add, [sm], [sm])
            tt(S(NIM), apw[:, 1, :, 1], S(ARE), ALU.mult, [apw, sm], [sm])
            tt(S(TB), S(TA), S(AIM), ALU.mult, [sm], [sm])
            tt(S(NIM), S(NIM), S(TB), ALU.subtract, [sm], [sm])
            tt(S(DEN), S(ARE), S(ARE), ALU.mult, [sm], [sm])
            tt(S(TB), S(AIM), S(AIM), ALU.mult, [sm], [sm])
            tt(S(DEN), S(DEN), S(TB), ALU.add, [sm], [sm])
            k.dve("reciprocal", S(DEN), S(DEN), R=[sm], W=[sm])
            tt(S(KRE), S(NRE), S(DEN), ALU.mult, [sm], [sm])
            tt(S(KIM), S(NIM), S(DEN), ALU.mult, [sm], [sm])
            k.dve("tensor_copy", S(RR), magm[:, :, 8], R=[yst], W=[sm])
            Bb = [xt[0][:, 0:256].rearrange("p (a h) -> p a h", a=16), xt[0][:, 256:512].rearrange("p (a h) -> p a h", a=16)]
            U1 = xt[0][:, 512:768].rearrange("p (a h) -> p a h", a=16)
            U2 = xt[0][:, 768:1024].rearrange("p (a h) -> p a h", a=16)
            tt(U1, Bq[0], bc16(S(KRE)), ALU.mult, [tmp, sm], [xt[0]])
            tt(U2, Bq[1], bc16(S(KIM)), ALU.mult, [tmp, sm], [xt[0]])
            tt(Bb[0], U1, U2, ALU.subtract, [xt[0]], [xt[0]])
            tt(U1, Bq[1], bc16(S(KRE)), ALU.mult, [tmp, sm], [xt[0]])
            tt(U2, Bq[0], bc16(S(KIM)), ALU.mult, [tmp, sm], [xt[0]])
            tt(Bb[1], U1, U2, ALU.add, [xt[0]], [xt[0]])
            Xs = [vaug[:, 0:4096].rearrange("p (m a c) -> p m a c", a=16, m=8),
                  vaug[:, 4096:8192].rearrange("p (m a c) -> p m a c", a=16, m=8)]
            k.pool("memset", vaug[:, 0:8192], 0.0, W=[vaug])
            k.pool("memset", Ms[:], 0.0, W=[Ms])
            k.pool("memset", zb[:], 0.0, W=[zb])
            V1 = xt[1][:, 0:256].rearrange("p (a h) -> p a h", a=16)
            V2 = xt[1][:, 256:512].rearrange("p (a h) -> p a h", a=16)
            V3 = xt[1][:, 512:768].rearrange("p (a h) -> p a h", a=16)

            def cmul_strips(src, m, dst_re, dst_im, neg_im):
                ar, ai = bc16(apw[:, 0, :, m]), bc16(apw[:, 1, :, m])
                tt(V1, src[0], ar, ALU.mult, [tmp, xt[0], apw], [xt[1]])
                tt(V2, src[1], ai, ALU.mult, [tmp, xt[0], apw], [xt[1]])
                tt(V3, V1, V2, ALU.subtract, [xt[1]], [xt[1]])
                for gm in range(2):
                    k.act("copy", dst_re(gm), V3[gm * 64:(gm + 1) * 64, :, :], R=[xt[1]], W=[vaug, Ms])
                tt(V1, src[1], ar, ALU.mult, [tmp, xt[0], apw], [xt[1]])
                tt(V2, src[0], ai, ALU.mult, [tmp, xt[0], apw], [xt[1]])
                tt(V3, V1, V2, ALU.add, [xt[1]], [xt[1]])
                if neg_im:
                    k.dve("tensor_scalar", V3, V3, -1.0, None, op0=ALU.mult, R=[xt[1]], W=[xt[1]])
                for gm in range(2):
                    k.act("copy", dst_im(gm), V3[gm * 64:(gm + 1) * 64, :, :], R=[xt[1]], W=[vaug, Ms])

            for m in range(8):
                cmul_strips(Bb, m,
                            lambda gm, m=m: Xs[0][gm * 64:(gm + 1) * 64, m, :, gm * 16:(gm + 1) * 16],
                            lambda gm, m=m: Xs[1][gm * 64:(gm + 1) * 64, m, :, gm * 16:(gm + 1) * 16], False)
            for jj in range(9):
                cmul_strips(Cq, jj,
                            lambda gm, jj=jj: Ms[gm * 64:(gm + 1) * 64, :, jj, 0, gm * 16:(gm + 1) * 16],
                            lambda gm, jj=jj: Ms[gm * 64:(gm + 1) * 64, :, jj, 1, gm * 16:(gm + 1) * 16], True)
            for i in range(4):
                for s_ in range(8):
                    for ri in range(2):
                        k.pe("transpose", PT[:, 0, :], Xs[ri][:, 7 - s_, 4 * i:4 * i + 4, :].rearrange("p a c -> p (a c)"), ident[:, :],
                             R=[vaug, ident], W=[PT])
                        k.act("copy", Wx[:, i, s_, ri, :], PT[:, 0, :], R=[PT], W=[Wx])
            for i in range(4):
                for tau in range(8):
                    pk = P[(i * 8 + tau) % 2]
                    k.pe("matmul", pk[:, 0:128], zb[:, :], zb[:, :], start=True, stop=False, R=[zb], W=[pk], inc=False)
                    for kk in range(4):
                        for ri in range(2):
                            last = (kk == 3 and ri == 1)
                            k.pe("matmul", pk[32 * kk:32 * kk + 32, 32 * kk:32 * kk + 32], Xs[ri][:, tau, 4 * i + kk, :],
                                 Ms[:, 4 * i + kk, 0, ri, :], start=False, stop=last, R=[vaug, Ms], W=[pk], inc=last,
                                 tile_position=(0, 32 * kk), skip_group_check=True)
                    if tau == 0:
                        k.dve("scalar_tensor_tensor", Klag[:, i, tau, :], ident32[:, :], dcol[:, i:i + 1], pk[:, 0:128],
                              op0=ALU.mult, op1=ALU.add, R=[ident32, dcol, pk], W=[Klag])
                    else:
                        k.act("copy", Klag[:, i, tau, :], pk[:, 0:128], R=[pk], W=[Klag])
            k.dma("sp", tmp[:], m64_d, W=[tmp])
            k.dve("tensor_scalar", S(TA), S(LI), 8.0, None, op0=ALU.mult, R=[sm], W=[sm])
            v3 = lambda b: b[:].rearrange("p (a c) -> p a c", a=16)
            tt(v3(yst), v3(tmp), S(TA).unsqueeze(2).broadcast_to([128, 16, 64]), ALU.mult, [tmp, sm], [yst])
            sincos(yst, v3(yst), rs, rs[:], rc, rc[:], xt[0], v3(xt[0]), xt[1], v3(xt[1]))
            k.pool("memset", vaug[:], 1.0, W=[vaug])
            if DBG.get("dump"):
                k.dma("sp", d_apw, apw[:].rearrange("p a b c -> p (a b c)"), R=[apw])
                k.dma("sp", d_sm, sm[:].rearrange("p a b -> p (a b)"), R=[sm])
                k.dma("sp", d_klag, Klag[:].rearrange("p a b c -> p (a b c)"), R=[Klag])
                k.dma("sp", d_wx, Wx[:].rearrange("p a b c d -> p (a b c d)"), R=[Wx])
                k.dma("sp", d_ms, Ms[:].rearrange("p a b c d -> p (a b c d)"), R=[Ms])
                k.dma("sp", d_rc, rc[:].rearrange("p a b -> p (a b)"), R=[rc])
                k.dma("sp", d_rs, rs[:].rearrange("p a b -> p (a b)"), R=[rs])
            _ck(2)

            def bcast_mod(row):
                k.dma("sp", tmp[:], g_norm.broadcast_to([128, D]), W=[tmp])
                for part, dst in ((0, shift_b), (1, sc1_b), (2, gate_b)):
                    for h in range(2):
                        p = P[2 + h]
                        k.pe("matmul", p[:], sel[:, row * 128:(row + 1) * 128],
                             mod[:, part * D + h * 512:part * D + (h + 1) * 512],
                             start=True, stop=True, R=[sel, mod], W=[p])
                        if part == 1:
                            k.dve("scalar_tensor_tensor", dst[:, h * 512:(h + 1) * 512], p[:], 1.0,
                                  tmp[:, h * 512:(h + 1) * 512], op0=ALU.add, op1=ALU.mult, R=[p, tmp], W=[dst])
                        else:
                            k.act("copy", dst[:, h * 512:(h + 1) * 512], p[:], R=[p], W=[dst])

            st = {"piece": 0, "xt": 0}

            def load_piece(col0, ncols):
                b = wring[st["piece"] % 3]
                st["piece"] += 1
                k.dma("pool", b[:, :, 0:ncols], w_in_v[:, :, col0:col0 + ncols], W=[b])
                return b

            def rms_rstd(src, Pn):
                k.act("activation", yst[0:Pn, :], src, AF.Square, accum_out=ss[0:Pn, :], R=[tmp, xt[0], xt[1]], W=[yst, ss])
                k.dve("tensor_scalar", rstd[0:Pn, :], ss[0:Pn, :], 1.0 / D, EPS, op0=ALU.mult, op1=ALU.add,
                      R=[ss], W=[rstd])
                k.act("activation", rstd[0:Pn, :], rstd[0:Pn, :], AF.Sqrt, R=[rstd], W=[rstd])
                k.dve("reciprocal", rstd[0:Pn, :], rstd[0:Pn, :], R=[rstd], W=[rstd])

            def pre_tile(Pn, x_src, sc1, shf, col):
                xb = xt[st["xt"] % 2]
                st["xt"] += 1
                k.dma("sp", xb[0:Pn, :], x_src, W=[xb])
                rms_rstd(xb[0:Pn, :], Pn)
                k.dve("scalar_tensor_tensor", tmp[0:Pn, :], xb[0:Pn, :], rstd[0:Pn, 0:1], sc1,
                      op0=ALU.mult, op1=ALU.mult, R=[xb, rstd, sc1_b, mod], W=[tmp])
                k.dve("tensor_tensor", hb[0:Pn, :], tmp[0:Pn, :], shf, op=ALU.add, R=[tmp, shift_b, mod], W=[hb])
                for c in range(KC):
                    k.pe("transpose", PT[:, c, 0:Pn], hb[0:Pn, c * 128:(c + 1) * 128], ident[0:Pn, 0:Pn],
                         R=[hb, ident], W=[PT], inc=(c == KC - 1))
                k.act("copy", hT[:, :, col:col + Pn], PT[:, :, 0:Pn], R=[PT], W=[hT])

            def logf_from(psrc, Pn, nj, dst):
                k.dve("tensor_tensor", lfz[0:Pn, 0:nj, :], psrc, bfg_b[0:Pn, :].unsqueeze(1).broadcast_to([Pn, nj, 8]),
                      op=ALU.add, R=[P[3], bfg_b], W=[lfz])
                k.act("activation", lfz[0:Pn, 0:nj, :], lfz[0:Pn, 0:nj, :], AF.Exp, scale=-1.0, R=[lfz], W=[lfz])
                k.act("activation", lfz[0:Pn, 0:nj, :], lfz[0:Pn, 0:nj, :], AF.Ln, bias=1.0, R=[lfz], W=[lfz])
                k.dve("tensor_scalar", dst, lfz[0:Pn, 0:nj, :], -1.0, None, op0=ALU.mult, R=[lfz], W=[lf])

            def fm_piece(col0, evac):
                wb = load_piece(col0, 128)
                p = P[st["piece"] % 2]
                for c in range(KC):
                    k.pe("matmul", p[:, 0:BLK], wb[:, c, 0:128], hT[:, c, 0:BLK], start=(c == 0), stop=(c == KC - 1),
                         R=[wb, hT], W=[p], inc=(c == KC - 1))
                evac(p)

            for n in range(DBG["nseq"]):
                bcast_mod(NSMP + n)
                _ck(3)
                for B in range(DBG["nblk"]):
                    t0 = B * BLK
                    r0 = n * SEQ + t0
                    for j in range(4):
                        pre_tile(128, x[r0 + j * 128:r0 + (j + 1) * 128, :], sc1_b[:], shift_b[:], j * 128)
                    _ck(4)
                    for i in range(4):
                        fm_piece(COL_Q + i * 128,
                                 lambda p, i=i: k.act("copy", qT[:, i, :], p[:, 0:BLK], R=[p], W=[qT]))
                    _ck(5)
                    kpend = [None]
                    for i in range(4):
                        def k_part2(i=i):
                            for j in range(4):
                                k.pe("transpose", P[4][:, j * 128:(j + 1) * 128], kTf[:, j * 128:(j + 1) * 128],
                                     ident32[:], R=[kTf, ident32], W=[P[4]], inc=(j == 3))
                            k.dve("tensor_copy", kst[:, :, :],
                                  P[4][:, 0:512].rearrange("p (j c) -> p j c", j=4), R=[P[4]], W=[kst])
                            for j in range(4):
                                k.dma("sp", kp[r0 + j * 128:r0 + (j + 1) * 128, i * 128:(i + 1) * 128], kst[:, j, :], R=[kst])

                        def ev_k(p, i=i, part2=k_part2):
                            if kpend[0] is not None:
                                kpend[0]()
                                kpend[0] = None
                            k.act("copy", kT[:, i, t0:t0 + BLK], p[:, 0:BLK], R=[p], W=[kT])
                            if not DBG["ktr"]:
                                return
                            k.dve("tensor_copy", kTf[:], p[:, 0:BLK], R=[p], W=[kTf])
                            kpend[0] = part2
                        fm_piece(COL_K + i * 128, ev_k)
                    _ck(6)
                    for i in range(4):
                        wb = load_piece(COL_V + i * 128, 128)
                        pv = P[5 + i % 2]
                        for j in range(4):
                            for c in range(KC):
                                k.pe("matmul", pv[:, j * 128:(j + 1) * 128], hT[:, c, j * 128:(j + 1) * 128], wb[:, c, 0:128],
                                     start=(c == 0), stop=(c == KC - 1), R=[hT, wb], W=[pv],
                                     inc=(c == KC - 1 and j == 3))
                        if kpend[0] is not None:
                            kpend[0]()
                            kpend[0] = None
                        k.dve("tensor_copy", vst[:, :, :],
                              pv[:, 0:512].rearrange("p (j c) -> p j c", j=4), R=[pv], W=[vst])
                        for j in range(4):
                            k.dma("sp", vp[r0 + j * 128:r0 + (j + 1) * 128, i * 128:(i + 1) * 128], vst[:, j, :], R=[vst])
                        for j in range(4):
                            T = B * 4 + j
                            if DBG.get("novaug"):
                                continue
                            for a in range(2):
                                o = T * VA_W + (2 * i + a) * 66
                                k.act("copy", vaug[:, o:o + 64], pv[:, j * 128 + a * 64:j * 128 + (a + 1) * 64],
                                      R=[pv], W=[vaug])
                    _ck(7)
                    for i in range(4):
                        fm_piece(COL_ZA + i * 128,
                                 lambda p, i=i: k.act("activation", mixT[:, i, :], p[:, 0:BLK], AF.Silu, R=[p], W=[mixT]))
                    _ck(8)
                    wb = load_piece(COL_FL, 8)
                    pf = P[3]
                    for j in range(4):
                        for c in range(KC):
                            k.pe("matmul", pf[:, j * 8:(j + 1) * 8], hT[:, c, j * 128:(j + 1) * 128], wb[:, c, 0:8],
                                 start=(c == 0), stop=(c == KC - 1), R=[hT, wb], W=[pf], inc=(c == KC - 1 and j == 3))
                    logf_from(pf[:, 0:32].rearrange("p (j h) -> p j h", j=4), 128, 4, lf[:])
                    for j in range(4):
                        k.dma("sp", lfp[r0 + j * 128:r0 + (j + 1) * 128, :], lf[:, j, :], R=[lf])
                    if DBG["cum"]:
                      k.dve("memset", E5[:, 0, :], 0.0, W=[E5])
                      for j in range(4):
                          k.dve("tensor_tensor", E5[:, j + 1, :], E5[:, j, :], lf[:, j, :], op=ALU.add, R=[E5, lf], W=[E5])
                      pc = P[6]
                      k.pe("matmul", pc[:, 0:32], tri32[:], lf[:].rearrange("p j h -> p (j h)"), start=True, stop=False,
                           R=[tri32, lf], W=[pc], inc=False)
                      k.pe("matmul", pc[:, 0:32], ones32[:], E5[:, 0:4, :].rearrange("p j h -> p (j h)"), start=False, stop=True,
                           R=[ones32, E5], W=[pc], inc=False)
                      k.pe("matmul", pc[:, 32:40], ones32[:], E5[:, 4, :], start=True, stop=True, R=[ones32, E5], W=[pc])
                      k.dve("tensor_scalar", ncum[:, B, :, :], pc[:, 0:32].rearrange("p (j h) -> p j h", j=4), -1.0, None,
                            op0=ALU.mult, R=[pc], W=[ncum])
                      k.dve("tensor_copy", totb[:, B, :], pc[:, 32:40], R=[pc], W=[totb])
                      k.dve("memset", sacc[:], 0.0, W=[sacc])
                      for Bp in range(B, -1, -1):
                          k.dve("tensor_tensor", biask[:, Bp, :, :], ncum[:, Bp, :, :],
                                sacc[:].unsqueeze(1).broadcast_to([128, 4, 8]), op=ALU.add, R=[ncum, sacc], W=[biask])
                          if Bp > 0:
                              k.dve("tensor_tensor", sacc[:], sacc[:], totb[:, Bp - 1, :], op=ALU.add, R=[sacc, totb], W=[sacc])
                    _ck(9)
                    for i in range(4):
                        fm_piece(COL_U + i * 128,
                                 lambda p, i=i: k.act("copy", uT[:, i, :], p[:, 0:BLK], R=[p], W=[uT]))
                    for i in range(4):
                        fm_piece(COL_ZS + i * 128,
                                 lambda p, i=i: k.act("activation", mixT[:, 4 + i, :], p[:, 0:BLK], AF.Silu, R=[p], W=[mixT]))

                    nT = 4 * (B + 1)
                    pend = None
                    for h in (range(8) if DBG["att"] else []):
                        i, odd = h // 2, h % 2
                        pb = 64 * odd
                        po = P[2 + odd]

                        def qk(T):
                            jd = T - 4 * B
                            c0 = max(jd, 0) * 128
                            pS = P[T % 2]
                            k.pe("matmul", pS[:, c0:BLK], kT[pb:pb + 64, i, T * 128:(T + 1) * 128], qT[pb:pb + 64, i, c0:BLK],
                                 start=True, stop=True, R=[kT, qT], W=[pS], tile_position=(pb, 0))
                            return c0
                        c0n = qk(0)
                        for T in range(nT):
                            c0 = c0n
                            pS = P[T % 2]
                            pt = pTb[T % 2]
                            if T + 1 < nT:
                                c0n = qk(T + 1)
                            Bp, j = T // 4, T % 4
                            k.act("activation", pt[:, c0:BLK], pS[:, c0:BLK], AF.Exp, scale=0.125,
                                  bias=biask[:, Bp, j, h:h + 1], R=[pS, biask], W=[pt])
                            if Bp == B:
                                k.dve("tensor_tensor", pt[:, c0:c0 + 128], pt[:, c0:c0 + 128], trib[:], op=ALU.mult,
                                      R=[pt, trib], W=[pt])
                            o = T * VA_W + h * 66
                            k.pe("matmul", po[0:65, c0:BLK], vaug[:, o:o + 65], pt[:, c0:BLK],
                                 start=(T == 0), stop=(T == nT - 1), R=[vaug, pt], W=[po], inc=(T == nT - 1))
                        def _norm(h=h, i=i, odd=odd, pb=pb, po=po):
                            k.dve("reciprocal", rec[64:65, :], po[64:65, 0:BLK], R=[po], W=[rec])
                            k.pe("matmul", P[4][pb:pb + 64, 0:BLK], ones32[64:65, 0:64], rec[64:65, :],
                                 start=True, stop=True, R=[ones32, rec], W=[P[4]], tile_position=(64, pb))
                            k.dve("tensor_copy", bc_sb[pb:pb + 64, :], P[4][pb:pb + 64, 0:BLK], R=[P[4]], W=[bc_sb])
                            if not odd:
                                osrc = po
                            else:
                                k.dve("tensor_copy", o_sb[0:64, :], po[0:64, 0:BLK], R=[po], W=[o_sb])
                                k.pe("matmul", P[6][:, 0:BLK], shiftm[:, :], o_sb[0:64, :], start=True, stop=True,
                                     R=[shiftm, o_sb], W=[P[6]])
                                osrc = P[6]
                            k.dve("tensor_tensor", attf[pb:pb + 64, :], osrc[pb:pb + 64, 0:BLK], bc_sb[pb:pb + 64, :], op=ALU.mult,
                                  R=[osrc, bc_sb], W=[attf])
                            k.dve("tensor_tensor", mixT[pb:pb + 64, i, :], attf[pb:pb + 64, :], mixT[pb:pb + 64, i, :], op=ALU.mult,
                                  R=[attf, mixT], W=[mixT])
                        if pend is not None:
                            pend()
                        pend = _norm
                    if pend is not None:
                        pend()

                    if DBG.get("s5off"):
                        k.dve("memset", mixT[:, 4:8, :], 0.0, W=[mixT])
                    else:
                        if B == 0:
                            k.dve("memset", hin[:], 0.0, W=[hin])
                        NC_ = BLK // 8
                        S2, Gin, TT_, Gout = attf, bc_sb, rec, kTf
                        HPB, YB, ZB = [sg, hb], [tmp, yst], [xt[0], xt[1]]
                        v4 = lambda b: b[:, 0:BLK].rearrange("p (r a c) -> p r a c", r=2, a=4)

                        def stage_a(i):
                            Hp = HPB[i % 2]
                            tb = lambda t: t[:, 4 * i:4 * i + 4, :]
                            Sv, Gi, Tv, Go = v4(S2), v4(Gin), v4(TT_), v4(Gout)
                            for kk in range(4):
                                PS = P[2 + kk]
                                for ri in range(2):
                                    for s_ in range(8):
                                        last = (s_ == 7 and ri == 1)
                                        k.pe("matmul", PS[:, ri * NC_:(ri + 1) * NC_],
                                             Wx[32 * kk:32 * kk + 32, i, s_, ri, :], uT[32 * kk:32 * kk + 32, i, s_:BLK:8],
                                             start=(s_ == 0), stop=(s_ == 7), R=[Wx, uT], W=[PS], inc=last,
                                             tile_position=(32 * kk, 0))
                                k.act("copy", Sv[:, :, kk, :], PS[:, 0:2 * NC_].rearrange("p (r c) -> p r c", r=2), R=[PS], W=[S2])
                            tt(Tv[:, 0], Sv[:, 0], tb(rc), ALU.mult, [S2, rc], [TT_])
                            tt(Tv[:, 1], Sv[:, 1], tb(rs), ALU.mult, [S2, rs], [TT_])
                            tt(Gi[:, 0], Tv[:, 0], Tv[:, 1], ALU.add, [TT_], [Gin])
                            tt(Tv[:, 0], Sv[:, 1], tb(rc), ALU.mult, [S2, rc], [TT_])
                            tt(Tv[:, 1], Sv[:, 0], tb(rs), ALU.mult, [S2, rs], [TT_])
                            tt(Gi[:, 1], Tv[:, 0], Tv[:, 1], ALU.subtract, [TT_], [Gin])
                            for ri in range(2):
                                for kk in range(4):
                                    Pp = 4 * i + kk
                                    k.dve("tensor_tensor_scan", Go[:, ri, kk, :], sm[:, RR, Pp:Pp + 1].broadcast_to([128, NC_]),
                                          Gi[:, ri, kk, :], hin[:, ri, Pp:Pp + 1], op0=ALU.mult, op1=ALU.add,
                                          R=[sm, Gin, hin], W=[Gout])
                            tt(Tv[:, 0], Go[:, 0], tb(rc), ALU.mult, [Gout, rc], [TT_])
                            tt(Tv[:, 1], Go[:, 1], tb(rs), ALU.mult, [Gout, rs], [TT_])
                            tt(Sv[:, 0], Tv[:, 0], Tv[:, 1], ALU.subtract, [TT_], [S2])
                            tt(Tv[:, 0], Go[:, 1], tb(rc), ALU.mult, [Gout, rc], [TT_])
                            tt(Tv[:, 1], Go[:, 0], tb(rs), ALU.mult, [Gout, rs], [TT_])
                            tt(Sv[:, 1], Tv[:, 0], Tv[:, 1], ALU.add, [TT_], [S2])
                            Hv = Hp[:, 0:BLK].rearrange("p (r a c) -> p r a c", r=2, a=4)
                            k.dve("tensor_copy", Hv[:, :, :, 0], hin[:, :, 4 * i:4 * i + 4], R=[hin], W=[Hp])
                            k.dve("tensor_copy", Hv[:, :, :, 1:NC_], Sv[:, :, :, 0:NC_ - 1], R=[S2], W=[Hp])
                            k.dve("tensor_copy", hin[:, :, 4 * i:4 * i + 4], Sv[:, :, :, NC_ - 1], R=[S2], W=[hin])

                        def stage_b(i):
                            Hp, Yb, Zb = HPB[i % 2], YB[i % 2], ZB[i % 2]
                            Hv = Hp[:, 0:BLK].rearrange("p (r a c) -> p r a c", r=2, a=4)
                            for j in range(8):
                                PY = P[j % 2]
                                for s_ in range(j + 1):
                                    k.pe("matmul", PY[:, 0:NC_], Klag[:, i, j - s_, :], uT[:, i, s_:BLK:8],
                                         start=(s_ == 0), stop=False, R=[Klag, uT], W=[PY], inc=False, skip_group_check=True)
                                for kk in range(4):
                                    for ri in range(2):
                                        last = (kk == 3 and ri == 1)
                                        k.pe("matmul", PY[32 * kk:32 * kk + 32, 0:NC_], Ms[:, 4 * i + kk, j + 1, ri, :],
                                             Hv[:, ri, kk, :], start=False, stop=last, R=[Ms, Hp], W=[PY], inc=last,
                                             tile_position=(0, 32 * kk), skip_group_check=True)
                                k.act("copy", Yb[:, j:BLK:8], PY[:, 0:NC_], R=[PY], W=[Yb])
                            Y, Z = Yb[:, 0:BLK], Zb[:, 0:BLK]
                            tt(Z, Y, Y, ALU.mult, [Yb], [Zb])
                            k.dve("tensor_scalar", Z, Z, 0.044715, 1.0, op0=ALU.mult, op1=ALU.add, R=[Zb], W=[Zb])
                            tt(Z, Z, Y, ALU.mult, [Zb, Yb], [Zb])
                            k.act("activation", Z, Z, AF.Sigmoid, scale=1.5957691216057308, R=[Zb], W=[Zb])
                            tt(gT[:, i, :], Y, Z, ALU.mult, [Yb, Zb], [gT])

                        stage_a(0)
                        for i in range(4):
                            if i + 1 < 4:
                                stage_a(i + 1)
                            stage_b(i)
                        for ct in range(4):
                            pg = P[ct % 2]
                            for c in range(4):
                                k.pe("matmul", pg[:, 0:BLK], w_glu_bf[:, c, ct * 128:(ct + 1) * 128], gT[:, c, :],
                                     start=(c == 0), stop=(c == 3), R=[w_glu_bf, gT], W=[pg], inc=(c == 3))
                            k.act("activation", sg[:, :], pg[:, 0:BLK], AF.Sigmoid, bias=bglu[:, ct:ct + 1], R=[pg, bglu], W=[sg])
                            tt(sg[:, :], sg[:, :], gT[:, ct, :], ALU.mult, [sg, gT], [sg])
                            tt(mixT[:, 4 + ct, :], sg[:, :], mixT[:, 4 + ct, :], ALU.mult, [sg, mixT], [mixT])
                        if B == NBLK - 1:
                            with nc.allow_non_contiguous_dma(reason="final S5 state, small"):
                                l1o = lambda d: d.rearrange("(P gm) p -> (gm p) P", gm=2)
                                k.dma("sp", l1o(hre_o[n]), hin[:, 0, :], R=[hin])
                                k.dma("sp", l1o(him_o[n]), hin[:, 1, :], R=[hin])
                    for j in (range(4) if DBG["post"] else []):
                        xb = xt[st["xt"] % 2]
                        st["xt"] += 1
                        k.dma("sp", xb[:], x[r0 + j * 128:r0 + (j + 1) * 128, :], W=[xb])
                        for hf in range(2):
                            p = P[5 + hf]
                            for c in range(KC):
                                k.pe("matmul", p[:, :], mixT[:, c, j * 128:(j + 1) * 128], w_out_bf[:, c, hf * 512:(hf + 1) * 512],
                                     start=(c == 0), stop=(c == KC - 1), R=[mixT, w_out_bf], W=[p], inc=(c == KC - 1))
                            k.dve("tensor_tensor", tmp[:, hf * 512:(hf + 1) * 512], p[:, :], gate_b[:, hf * 512:(hf + 1) * 512],
                                  op=ALU.mult, R=[p, gate_b], W=[tmp])
                        k.dve("tensor_tensor", tmp[:], tmp[:], xb[:], op=ALU.add, R=[tmp, xb], W=[tmp])
                        rms_rstd(tmp[:], 128)
                        k.dve("scalar_tensor_tensor", yst[:], tmp[:], rstd[:, 0:1], gfin_b[:], op0=ALU.mult, op1=ALU.mult,
                              R=[tmp, rstd, gfin_b], W=[yst])
                        k.dma("sp", yp[r0 + j * 128:r0 + (j + 1) * 128, :], yst[:], R=[yst])


            if DBG.get("sample", True):
                k.barrier()
                NP_ = 128
                GP = 8
                kring = [k.wrap("kring%d" % i_, kT[:, 2 * i_:2 * i_ + 2, :].rearrange("p a (b c) -> p (a b) c", c=512)) for i_ in range(2)]
                vring = [k.wrap("vring%d" % i_, vaug[:, 4096 * i_:4096 * (i_ + 1)].rearrange("p (b c) -> p b c", c=512)) for i_ in range(2)]
                LFb, Fb, rows, prod = tmp, yst, xt[0], xt[1]
                LF = LFb[:].rearrange("p (j h) -> p j h", h=8)
                Fv = Fb[:].rearrange("p (j h) -> p j h", h=8)
                idx = sb("idx", [128, 128], I32)
                sel8 = sb("sel8", [8, NSMP * NSMP], F32)
                sml = sb("sml", [128, 64], F32)
                onesb = sb("onesb", [128, 8], BF16)
                qb = pTb[0]
                pbf = pTb[1]
                k.dma("sp", sel8[:], sel8_d, W=[sel8])
                k.dve("memset", onesb[:], 1.0, W=[onesb])
                RG = 8
                NG = 128 // RG
                sutB, bd8B = bc_sb, rec
                k.dma("sp", sutB[:, 0:128], sut_d, W=[sutB])
                k.dma("sp", bd8B[0:8, :], bd8_d, W=[bd8B])
                idxL = idx[:, 0:NSMP]
                idxK = idx[:, 64:64 + NSMP * NG].rearrange("p (s g) -> p s g", s=NSMP)
                with nc.allow_non_contiguous_dma(reason="page table, pages on partitions"):
                    k.dma("sp", idxL, pt_d.rearrange("o (s j) -> j (o s)", s=NSMP), W=[idx])
                k.dve("tensor_copy", sml[:, 0:NSMP], idxL, R=[idx], W=[sml])
                for g_ in range(NG):
                    k.dve("tensor_scalar", sml[:, 8:8 + NSMP], sml[:, 0:NSMP], float(NG), float(g_), op0=ALU.mult, op1=ALU.add,
                          R=[sml], W=[sml])
                    k.dve("tensor_copy", idxK[:, :, g_], sml[:, 8:8 + NSMP], R=[sml], W=[idx])
                ck_v = ck_d.rearrange("(n r) c -> n (r c)", r=RG)
                cv_v = cv_d.rearrange("(n r) c -> n (r c)", r=RG)
                clf_v = clf_d.rearrange("(n r) c -> n (r c)", r=128)

                R4 = slice(0, NSMP)
                k.dma("sp", prod[R4, :], g_norm.broadcast_to([NSMP, D]), W=[prod])
                k.dve("scalar_tensor_tensor", mod[R4, D:2 * D], mod[R4, D:2 * D], 1.0, prod[R4, :],
                      op0=ALU.add, op1=ALU.mult, R=[mod, prod], W=[mod])
                pre_tile(NSMP, xs, mod[R4, D:2 * D], mod[R4, 0:D], 0)
                vrow = attf
                for ci, (col0, dst, dB) in enumerate([(COL_Q + i_ * 128, rows[R4, i_ * 128:(i_ + 1) * 128], rows) for i_ in range(8)]
                                                     + [(COL_V + i_ * 128, vrow[R4, i_ * 128:(i_ + 1) * 128], vrow) for i_ in range(4)]):
                    wb = load_piece(col0, 128)
                    p = P[5 + ci % 2]
                    for c in range(KC):
                        k.pe("matmul", p[R4, 0:128], hT[:, c, 0:NSMP], wb[:, c, 0:128], start=(c == 0), stop=(c == KC - 1),
                             R=[hT, wb], W=[p], inc=(c == KC - 1))
                    k.act("copy", dst, p[R4, 0:128], R=[p], W=[dB])
                wb = load_piece(COL_FL, 8)
                for c in range(KC):
                    k.pe("matmul", P[3][R4, 0:8], hT[:, c, 0:NSMP], wb[:, c, 0:8], start=(c == 0), stop=(c == KC - 1),
                         R=[hT, wb], W=[P[3]], inc=(c == KC - 1))
                logf_from(P[3][R4, 0:8].rearrange("p (j h) -> p j h", j=1), NSMP, 1, lf[R4, 0:1, :])
                k.dma("sp", ks, rows[R4, 512:1024], R=[rows])
                k.dma("sp", vs, vrow[R4, :], R=[vrow])
                k.dma("sp", lfs, lf[R4, 0, :], R=[lf])
                fms = sb("fms", [128, 3, 4, NSMP], BF16)
                for gi, (colb, fn) in enumerate(((COL_ZA, AF.Silu), (COL_U, AF.Copy), (COL_ZS, AF.Silu))):
                    for i_ in range(4):
                        wb = load_piece(colb + i_ * 128, 128)
                        p = P[5 + i_ % 2]
                        for c in range(KC):
                            k.pe("matmul", p[:, 0:NSMP], wb[:, c, 0:128], hT[:, c, 0:NSMP], start=(c == 0), stop=(c == KC - 1),
                                 R=[wb, hT], W=[p], inc=(c == KC - 1))
                        if fn == AF.Copy:
                            k.act("copy", fms[:, gi, i_, :], p[:, 0:NSMP], R=[p], W=[fms])
                        else:
                            k.act("activation", fms[:, gi, i_, :], p[:, 0:NSMP], fn, R=[p], W=[fms])
                pself = sml[R4, 8:16]
                k.dve("tensor_tensor", prod[R4, 0:512], rows[R4, 0:512], rows[R4, 512:1024], op=ALU.mult, R=[rows], W=[prod])
                k.dve("tensor_reduce", sml[R4, 8:16], prod[R4, 0:512].rearrange("p (h d) -> p h d", h=8), axis=AX.X, op=ALU.add,
                      R=[prod], W=[sml])
                k.act("activation", sml[R4, 8:16], sml[R4, 8:16], AF.Exp, scale=0.125, R=[sml], W=[sml])
                k.act("activation", rows[R4, 0:512], rows[R4, 0:512], AF.Copy, scale=0.125, R=[rows], W=[rows])

                PO, PD, PA = P[2], P[3], P[4]
                for s_ in range(NSMP):
                    j0 = s_ * NP_
                    k.dma("pool", LFb[:, :], clf_v, R=[idx], W=[LFb], fn_indirect=idx[:, s_:s_ + 1])
                    for hh in range(8):
                        k.dve("tensor_tensor_scan", Fv[:, :, hh], ones32[:, 0:NP_], LF[:, :, hh], 0.0,
                              op0=ALU.mult, op1=ALU.add, R=[ones32, LFb], W=[Fb])
                    k.dve("tensor_copy", sml[:, 16:24], Fv[:, NP_ - 1, :], R=[Fb], W=[sml])
                    k.dve("scalar_tensor_tensor", Fv[:, :, :], Fv[:, :, :], -1.0,
                          sml[:, 16:24].unsqueeze(1).broadcast_to([128, NP_, 8]), op0=ALU.mult, op1=ALU.add,
                          R=[Fb, sml], W=[Fb])
                    k.pe("matmul", P[0][:, 0:8], sutB[:, 0:128], sml[:, 16:24], start=True, stop=False,
                         R=[sutB, sml], W=[P[0]], inc=False)
                    k.pe("matmul", P[0][:, 0:8], sel[0:NSMP, s_ * 128:(s_ + 1) * 128], lf[R4, 0, :], start=False, stop=True,
                         R=[sel, lf], W=[P[0]])
                    k.dve("tensor_tensor", Fv[:, :, :], Fv[:, :, :], P[0][:, 0:8].unsqueeze(1).broadcast_to([128, NP_, 8]),
                          op=ALU.add, R=[Fb, P[0]], W=[Fb])
                    k.pe("matmul", PA[:, :], sel[0:NSMP, s_ * 128:(s_ + 1) * 128], rows[R4, 0:512], start=True, stop=True,
                         R=[sel, rows], W=[PA])
                    k.act("copy", qb[:, :], PA[:, :], R=[PA], W=[qb])
                    for g_ in range(NP_ // GP):
                        kr, vr = kring[g_ % 2], vring[g_ % 2]
                        k.dma("pool", kr[:, :, :].rearrange("p a c -> p (a c)"), ck_v, R=[idx], W=[kr],
                              fn_indirect=idxK[:, s_, g_:g_ + 1])
                        k.dma("pool", vr[:, :, :].rearrange("p a c -> p (a c)"), cv_v, R=[idx], W=[vr],
                              fn_indirect=idxK[:, s_, g_:g_ + 1])
                        for jp in range(0, GP, 2):
                            k.dve("tensor_tensor", prod[:, :].rearrange("p (a c) -> p a c", a=2), kr[:, jp:jp + 2, :],
                                  qb[:, :].unsqueeze(1).broadcast_to([128, 2, 512]), op=ALU.mult, R=[kr, qb], W=[prod])
                            k.dve("tensor_reduce", scg[:, jp * 8:(jp + 2) * 8], prod[:, :].rearrange("p (a d) -> p a d", d=64),
                                  axis=AX.X, op=ALU.add, R=[prod], W=[scg])
                        k.dve("tensor_tensor", scg[:, :], scg[:, :], Fb[:, g_ * GP * 8:(g_ + 1) * GP * 8], op=ALU.add,
                              R=[scg, Fb], W=[scg])
                        k.act("activation", pbf[:, 0:GP * 8], scg[:, :], AF.Exp, R=[scg], W=[pbf])
                        for jj in range(GP):
                            j = g_ * GP + jj
                            k.pe("matmul", PO[0:8, :], pbf[:, jj * 8:(jj + 1) * 8], vr[:, jj, :], start=(j == 0), stop=(j == NP_ - 1),
                                 R=[pbf, vr], W=[PO], inc=False)
                            k.pe("matmul", PD[0:8, 0:1], pbf[:, jj * 8:(jj + 1) * 8], onesb[:, 0:1], start=(j == 0), stop=(j == NP_ - 1),
                                 R=[pbf, onesb], W=[PD], inc=(jj == GP - 1))
                    k.dve("tensor_tensor", kTf[0:8, :], PO[0:8, :], bd8B[0:8, :], op=ALU.mult, R=[PO, bd8B], W=[kTf])
                    k.dve("tensor_scalar", sml[0:8, 32:40], ident32[0:8, 0:8], PD[0:8, 0:1], None, op0=ALU.mult, R=[ident32, PD], W=[sml])
                    k.pe("matmul", P[5][R4, :], sel8[:, s_ * NSMP:(s_ + 1) * NSMP], kTf[0:8, :], start=(s_ == 0), stop=(s_ == NSMP - 1),
                         R=[sel8, kTf], W=[P[5]], inc=False)
                    k.pe("matmul", P[6][R4, 0:8], sel8[:, s_ * NSMP:(s_ + 1) * NSMP], sml[0:8, 32:40], start=(s_ == 0), stop=(s_ == NSMP - 1),
                         R=[sel8, sml], W=[P[6]])
                orow = kTf
                k.dve("tensor_tensor", prod[R4, 0:512].rearrange("p (h d) -> p h d", h=8), vrow[R4, :].rearrange("p (h d) -> p h d", h=8),
                      sml[R4, 8:16].unsqueeze(2).broadcast_to([NSMP, 8, 64]), op=ALU.mult, R=[vrow, sml], W=[prod])
                k.dve("tensor_tensor", prod[R4, 0:512], prod[R4, 0:512], P[5][R4, :], op=ALU.add, R=[prod, P[5]], W=[prod])
                k.dve("tensor_tensor", sml[R4, 40:48], sml[R4, 8:16], P[6][R4, 0:8], op=ALU.add, R=[sml, P[6]], W=[sml])
                k.dve("reciprocal", sml[R4, 40:48], sml[R4, 40:48], R=[sml], W=[sml])
                k.dve("tensor_tensor", orow[R4, :].rearrange("p (h d) -> p h d", h=8), prod[R4, 0:512].rearrange("p (h d) -> p h d", h=8),
                      sml[R4, 40:48].unsqueeze(2).broadcast_to([NSMP, 8, 64]), op=ALU.mult, R=[prod, sml], W=[orow])
                mixs = sb("mixs", [128, KC, NSMP], BF16)
                for i_ in range(4):
                    k.pe("transpose", P[0][:, i_ * NSMP:(i_ + 1) * NSMP], orow[R4, i_ * 128:(i_ + 1) * 128], ident32[R4, R4],
                         R=[orow, ident32], W=[P[0]], inc=(i_ == 3))
                k.dve("tensor_tensor", mixs[:, 0:4, :], P[0][:, 0:4 * NSMP].rearrange("p (a s) -> p a s", a=4), fms[:, 0, :, :],
                      op=ALU.mult, R=[P[0], fms], W=[mixs])
                h0b = sb("h0b", [128, 2, 16, NSMP], BF16)
                h0v = kst[:, 0, :].rearrange("p (r a s) -> p r a s", r=2, a=16)
                hnv = kst[:, 1, :].rearrange("p (r a s) -> p r a s", r=2, a=16)
                with nc.allow_non_contiguous_dma(reason="S5 state re-layout, small"):
                    for ri, hd in enumerate((h0re_d, h0im_d)):
                        for s_ in range(NSMP):
                            k.dma("sp", h0v[:, ri, :, s_], hd[s_].rearrange("(P gm) p -> (gm p) P", gm=2), W=[kst])
                k.act("copy", h0b[:], h0v, R=[kst], W=[h0b])
                uS = fms[:, 1, :, :]
                for kk in range(4):
                    PSk = P[2 + kk]
                    for i_ in range(4):
                        for ri in range(2):
                            k.pe("matmul", PSk[:, (i_ * 2 + ri) * NSMP:(i_ * 2 + ri + 1) * NSMP], Wx[32 * kk:32 * kk + 32, i_, 7, ri, :],
                                 fms[32 * kk:32 * kk + 32, 1, i_, :], start=True, stop=True, R=[Wx, fms], W=[PSk],
                                 inc=(i_ == 3 and ri == 1), tile_position=(32 * kk, 0))
                    k.act("copy", hnv[:, :, kk:16:4, :].rearrange("p r a s -> p a r s"),
                          PSk[:, 0:8 * NSMP].rearrange("p (a r s) -> p a r s", a=4, r=2), R=[PSk], W=[kst])
                a1r = apw[:, 0, :, 1].unsqueeze(2).broadcast_to([128, 16, NSMP])
                a1i = apw[:, 1, :, 1].unsqueeze(2).broadcast_to([128, 16, NSMP])
                T1 = sml[:, 0:64].rearrange("p (a s) -> p a s", a=16)
                tt(T1, h0v[:, 0], a1r, ALU.mult, [kst, apw], [sml]); tt(hnv[:, 0], hnv[:, 0], T1, ALU.add, [kst, sml], [kst])
                tt(T1, h0v[:, 1], a1i, ALU.mult, [kst, apw], [sml]); tt(hnv[:, 0], hnv[:, 0], T1, ALU.subtract, [kst, sml], [kst])
                tt(T1, h0v[:, 1], a1r, ALU.mult, [kst, apw], [sml]); tt(hnv[:, 1], hnv[:, 1], T1, ALU.add, [kst, sml], [kst])
                tt(T1, h0v[:, 0], a1i, ALU.mult, [kst, apw], [sml]); tt(hnv[:, 1], hnv[:, 1], T1, ALU.add, [kst, sml], [kst])
                with nc.allow_non_contiguous_dma(reason="S5 state re-layout, small"):
                    for ri, hd in enumerate((hres_o, hims_o)):
                        for s_ in range(NSMP):
                            k.dma("sp", hd[s_].rearrange("(P gm) p -> (gm p) P", gm=2), hnv[:, ri, :, s_], R=[kst])
                for i_ in range(4):
                    PY = P[i_ % 2]
                    k.pe("matmul", PY[:, 0:NSMP], Klag[:, i_, 0, :], fms[:, 1, i_, :], start=True, stop=False,
                         R=[Klag, fms], W=[PY], inc=False, skip_group_check=True)
                    for kk in range(4):
                        for ri in range(2):
                            last = (kk == 3 and ri == 1)
                            k.pe("matmul", PY[32 * kk:32 * kk + 32, 0:NSMP], Ms[:, 4 * i_ + kk, 1, ri, :], h0b[:, ri, 4 * i_ + kk, :],
                                 start=False, stop=last, R=[Ms, h0b], W=[PY], inc=last, tile_position=(0, 32 * kk),
                                 skip_group_check=True)
                    Yv, Zv = sml[:, 0:NSMP], sml[:, 8:8 + NSMP]
                    k.act("copy", Yv, PY[:, 0:NSMP], R=[PY], W=[sml])
                    tt(Zv, Yv, Yv, ALU.mult, [sml], [sml])
                    k.dve("tensor_scalar", Zv, Zv, 0.044715, 1.0, op0=ALU.mult, op1=ALU.add, R=[sml], W=[sml])
                    tt(Zv, Zv, Yv, ALU.mult, [sml], [sml])
                    k.act("activation", Zv, Zv, AF.Sigmoid, scale=1.5957691216057308, R=[sml], W=[sml])
                    tt(fms[:, 1, i_, :], Yv, Zv, ALU.mult, [sml], [fms])
                for ct in range(4):
                    pg = P[ct % 2]
                    for c in range(4):
                        k.pe("matmul", pg[:, 0:NSMP], w_glu_bf[:, c, ct * 128:(ct + 1) * 128], fms[:, 1, c, :],
                             start=(c == 0), stop=(c == 3), R=[w_glu_bf, fms], W=[pg], inc=(c == 3))
                    k.act("activation", sml[:, 16:16 + NSMP], pg[:, 0:NSMP], AF.Sigmoid, bias=bglu[:, ct:ct + 1], R=[pg, bglu], W=[sml])
                    tt(sml[:, 16:16 + NSMP], sml[:, 16:16 + NSMP], fms[:, 1, ct, :], ALU.mult, [sml, fms], [sml])
                    tt(mixs[:, 4 + ct, :], sml[:, 16:16 + NSMP], fms[:, 2, ct, :], ALU.mult, [sml, fms], [mixs])
                k.dma("sp", xt[1][R4, :], xs, W=[xt[1]])
                for hf in range(2):
                    p = P[5 + hf]
                    for c in range(KC):
                        k.pe("matmul", p[R4, :], mixs[:, c, :], w_out_bf[:, c, hf * 512:(hf + 1) * 512], start=(c == 0), stop=(c == KC - 1),
                             R=[mixs, w_out_bf], W=[p], inc=(c == KC - 1))
                    k.dve("tensor_tensor", tmp[R4, hf * 512:(hf + 1) * 512], p[R4, :], mod[R4, 2 * D + hf * 512:2 * D + (hf + 1) * 512],
                          op=ALU.mult, R=[p, mod], W=[tmp])
                k.dve("tensor_tensor", tmp[R4, :], tmp[R4, :], xt[1][R4, :], op=ALU.add, R=[tmp, xt[1]], W=[tmp])
                rms_rstd(tmp[R4, :], NSMP)
                k.dve("scalar_tensor_tensor", yst[R4, :], tmp[R4, :], rstd[R4, 0:1], gfin_b[R4, :], op0=ALU.mult, op1=ALU.mult,
                      R=[tmp, rstd, gfin_b], W=[yst])
                k.dma("sp", ys_o, yst[R4, :], R=[yst])
        except _Stop:
            pass
        k.finish()
        print("instructions:", k.nins, "sbuf bytes/partition:", acct["bytes"])
    return nc


_CONST = {}


def _consts():
    if not _CONST:
        sel = np.zeros((6, 6, 128), np.float32)
        for r in range(6):
            sel[r, r, :] = 1.0
        _CONST["ident"] = np.eye(128, dtype=np.float32)
        _CONST["tri"] = np.triu(np.ones((128, 128), np.float32))
        _CONST["sel"] = sel.reshape(6, 6 * 128)
        sh = np.zeros((64, 128), np.float32)
        sh[np.arange(64), 64 + np.arange(64)] = 1.0
        _CONST["shift"] = sh
        _CONST["sut"] = np.tril(np.ones((128, 128), np.float32), -1)
        bd = np.zeros((8, 8, 64), np.float32)
        bd[np.arange(8), np.arange(8), :] = 1.0
        _CONST["bd8"] = bd.reshape(8, 512)
        s8 = np.zeros((8, 4, 4), np.float32)
        s8[:, np.arange(4), np.arange(4)] = 1.0
        _CONST["sel8"] = s8.reshape(8, 16)
        _CONST["m9"] = np.tile(np.arange(9, dtype=np.float32), (128, 16)).reshape(128, 144)
        _CONST["m64"] = np.tile(np.arange(1, 65, dtype=np.float32), (128, 16)).reshape(128, 1024)
    return _CONST


def kernel(x_prompt, x_sample, c_prompt, c_sample, cache_k, cache_v, cache_logf,
           state_ssm_re, state_ssm_im, page_table, g_norm, w_ada, b_ada, w_in, b_fgate,
           a_re, a_im, log_dt, b_re, b_im, c_re, c_im, d_skip, w_glu, b_glu, w_out, g_final):
    f = lambda a: np.ascontiguousarray(np.asarray(a, dtype=np.float32))
    x_prompt, x_sample, c_prompt, c_sample = f(x_prompt), f(x_sample), f(c_prompt), f(c_sample)
    cst = _consts()
    nc = build_program()
    shared = {
        "g_norm_d": f(g_norm).reshape(1, D), "g_final_d": f(g_final).reshape(1, D),
        "w_ada_d": f(w_ada).reshape(D, 3 * D), "b_ada_d": f(b_ada).reshape(1, 3 * D),
        "w_in_d": f(w_in).reshape(D, DIN), "w_out_d": f(w_out).reshape(D, D),
        "w_glu_d": f(w_glu).reshape(512, 512), "b_glu_d": f(b_glu).reshape(512),
        "b_fgate_d": f(b_fgate).reshape(1, 8),
        "ident_c": cst["ident"], "tri_c": cst["tri"], "sel_c": cst["sel"], "shift_c": cst["shift"], "m9_c": cst["m9"], "m64_c": cst["m64"],
        **({"sut_c": cst["sut"], "bd8_c": cst["bd8"], "sel8_c": cst["sel8"]} if DBG.get("sample", True) else {}),
        "a_re_d": f(a_re).reshape(32, 64), "a_im_d": f(a_im).reshape(32, 64), "log_dt_d": f(log_dt).reshape(32),
        "b_re_d": f(b_re).reshape(32, 64, 16), "b_im_d": f(b_im).reshape(32, 64, 16),
        "c_re_d": f(c_re).reshape(32, 16, 64), "c_im_d": f(c_im).reshape(32, 16, 64), "d_skip_d": f(d_skip).reshape(32, 16),
    }
    in_maps = []
    for i in range(NCORES):
        m = dict(shared)
        m["x_d"] = x_prompt[NSEQ * i:NSEQ * (i + 1)].reshape(NTOK, D)
        m["xs_d"] = x_sample[NSMP * i:NSMP * (i + 1)].reshape(NSMP, D)
        if DBG.get("sample", True):
            m["ck_d"] = np.asarray(cache_k, dtype=np.float32).reshape(5120 * 128, 512)
            m["cv_d"] = np.asarray(cache_v, dtype=np.float32).reshape(5120 * 128, 512)
            m["clf_d"] = np.asarray(cache_logf, dtype=np.float32).reshape(5120 * 128, 8)
            m["pt_d"] = np.ascontiguousarray(np.asarray(page_table, dtype=np.int32)[NSMP * i:NSMP * (i + 1)]).reshape(1, NSMP * 128)
            m["h0re_d"] = f(state_ssm_re)[0, NSMP * i:NSMP * (i + 1)]
            m["h0im_d"] = f(state_ssm_im)[0, NSMP * i:NSMP * (i + 1)]
        m["cnd"] = np.concatenate([c_sample[NSMP * i:NSMP * (i + 1)], c_prompt[NSEQ * i:NSEQ * (i + 1)]], 0)
        in_maps.append(m)
    in_maps = in_maps[:DBG["cores"]]
    res = run_bass_kernel_spmd(nc, in_maps, core_ids=list(range(DBG["cores"])), **({"trace": True} if DBG.get("trace") else {}))
    if DBG.get("trace"):
        DBG["exec_ns"] = res.exec_time_ns
    R = res.results
    if DBG.get("dump"):
        DBG["dumped"] = {kk: np.asarray(R[0][kk]) for kk in R[0] if kk.startswith("d_")}
    cat = lambda key: np.concatenate([r[key] for r in R], 0)
    y_prompt = cat("yp").reshape(-1, SEQ, D)
    k_prompt = cat("kp").reshape(1, -1, SEQ, 8, 64)
    v_prompt = cat("vp").reshape(1, -1, SEQ, 8, 64)
    logf_prompt = cat("lfp").reshape(1, -1, SEQ, 8)
    hre_p = cat("hre_p").reshape(1, -1, 32, 64)
    him_p = cat("him_p").reshape(1, -1, 32, 64)
    return (y_prompt, cat("ys").reshape(-1, 1, D), k_prompt, v_prompt, logf_prompt, hre_p, him_p,
            cat("ks").reshape(1, -1, 1, 8, 64), cat("vs").reshape(1, -1, 1, 8, 64), cat("lfs").reshape(1, -1, 1, 8),
            cat("hre_s").reshape(1, -1, 32, 64), cat("him_s").reshape(1, -1, 32, 64))
```
